# Optimizing a Trainium2 kernel written in Bass

```python
import jax, jax.numpy as jnp
from jax import lax
import numpy as np

D_MODEL = 1024
BATCH = 8
SEQ = 4096
DEPTH = 2

N_MEM = 256
D_MIX = 2 * D_MODEL
MLA_HEADS = 16
MLA_NOPE = 64
MLA_ROPE = 32
MLA_V = 64
MLA_Q_RANK = 384
MLA_KV_RANK = 256
ROPE_BASE = 10000.0
Q_BLOCK = 128
SWA_HEADS = 8
SWA_KV_HEADS = 2
SWA_HEAD_DIM = 64
WINDOW = 128
MEM_HEADS = 4
MEM_HEAD_DIM = 128
EPS = 1e-6
NEG = -1e30

MLA_WIDTH = MLA_HEADS * MLA_V
SWA_WIDTH = SWA_HEADS * SWA_HEAD_DIM
SWA_KV_WIDTH = SWA_KV_HEADS * SWA_HEAD_DIM
MEM_WIDTH = MEM_HEADS * MEM_HEAD_DIM
SPLITS = (MLA_Q_RANK, MLA_KV_RANK, MLA_ROPE, MLA_WIDTH,
          SWA_WIDTH, SWA_KV_WIDTH, SWA_KV_WIDTH, SWA_WIDTH,
          MEM_WIDTH, MEM_WIDTH)
D_IN_PROJ = sum(SPLITS)
SPLIT_IDX = tuple(int(i) for i in np.cumsum(SPLITS)[:-1])

kernel_name = "hymba_mla_swa_sink_alibi_memxattn"


def rmsnorm(x, g):
    xf = x.astype(jnp.float32)
    y = xf * lax.rsqrt(jnp.mean(xf * xf, axis=-1, keepdims=True) + EPS)
    return (y * g.astype(jnp.float32)).astype(x.dtype)


def rope_tables(seq, dim):
    inv = ROPE_BASE ** (-jnp.arange(0, dim, 2, dtype=jnp.float32) / dim)
    ang = jnp.arange(seq, dtype=jnp.float32)[:, None] * inv[None, :]
    return jnp.cos(ang), jnp.sin(ang)


def apply_rope(x, cos, sin):
    x1, x2 = jnp.split(x, 2, axis=-1)
    c = cos[None, :, None, :].astype(x.dtype)
    s = sin[None, :, None, :].astype(x.dtype)
    return jnp.concatenate([x1 * c - x2 * s, x1 * s + x2 * c], axis=-1)


def alibi_slopes(n):
    return jnp.exp2(-8.0 * jnp.arange(1, n + 1, dtype=jnp.float32) / n)


def mla_group(c_q, c_kv, k_rope_in, q_norm, w_uq, kv_norm, w_ukv):
    B, S, _ = c_q.shape
    cos, sin = rope_tables(S, MLA_ROPE)
    q = (rmsnorm(c_q, q_norm) @ w_uq).reshape(B, S, MLA_HEADS, MLA_NOPE + MLA_ROPE)
    q_nope, q_pe = q[..., :MLA_NOPE], apply_rope(q[..., MLA_NOPE:], cos, sin)
    kv = (rmsnorm(c_kv, kv_norm) @ w_ukv).reshape(B, S, MLA_HEADS, MLA_NOPE + MLA_V)
    k_nope, v = kv[..., :MLA_NOPE], kv[..., MLA_NOPE:]
    k_pe = apply_rope(k_rope_in[:, :, None, :], cos, sin)[:, :, 0, :]
    scale = (MLA_NOPE + MLA_ROPE) ** -0.5
    nblk = S // Q_BLOCK
    qn_b = q_nope.reshape(B, nblk, Q_BLOCK, MLA_HEADS, MLA_NOPE).transpose(1, 0, 2, 3, 4)
    qp_b = q_pe.reshape(B, nblk, Q_BLOCK, MLA_HEADS, MLA_ROPE).transpose(1, 0, 2, 3, 4)
    kpos = jnp.arange(S)

    def block(args):
        qn, qp, i = args
        s = (jnp.einsum('bqhd,bkhd->bhqk', qn, k_nope)
             + jnp.einsum('bqhr,bkr->bhqk', qp, k_pe)).astype(jnp.float32) * scale
        qpos = i * Q_BLOCK + jnp.arange(Q_BLOCK)
        causal = kpos[None, :] <= qpos[:, None]
        s = jnp.where(causal[None, None], s, NEG)
        p = jax.nn.softmax(s, axis=-1).astype(v.dtype)
        return jnp.einsum('bhqk,bkhd->bqhd', p, v)

    out = lax.map(block, (qn_b, qp_b, jnp.arange(nblk)))
    return out.transpose(1, 0, 2, 3, 4).reshape(B, S, MLA_WIDTH)


def swa_group(q, k, v, sinks):
    B, S, _ = q.shape
    W = WINDOW
    G = SWA_HEADS // SWA_KV_HEADS
    nblk = S // W
    qb = q.reshape(B, nblk, W, SWA_KV_HEADS, G, SWA_HEAD_DIM)
    k = k.reshape(B, S, SWA_KV_HEADS, SWA_HEAD_DIM)
    v = v.reshape(B, S, SWA_KV_HEADS, SWA_HEAD_DIM)
    pad = ((0, 0), (W, 0), (0, 0), (0, 0))
    kp, vp = jnp.pad(k, pad), jnp.pad(v, pad)
    shp = (B, nblk, W, SWA_KV_HEADS, SWA_HEAD_DIM)
    kb = jnp.concatenate([kp[:, :S].reshape(shp), k.reshape(shp)], axis=2)
    vb = jnp.concatenate([vp[:, :S].reshape(shp), v.reshape(shp)], axis=2)
    s = jnp.einsum('bnqkgd,bnskd->bnkgqs', qb, kb).astype(jnp.float32) * (SWA_HEAD_DIM ** -0.5)
    i = jnp.arange(W)[:, None]
    j = jnp.arange(2 * W)[None, :]
    dist = W + i - j
    kglob = jnp.arange(nblk)[:, None, None] * W + j[None] - W
    mask = (dist >= 0)[None] & (dist < WINDOW)[None] & (kglob >= 0)
    slopes = alibi_slopes(SWA_HEADS).reshape(SWA_KV_HEADS, G)
    s = s - slopes[None, None, :, :, None, None] * dist.astype(jnp.float32)[None, None, None, None]
    s = jnp.where(mask[None, :, None, None], s, NEG)
    sink = sinks.astype(jnp.float32).reshape(SWA_KV_HEADS, G)[None, None, :, :, None, None]
    m = jnp.maximum(jnp.max(s, axis=-1, keepdims=True), sink)
    e = jnp.exp(s - m)
    p = (e / (jnp.sum(e, axis=-1, keepdims=True) + jnp.exp(sink - m))).astype(v.dtype)
    out = jnp.einsum('bnkgqs,bnskd->bnqkgd', p, vb)
    return out.reshape(B, S, SWA_WIDTH)


def mem_group(q, mem_n, w_mem_kv):
    B, S, _ = q.shape
    M = mem_n.shape[1]
    kv = mem_n @ w_mem_kv
    k = kv[..., :MEM_WIDTH].reshape(B, M, MEM_HEADS, MEM_HEAD_DIM)
    v = kv[..., MEM_WIDTH:].reshape(B, M, MEM_HEADS, MEM_HEAD_DIM)
    q = q.reshape(B, S, MEM_HEADS, MEM_HEAD_DIM)
    s = jnp.einsum('bshd,bmhd->bhsm', q, k).astype(jnp.float32) * (MEM_HEAD_DIM ** -0.5)
    p = jax.nn.softmax(s, axis=-1).astype(v.dtype)
    return jnp.einsum('bhsm,bmhd->bshd', p, v).reshape(B, S, MEM_WIDTH)


def setup_inputs(seed: int = 0) -> dict:
    key = jax.random.key(seed)
    ks = jax.random.split(key, 16)
    f32 = jnp.float32

    def nrm(k, shape, fan_in):
        return jax.random.normal(k, shape, f32) * (fan_in ** -0.5)

    def gain(k, shape):
        return 1.0 + 0.02 * jax.random.normal(k, shape, f32)

    return {
        "x": jax.random.normal(ks[0], (BATCH, SEQ, D_MODEL), f32),
        "mem": jax.random.normal(ks[1], (BATCH, N_MEM, D_MODEL), f32),
        "attn_norm": gain(ks[2], (DEPTH, D_MODEL)),
        "w_in": nrm(ks[3], (DEPTH, D_MODEL, D_IN_PROJ), D_MODEL),
        "mla_q_norm": gain(ks[4], (DEPTH, MLA_Q_RANK)),
        "w_uq": nrm(ks[5], (DEPTH, MLA_Q_RANK, MLA_HEADS * (MLA_NOPE + MLA_ROPE)), MLA_Q_RANK),
        "mla_kv_norm": gain(ks[6], (DEPTH, MLA_KV_RANK)),
        "w_ukv": nrm(ks[7], (DEPTH, MLA_KV_RANK, MLA_HEADS * (MLA_NOPE + MLA_V)), MLA_KV_RANK),
        "swa_sinks": jax.random.normal(ks[8], (DEPTH, SWA_HEADS), f32),
        "mem_norm": gain(ks[9], (DEPTH, D_MODEL)),
        "w_mem_kv": nrm(ks[10], (DEPTH, D_MODEL, 2 * MEM_WIDTH), D_MODEL),
        "w_out": nrm(ks[11], (DEPTH, D_MIX, D_MODEL), D_MIX),
        "final_norm": gain(ks[12], (D_MODEL,)),
    }


def reference(x, mem, attn_norm, w_in, mla_q_norm, w_uq, mla_kv_norm, w_ukv,
              swa_sinks, mem_norm, w_mem_kv, w_out, final_norm):
    for l in range(DEPTH):
        h = rmsnorm(x, attn_norm[l])
        proj = h @ w_in[l]
        (c_q, c_kv, k_rope, z_mla, q_swa, k_swa, v_swa, z_swa,
         q_mem, z_mem) = jnp.split(proj, SPLIT_IDX, axis=-1)
        y_mla = mla_group(c_q, c_kv, k_rope, mla_q_norm[l], w_uq[l],
                          mla_kv_norm[l], w_ukv[l]) * jax.nn.silu(z_mla)
        y_swa = swa_group(q_swa, k_swa, v_swa, swa_sinks[l]) * jax.nn.silu(z_swa)
        y_mem = mem_group(q_mem, rmsnorm(mem, mem_norm[l]), w_mem_kv[l]) * jax.nn.silu(z_mem)
        y = jnp.concatenate([y_mla, y_swa, y_mem], axis=-1) @ w_out[l]
        x = x + y
    return rmsnorm(x, final_norm)
```

```python
import numpy as np
import concourse.bass as bass
import concourse.mybir as mybir
from concourse.bass_utils import run_bass_kernel_spmd

F32 = mybir.dt.float32
BF16 = mybir.dt.bfloat16
AF = mybir.ActivationFunctionType
ALU = mybir.AluOpType

S = 4096
D = 1024
NT = S // 128
NCH = S // 512
DEPTH = 2
WIN_COLS = 4032
WSPLIT = 1696
EPS = 1e-6
NEGB = -30000.0


class Buf:
    __slots__ = ("name", "w", "rc", "rd", "excl")

    def __init__(self, name, excl=False):
        self.name = name
        self.excl = excl
        self.w = None
        self.rc = {}
        self.rd = []


class Op:
    __slots__ = ("eng", "fn", "deps", "signaled", "token", "dma_sem")

    def __init__(self, eng, fn, dma_sem=None):
        self.eng = eng
        self.fn = fn
        self.deps = []
        self.signaled = False
        self.token = None
        self.dma_sem = dma_sem


class P:
    ENGS = ("pe", "act", "dve", "pool", "sp")

    def __init__(self, nc):
        self.nc = nc
        self.e = {"pe": nc.tensor, "act": nc.scalar, "dve": nc.vector, "pool": nc.gpsimd, "sp": nc.sync}
        self.ops = []
        self.last_real = {}
        self.last_dma = {}
        self.nbuf = 0

    def buf(self, name="b", excl=False):
        self.nbuf += 1
        return Buf(f"{name}{self.nbuf}", excl)

    def op(self, eng, fn, reads=(), writes=(), dma_sem=None, extra_deps=()):
        o = Op(eng, fn, dma_sem)
        deps = {id(d): d for d in extra_deps}
        is_dma = dma_sem is not None

        def add(d, raw):
            if d is None or d is o:
                return
            d_dma = d.dma_sem is not None
            if not is_dma and not d_dma and d.eng == eng:
                if eng == "pe" or not raw:
                    return
            deps[id(d)] = d

        for b in reads:
            add(b.w, True)
            if b.excl:
                for re, r in b.rc.items():
                    if re != eng:
                        add(r, False)
        for b in writes:
            if not (is_dma and b.w is not None and b.w.dma_sem == dma_sem and b.w.eng == eng):
                add(b.w, False)
            for r in b.rc.values():
                add(r, False)
            for r in b.rd:
                add(r, False)
        o.deps = list(deps.values())
        for d in o.deps:
            d.signaled = True
        for b in reads:
            if is_dma:
                b.rd.append(o)
            else:
                b.rc[eng] = o
        for b in writes:
            b.w = o
            b.rc = {}
            b.rd = []
        self.ops.append(o)
        if is_dma:
            self.last_dma[dma_sem] = o
        else:
            self.last_real[eng] = o
        return o

    def barrier(self, skip_sems=()):
        deps = [o for o in self.last_real.values()] + [o for k, o in self.last_dma.items() if k not in skip_sems]
        for d in deps:
            d.signaled = True
        for eng in self.ENGS:
            o = Op(eng, None)
            o.deps = [d for d in deps if not (d.dma_sem is None and d.eng == eng and eng == "pe")]
            self.ops.append(o)

    def emit(self):
        nc = self.nc
        esem = {e: nc.alloc_semaphore(f"es_{e}") for e in self.ENGS}
        dsem = {}
        ecnt = {e: 0 for e in self.ENGS}
        dcnt = {}
        waited = {e: {} for e in self.ENGS}
        nwait = 0
        for o in self.ops:
            eng = self.e[o.eng]
            need = {}
            for d in o.deps:
                sem, val = d.token
                k = id(sem)
                if k not in need or need[k][1] < val:
                    need[k] = (sem, val)
            for k, (sem, val) in need.items():
                if waited[o.eng].get(k, 0) < val:
                    eng.wait_ge(sem, val)
                    waited[o.eng][k] = val
                    nwait += 1
            if o.fn is None:
                continue
            ins = o.fn(eng)
            if o.dma_sem is not None:
                if o.dma_sem not in dsem:
                    dsem[o.dma_sem] = nc.alloc_semaphore(f"ds_{o.dma_sem}")
                    dcnt[o.dma_sem] = 0
                dcnt[o.dma_sem] += 16
                ins.then_inc(dsem[o.dma_sem], 16)
                o.token = (dsem[o.dma_sem], dcnt[o.dma_sem])
            elif o.signaled:
                ecnt[o.eng] += 1
                ins.then_inc(esem[o.eng], 1)
                o.token = (esem[o.eng], ecnt[o.eng])
        return dict(n_ops=len(self.ops), n_wait=nwait, ecnt=ecnt, n_dsem=len(dsem))


class Builder:
    def __init__(self, depth=DEPTH, dbg=False, stop_after=None):
        self.depth = depth
        self.dbg = dbg
        self.stop_after = stop_after
        self.nc = bass.Bass("TRN2", target_bir_lowering=False)
        self.p = P(self.nc)
        self.uid = 0
        self.guards = []
        self.region = None

    def sb(self, shape, dtype, name="t"):
        self.uid += 1
        if self.region is None:
            g = self.nc.sbuf_tensor(f"{name}_{self.uid}", list(shape), dtype)
            t = g.__enter__()
            self.guards.append(g)
            return t
        nbytes = 2 if dtype == BF16 else 4
        for d in shape[1:]:
            nbytes *= d
        nbytes = (nbytes + 63) // 64 * 64
        off = self.region[0]
        assert off + nbytes <= self.region[1], f"arena region overflow: {name} {off}+{nbytes} > {self.region[1]}"
        self.region[0] = off + nbytes
        return self.nc.alloc_sbuf_tensor_at(f"{name}_{self.uid}", list(shape), dtype, offset=off)

    def set_region(self, lo, hi):
        self.region = [lo, hi]

    def mark(self):
        return len(self.guards)

    def release(self, mark):
        while len(self.guards) > mark:
            g = self.guards.pop()
            g.__exit__(None, None, None)

    def dram(self, name, shape, dtype, kind="Internal"):
        if self.dbg and kind == "Internal":
            kind = "ExternalOutput"
        return self.nc.dram_tensor(name, list(shape), dtype, kind=kind).ap()

    def dma(self, q, out, in_, sem, reads=(), writes=(), after=()):
        return self.p.op(q, lambda e: e.dma_start(out=out, in_=in_), reads, writes, dma_sem=sem, extra_deps=after)

    def mm(self, out, lhsT, rhs, start, stop, reads=(), writes=()):
        return self.p.op("pe", lambda e: e.matmul(out, lhsT, rhs, start=start, stop=stop), reads, writes)

    def act(self, out, in_, func, reads=(), writes=(), **kw):
        return self.p.op("act", lambda e: e.activation(out=out, in_=in_, func=func, **kw), reads, writes)

    def copy(self, eng, out, in_, reads=(), writes=()):
        if eng == "act":
            return self.act(out, in_, AF.Copy, reads, writes)
        return self.p.op(eng, lambda e: e.tensor_copy(out=out, in_=in_), reads, writes)

    def tt(self, eng, out, in0, in1, op, reads=(), writes=()):
        return self.p.op(eng, lambda e: e.tensor_tensor(out=out, in0=in0, in1=in1, op=op), reads, writes)

    def ts(self, eng, out, in0, s1, op0, reads=(), writes=()):
        return self.p.op(eng, lambda e: e.tensor_scalar(out=out, in0=in0, scalar1=s1, scalar2=None, op0=op0),
                         reads, writes)

    def recip(self, out, in_, reads=(), writes=()):
        return self.p.op("dve", lambda e: e.reciprocal(out=out, in_=in_), reads, writes)

    def build(self):
        nc, p = self.nc, self.p
        L = self.depth
        ein = lambda n, s: nc.dram_tensor(n, list(s), F32, kind="ExternalInput").ap()
        self.x_in = ein("x", (S, D))
        self.mem_in = ein("mem", (256, D))
        self.w_in = ein("w_in", (L, D, WIN_COLS))
        self.w_uq = ein("w_uq", (L, 384, 2048))
        self.w_ukv = ein("w_ukv", (L, 256, 2048))
        self.w_mem = ein("w_mem", (L, D, 1024))
        self.w_out = ein("w_out", (L, 2048, D))
        self.g_attn = ein("g_attn", (L, 128, D))
        self.g_q = ein("g_q", (L, 128, 3))
        self.g_kv = ein("g_kv", (L, 128, 2))
        self.g_mem = ein("g_mem", (L, 128, D))
        self.sinks = ein("sinks", (L, 128, 8))
        self.g_final = ein("g_final", (128, D))
        self.c_ident = ein("c_ident", (128, 128))
        self.c_tri = ein("c_tri", (128, 128))
        self.c_swab = ein("c_swab", (128, 16 * 128))
        self.c_ropeC = ein("c_ropeC", (128, S))
        self.c_ropeS = ein("c_ropeS", (128, S))
        self.y_out = nc.dram_tensor("y", [S, D], F32, kind="ExternalOutput").ap()
        self.XRES = self.dram("XRES", (S, D), F32)
        self.QN = self.dram("QN", (16 * 64, S), BF16)
        self.QR = self.dram("QR", (16 * 32, S), BF16)
        self.KN = self.dram("KN", (16 * 64, S), BF16)
        self.KPE = self.dram("KPE", (32, S), BF16)
        self.VM = self.dram("VM", (NT, 128, 16, 128), BF16)
        self.QS = self.dram("QS", (8 * 64, S), BF16)
        self.KS = self.dram("KS", (2 * 64, S), BF16)
        self.VS = self.dram("VS", (NT, 128, 4, 128), BF16)
        self.QM = self.dram("QM", (4, 128, S), BF16)
        self.G = self.dram("G", (2048, S), BF16)
        self.YT = self.dram("YT", (16, 128, S), BF16)

        self.ps = []
        for i in range(6):
            t = nc.alloc_psum_tensor(f"ps{i}", [128, 512], F32)
            self.ps.append((t, p.buf("ps", excl=True)))
        self.pst = []
        for i in range(2):
            t = nc.alloc_psum_tensor(f"pst{i}", [128, 1024], BF16)
            self.pst.append((t, p.buf("pst", excl=True)))
        self.ps_rr = 0
        self.pst_rr = 0

        self.setup_consts()
        p.barrier()
        self.KTm = self.sb((128, 4, 256), BF16, "KTm")
        self.Vm = self.sb((128, 2, 512), BF16, "Vm")
        self.memb = p.buf("memkv")
        SB_END = 16512 + 212863
        base = SB_END - nc.sbuf_bytes_remaining
        base = (base + 63) // 64 * 64
        end = SB_END // 64 * 64
        self.set_region(base, end)
        self.w_in_sb = self.sb((128, 8, WIN_COLS), BF16, "w_in")
        self.w_uq_sb = self.sb((128, 3, 2048), BF16, "w_uq")
        self.w_ukv_sb = self.sb((128, 2, 2048), BF16, "w_ukv")
        o_wmem = self.region[0]
        self.w_mem_sb = self.sb((128, 8, 1024), BF16, "w_mem")
        o_wout = self.region[0]
        self.w_out_sb = self.sb((128, 16, D), BF16, "w_out")
        o_r2 = self.region[0]
        self.wA0b = p.buf("wA0")
        self.wA1b = p.buf("wA1")
        self.wuqb = p.buf("wuq")
        self.wukvb = p.buf("wukv")
        self.wmb = p.buf("wmem")
        self.wob = p.buf("wout")
        WSEMS = ("w_mem", "wA0", "wA1", "wuq", "wukv", "w_out")

        def load_WA(l):
            jobs = []

            def add(dst, src, sem, buf):
                jobs.append(lambda after=(): self.dma("pool", dst, src, sem, writes=[buf], after=after))

            for kc in range(8):
                add(self.w_mem_sb[:, kc, :], self.w_mem[l][kc * 128:(kc + 1) * 128, :], "w_mem", self.wmb)
            for kc in range(8):
                add(self.w_in_sb[:, kc, 0:WSPLIT], self.w_in[l][kc * 128:(kc + 1) * 128, 0:WSPLIT], "wA0", self.wA0b)
            for kc in range(8):
                add(self.w_in_sb[:, kc, WSPLIT:WIN_COLS], self.w_in[l][kc * 128:(kc + 1) * 128, WSPLIT:WIN_COLS], "wA1",
                    self.wA1b)
            for kc in range(3):
                add(self.w_uq_sb[:, kc, :], self.w_uq[l][kc * 128:(kc + 1) * 128, :], "wuq", self.wuqb)
            for kc in range(2):
                add(self.w_ukv_sb[:, kc, :], self.w_ukv[l][kc * 128:(kc + 1) * 128, :], "wukv", self.wukvb)
            return jobs

        for j in load_WA(0):
            j()
        for l in range(L):
            last = (l == L - 1)
            x_src = self.x_in if l == 0 else self.XRES
            self.set_region(o_wout, end)
            self.phase_M(l)
            p.barrier(skip_sems=WSEMS)
            self.set_region(o_wmem, end)
            self.phase_A(l, x_src)
            p.barrier(skip_sems=WSEMS)
            if self.stop_after == ("A", l):
                break
            self.set_region(base, o_wout)
            self.phase_B(l)
            p.barrier(skip_sems=WSEMS)
            if self.stop_after == ("B", l):
                break
            self.set_region(o_r2, end)
            self.phase_C(l, x_src, last, [] if last else load_WA(l + 1))
            p.barrier(skip_sems=WSEMS)
        p.barrier()
        self.stats = p.emit()
        return nc

    def next_ps(self):
        r = self.ps[self.ps_rr % len(self.ps)]
        self.ps_rr += 1
        return r

    def next_pst(self):
        r = self.pst[self.pst_rr % 2]
        self.pst_rr += 1
        return r

    def setup_consts(self):
        p = self.p
        self.epsb = self.sb((128, 1), F32, "eps")
        self.ident = self.sb((128, 128), BF16, "ident")
        self.tri = self.sb((128, 128), BF16, "tri")
        self.swab = self.sb((128, 16 * 128), BF16, "swab")
        self.ones = self.sb((128, 128), BF16, "ones")
        self.cb = p.buf("consts")
        p.op("dve", lambda e: e.memset(self.epsb[:], EPS), writes=[self.cb])
        self.gfin = self.sb((128, D), F32, "gfin")
        L = self.depth
        self.gq = self.sb((128, L, 3), F32, "gq")
        self.gkv = self.sb((128, L, 2), F32, "gkv")
        self.esink = self.sb((128, L, 8), F32, "esink")
        self.vbuf = [(self.sb((128, 16, 128), BF16, "vbuf"), p.buf("vbuf")) for _ in range(4)]
        self.vsbuf = [(self.sb((128, 4, 128), BF16, "vsbuf"), p.buf("vsbuf")) for _ in range(2)]
        mtmp = self.mark()
        tmp = self.sb((128, 2048), F32, "ctmp")
        tb = p.buf("ctmp")
        self.dma("sp", tmp[:, 0:128], self.c_ident[:, :], "c0", writes=[tb])
        self.copy("dve", self.ident[:], tmp[:, 0:128], reads=[tb], writes=[self.cb])
        self.dma("sp", tmp[:, 0:128], self.c_tri[:, :], "c0", reads=[], writes=[tb])
        self.copy("dve", self.tri[:], tmp[:, 0:128], reads=[tb], writes=[self.cb])
        self.dma("sp", tmp[:, :], self.c_swab[:, :], "c0", writes=[tb])
        self.copy("dve", self.swab[:], tmp[:, :], reads=[tb], writes=[self.cb])
        p.op("dve", lambda e: e.memset(self.ones[:], 1.0), writes=[self.cb])
        self.dma("sp", self.gfin[:], self.g_final[:, :], "c1", writes=[self.cb])
        for l in range(L):
            self.dma("sp", self.gq[:, l, :], self.g_q[l], "c1", writes=[self.cb])
            self.dma("sp", self.gkv[:, l, :], self.g_kv[l], "c1", writes=[self.cb])
            self.dma("sp", self.esink[:, l, :], self.sinks[l], "c1", writes=[self.cb])
        p.barrier()
        for l in range(L):
            self.act(self.esink[:, l, :], self.esink[:, l, :], AF.Exp, reads=[self.cb], writes=[self.cb])
        for (t, b) in self.vbuf + self.vsbuf:
            p.op("pool", lambda e, t=t: e.memset(t[:], 1.0), writes=[b])
        p.barrier()
        self.release(mtmp)

    def load_weight(self, dst, src, KC, N, dst_buf, sem):
        for kc in range(KC):
            for c0 in range(0, N, 2048):
                n = min(2048, N - c0)
                self.dma("pool", dst[:, kc, c0:c0 + n], src[kc * 128:(kc + 1) * 128, c0:c0 + n], sem,
                         writes=[dst_buf])

    def stt(self, eng, out, in0, scalar, in1, op0, op1, reads=(), writes=()):
        return self.p.op(eng, lambda e: e.scalar_tensor_tensor(out=out, in0=in0, scalar=scalar, in1=in1, op0=op0,
                                                               op1=op1), reads, writes)

    def tok_prep(self, src_rows, xt, xb, sem, hq, hb, junk, jb, st, stb, gt, gb):
        if src_rows is not None:
            self.dma("sp", xt[:], src_rows, sem, writes=[xb])
        self.act(junk[:], xt[:], AF.Square, reads=[xb], writes=[jb, stb], accum_out=st[:, 0:1])
        self.act(st[:, 1:2], st[:, 0:1], AF.Sqrt, reads=[stb, self.cb], writes=[stb], scale=1.0 / D, bias=self.epsb[:, 0:1])
        self.recip(st[:, 2:3], st[:, 1:2], reads=[stb], writes=[stb])
        self.stt("dve", hq[:], xt[:], st[:, 2:3], gt[:], ALU.mult, ALU.mult, reads=[xb, stb, gb], writes=[hb])

    def dma_transpose_to(self, h, hb, dstT, dst_buf, col0, sem):
        for kc in range(8):
            self.p.op("sp", lambda e, kc=kc: e.dma_start_transpose(out=dstT[:, kc, col0:col0 + 128],
                                                                   in_=h[:, kc * 128:(kc + 1) * 128]),
                      reads=[hb], writes=[dst_buf], dma_sem=sem)

    def transpose_to(self, h, hb, dstT, dst_buf, col0, evac_eng):
        pt, ptb = self.next_pst()
        for kc in range(8):
            self.p.op("pe", lambda e, kc=kc: e.transpose(pt[:, kc * 128:(kc + 1) * 128], h[:, kc * 128:(kc + 1) * 128],
                                                         self.ident[:]),
                      reads=[hb, self.cb], writes=[ptb])
        self.copy(evac_eng, dstT[:, :, col0:col0 + 128], pt[:].rearrange("p (k t) -> p k t", k=8), reads=[ptb],
                  writes=[dst_buf])

    def phase_M(self, l):
        p = self.p
        w_mem = self.w_mem_sb
        wmb = self.wmb
        gt = self.sb((128, D), F32, "gtm")
        gb = p.buf("gtm")
        self.dma("sp", gt[:], self.g_mem[l], "gt", writes=[gb])
        xts = [(self.sb((128, D), F32, "xt"), p.buf("xt")) for _ in range(2)]
        hs = [(self.sb((128, D), BF16, "h"), p.buf("h")) for _ in range(2)]
        junk = self.sb((128, D), BF16, "junk")
        jb = p.buf("junk")
        sts = [(self.sb((128, 4), F32, "st"), p.buf("st")) for _ in range(2)]
        memT = self.sb((128, 8, 256), BF16, "memT")
        memTb = p.buf("memT")
        for mt in range(2):
            xt, xb = xts[mt]
            hq, hb = hs[mt]
            st, stb = sts[mt]
            self.tok_prep(self.mem_in[mt * 128:(mt + 1) * 128, :], xt, xb, f"xt{mt}", hq, hb, junk, jb, st, stb,
                          gt, gb)
            self.transpose_to(hq, hb, memT, memTb, mt * 128, "dve")
        for hm in range(4):
            pt, pb = self.next_ps()
            for kc in range(8):
                self.mm(pt[:, 0:256], w_mem[:, kc, hm * 128:(hm + 1) * 128], memT[:, kc, :], kc == 0, kc == 7,
                        reads=[wmb, memTb], writes=[pb])
            self.copy("act", self.KTm[:, hm, :], pt[:, 0:256], reads=[pb], writes=[self.memb])
        for mt in range(2):
            pt, pb = self.next_ps()
            for kc in range(8):
                self.mm(pt[:, :], memT[:, kc, mt * 128:(mt + 1) * 128], w_mem[:, kc, 512:1024], kc == 0, kc == 7,
                        reads=[wmb, memTb], writes=[pb])
            self.copy("dve", self.Vm[:, mt, :], pt[:, :], reads=[pb], writes=[self.memb])

    def phase_A(self, l, x_src):
        p = self.p
        epsbuf = self.cb
        w_in, w_uq, w_ukv = self.w_in_sb, self.w_uq_sb, self.w_ukv_sb
        wbuf_of = {id(w_uq): [self.wuqb], id(w_ukv): [self.wukvb]}
        gt = self.sb((128, D), F32, "gta")
        gb = p.buf("gta")
        self.dma("sp", gt[:], self.g_attn[l], "gt", writes=[gb])
        xts = [(self.sb((128, D), F32, "xt"), p.buf("xt")) for _ in range(4)]
        hs = [(self.sb((128, D), BF16, "h"), p.buf("h")) for _ in range(4)]
        junk = self.sb((128, D), BF16, "junk")
        jb = p.buf("junk")
        sts = [(self.sb((128, 4), F32, "st"), p.buf("st")) for _ in range(4)]

        hT = [(self.sb((128, 8, 512), BF16, "hT"), p.buf("hT")) for _ in range(1)]
        cqn = self.sb((128, 3, 512), BF16, "cqn")
        cqnb = p.buf("cqn")
        ckvn = self.sb((128, 2, 512), BF16, "ckvn")
        ckvnb = p.buf("ckvn")
        sq = [(self.sb((128, 512), BF16, "sq"), p.buf("sq")) for _ in range(5)]
        csb = [(self.sb((128, 512), F32, "csb"), p.buf("csb")) for _ in range(5)]
        hT_cur = [None, None]
        rs = [(self.sb((128, 512), F32, "rs"), p.buf("rs")) for _ in range(2)]
        tabs = [(self.sb((128, 2, 512), F32, "tab"), p.buf("tab")) for _ in range(2)]
        rt = [(self.sb((128, 2, 512), F32, "rt"), p.buf("rt")) for _ in range(1)]
        stg = [(self.sb((128, 512), BF16, "stg"), p.buf("stg")) for _ in range(12)]
        self.stg_rr = 0

        def next_stg():
            i = self.stg_rr % len(stg)
            self.stg_rr += 1
            return stg[i][0], stg[i][1], f"stg{i}"

        def queue_of(sem):
            return "pool" if (sem.startswith("stg") and int(sem[3:]) % 3 == 2) else "sp"

        def prep_load(c):
            for tt in range(4):
                T = c * 4 + tt
                xt, xb = xts[tt]
                self.dma("sp", xt[:], x_src[T * 128:(T + 1) * 128, :], f"xt{tt}", writes=[xb])

        def prep(c):
            for tt in range(4):
                i = (c * 4 + tt)
                xt, xb = xts[tt]
                hq, hb = hs[i % 4]
                st, stb = sts[i % 4]
                self.tok_prep(None, xt, xb, None, hq, hb, junk, jb, st, stb, gt, gb)
            tb_t, tb_b = tabs[c % 2]
            self.dma("sp", tb_t[:, 0, :], self.c_ropeC[:, c * 512:(c + 1) * 512], f"tab{c % 2}", writes=[tb_b])
            self.dma("sp", tb_t[:, 1, :], self.c_ropeS[:, c * 512:(c + 1) * 512], f"tab{c % 2}", writes=[tb_b])

        def proj(pt, pb, w, wcols, KC, rhsT, rb, M):
            if id(w) in wbuf_of:
                wbs = wbuf_of[id(w)]
            else:
                wbs = [self.wA0b] if wcols[1] <= WSPLIT else ([self.wA1b] if wcols[0] >= WSPLIT else
                                                             [self.wA0b, self.wA1b])
            for kc in range(KC):
                self.mm(pt[0:M, :], w[:, kc, wcols[0]:wcols[1]], rhsT[:, kc, :], kc == 0, kc == KC - 1,
                        reads=wbs + [rb], writes=[pb])

        def store(src_ap, dst_ap, sbuf_, sem):
            self.dma(queue_of(sem), dst_ap, src_ap, sem, reads=[sbuf_])

        ev = [0]

        def evac_copy(pt, pb, M=128):
            s_t, s_b, sem = next_stg()
            eng = "act" if ev[0] % 2 == 0 else "dve"
            ev[0] += 1
            self.copy(eng, s_t[0:M, :], pt[0:M, :], reads=[pb], writes=[s_b])
            return s_t, s_b, sem

        def latent_start(cols, k0):
            items = []
            for j, c0 in enumerate(cols):
                pt, pb = self.next_ps()
                proj(pt, pb, w_in, (c0, c0 + 128), 8, hT_cur[0], hT_cur[1], 128)
                s_t, s_b = sq[k0 + j]
                c_t, c_b = csb[k0 + j]
                self.act(s_t[:], pt[:], AF.Square, reads=[pb], writes=[s_b])
                self.copy("dve", c_t[:], pt[:], reads=[pb], writes=[c_b])
                items.append((s_t, s_b, c_t, c_b))
            return items

        def latent_finish(items, nfeat, gain, dst, dstb, k):
            pss, pssb = self.next_ps()
            n = len(items)
            for j, (s_t, s_b, c_t, c_b) in enumerate(items):
                self.mm(pss[:, :], self.ones[:], s_t[:], j == 0, j == n - 1, reads=[s_b, self.cb], writes=[pssb])
            r_t, r_b = rs[k]
            self.act(r_t[:], pss[:], AF.Sqrt, reads=[pssb, epsbuf], writes=[r_b], scale=1.0 / nfeat,
                     bias=self.epsb[:, 0:1])
            self.recip(r_t[:], r_t[:], reads=[r_b], writes=[r_b])
            for j, (s_t, s_b, c_t, c_b) in enumerate(items):
                self.stt("dve", dst[:, j, :], c_t[:], gain[:, j:j + 1], r_t[:], ALU.mult, ALU.mult,
                         reads=[c_b, r_b, self.cb], writes=[dstb])

        def rope(ptA, pbA, ptB, pbB, M, tab, tabb, c):
            r_t, r_b = rt[0]
            self.tt("dve", r_t[0:M, 0, :], ptA[0:M, :], tab[0:M, 0, :], ALU.mult, reads=[pbA, tabb], writes=[r_b])
            self.tt("dve", r_t[0:M, 1, :], ptB[0:M, :], tab[0:M, 1, :], ALU.mult, reads=[pbB, tabb], writes=[r_b])
            s_t, s_b, sem = next_stg()
            self.tt("pool", s_t[0:M, :], r_t[0:M, 0, :], r_t[0:M, 1, :], ALU.add, reads=[r_b], writes=[s_b])
            return s_t, s_b, sem

        def xpose(c):
            for tt in range(4):
                i = c * 4 + tt
                hq, hb = hs[i % 4]
                self.dma_transpose_to(hq, hb, hT[0][0], hT[0][1], tt * 128, "hT")

        prep_load(0)
        prep(0)
        for c in range(NCH):
            tk = slice(c * 512, (c + 1) * 512)
            hT_t, hT_b = hT[0]
            tab_t, tab_b = tabs[c % 2]
            if c == 0:
                xpose(0)
            if c + 1 < NCH:
                prep_load(c + 1)
            hT_cur[0], hT_cur[1] = hT_t, hT_b
            it_q = latent_start([0, 128, 256], 0)
            it_kv = latent_start([384, 512], 3)
            ptA, pbA = self.next_ps()
            proj(ptA, pbA, w_in, (640, 672), 8, hT_t, hT_b, 32)
            ptB, pbB = self.next_ps()
            proj(ptB, pbB, w_in, (4000, 4032), 8, hT_t, hT_b, 32)
            s_t, s_b, sem = rope(ptA, pbA, ptB, pbB, 32, tab_t, tab_b, c)
            store(s_t[0:32, :], self.KPE[:, tk], s_b, sem)
            zcols = [672 + m * 128 for m in range(8)] + [2464 + m * 128 for m in range(4)] + \
                    [3488 + m * 128 for m in range(4)]
            for m, c0 in enumerate(zcols):
                pt, pb = self.next_ps()
                proj(pt, pb, w_in, (c0, c0 + 128), 8, hT_t, hT_b, 128)
                s_t, s_b, sem = next_stg()
                self.act(s_t[:], pt[:], AF.Silu, reads=[pb], writes=[s_b])
                store(s_t[:], self.G[m * 128:(m + 1) * 128, tk], s_b, sem)
                if m == 1:
                    latent_finish(it_q, 384, self.gq[:, l, :], cqn, cqnb, 0)
                    latent_finish(it_kv, 256, self.gkv[:, l, :], ckvn, ckvnb, 1)
            if c + 1 < NCH:
                prep(c + 1)
            for m in range(4):
                pt, pb = self.next_ps()
                c0 = 1696 + m * 128
                proj(pt, pb, w_in, (c0, c0 + 128), 8, hT_t, hT_b, 128)
                s_t, s_b, sem = evac_copy(pt, pb)
                store(s_t[:], self.QS[m * 128:(m + 1) * 128, tk], s_b, sem)
            pt, pb = self.next_ps()
            proj(pt, pb, w_in, (2208, 2336), 8, hT_t, hT_b, 128)
            s_t, s_b, sem = evac_copy(pt, pb)
            store(s_t[:], self.KS[:, tk], s_b, sem)
            for m in range(4):
                pt, pb = self.next_ps()
                c0 = 2976 + m * 128
                proj(pt, pb, w_in, (c0, c0 + 128), 8, hT_t, hT_b, 128)
                s_t, s_b, sem = evac_copy(pt, pb)
                store(s_t[:], self.QM[m, :, tk], s_b, sem)
            for tt in range(4):
                T = c * 4 + tt
                pt, pb = self.next_ps()
                for kc in range(8):
                    self.mm(pt[:, 0:128], hT_t[:, kc, tt * 128:(tt + 1) * 128], w_in[:, kc, 2336:2464], kc == 0,
                            kc == 7, reads=[self.wA1b, hT_b], writes=[pb])
                vs_t, vs_b = self.vsbuf[T % 2]
                pv = pt[:, 0:128].rearrange("p (k d) -> p k d", k=2)
                v4 = vs_t[:].rearrange("p (k v) c -> p k v c", k=2)
                ee = "dve" if tt % 2 == 0 else "act"
                self.copy(ee, v4[:, :, 0, 0:64], pv, reads=[pb], writes=[vs_b])
                self.copy(ee, v4[:, :, 1, 64:128], pv, reads=[pb], writes=[vs_b])
                store(vs_t[:], self.VS[T], vs_b, f"vsb{T % 2}")
            if c + 1 < NCH:
                xpose(c + 1)
            for tt in range(4):
                T = c * 4 + tt
                vb_t, vb_b = self.vbuf[T % 4]
                for half in range(2):
                    pt, pb = self.next_ps()
                    for kc in range(2):
                        self.mm(pt[:, :], ckvn[:, kc, tt * 128:(tt + 1) * 128],
                                w_ukv[:, kc, 1024 + half * 512:1024 + (half + 1) * 512], kc == 0, kc == 1,
                                reads=[self.wukvb, ckvnb], writes=[pb])
                    pv = pt[:, :].rearrange("p (h two d) -> p h two d", two=2, d=64)
                    vv = vb_t[:, half * 8:(half + 1) * 8, :].rearrange("p (h two) c -> p h two c", two=2)
                    ee = "dve" if half == 0 else "act"
                    self.copy(ee, vv[:, :, 0, 0:64], pv[:, :, 0, :], reads=[pb], writes=[vb_b])
                    self.copy(ee, vv[:, :, 1, 64:128], pv[:, :, 1, :], reads=[pb], writes=[vb_b])
                store(vb_t[:], self.VM[T], vb_b, f"vb{T % 4}")
            for m in range(8):
                pt, pb = self.next_ps()
                proj(pt, pb, w_uq, (m * 128, (m + 1) * 128), 3, cqn, cqnb, 128)
                s_t, s_b, sem = evac_copy(pt, pb)
                store(s_t[:], self.QN[m * 128:(m + 1) * 128, tk], s_b, sem)
            for m in range(4):
                ptA, pbA = self.next_ps()
                proj(ptA, pbA, w_uq, (1024 + m * 128, 1024 + (m + 1) * 128), 3, cqn, cqnb, 128)
                ptB, pbB = self.next_ps()
                proj(ptB, pbB, w_uq, (1536 + m * 128, 1536 + (m + 1) * 128), 3, cqn, cqnb, 128)
                s_t, s_b, sem = rope(ptA, pbA, ptB, pbB, 128, tab_t, tab_b, c)
                store(s_t[:], self.QR[m * 128:(m + 1) * 128, tk], s_b, sem)
            for m in range(8):
                pt, pb = self.next_ps()
                proj(pt, pb, w_ukv, (m * 128, (m + 1) * 128), 2, ckvn, ckvnb, 128)
                s_t, s_b, sem = evac_copy(pt, pb)
                store(s_t[:], self.KN[m * 128:(m + 1) * 128, tk], s_b, sem)
    def phase_B(self, l):
        p = self.p
        QTb = [(self.sb((128, S), BF16, "QTb"), p.buf("QTb")) for _ in range(2)]
        KTb = [(self.sb((128, S), BF16, "KTb"), p.buf("KTb")) for _ in range(2)]
        Vb = [(self.sb((128, NT, 128), BF16, "Vb"), p.buf("Vb")) for _ in range(2)]
        Vq = [[p.buf("Vq") for _ in range(4)] for _ in range(2)]
        Gb = [(self.sb((128, S), BF16, "Gb"), p.buf("Gb")) for _ in range(2)]
        PT = [(self.sb((128, 512), BF16, "PT"), p.buf("PT")) for _ in range(4)]
        Rb = [(self.sb((128, 512), F32, "R"), p.buf("R")) for _ in range(2)]
        Y1 = [(self.sb((128, 512), F32, "Y1"), p.buf("Y1")) for _ in range(2)]
        yb = [(self.sb((128, 512), BF16, "yb"), p.buf("yb")) for _ in range(3)]
        ST = self.ps[0:3]
        OT = self.ps[3:5]
        SM = self.ps[5]
        cnt = dict(ep=0, yb=0, u=0)
        LA = 2

        def run_pipeline(units):
            n = len(units)
            base = cnt["u"]
            cnt["u"] += n
            for i in range(min(LA, n)):
                units[i]["qk"](base + i)
            pending = None
            for i in range(n):
                if i + LA < n:
                    units[i + LA]["qk"](base + i + LA)
                units[i]["ex"](base + i)
                if pending is not None:
                    pending()
                    pending = None
                units[i]["pv"](base + i)
                if "post" in units[i]:
                    pending = units[i]["post"]
            if pending is not None:
                pending()

        for i in range(2):
            self.dma("sp", KTb[i][0][0:32, :], self.KPE[:, :], f"KTb{i}", writes=[KTb[i][1]])

        def epilogue(o_t, o_b, s_t, s_b, o_sl, s_sl, g_t, g_b, c, dst_ap, sink_ap=None, act_recip=False):
            i = cnt["ep"]
            cnt["ep"] += 1
            r_t, r_b = Rb[i % 2]
            y1_t, y1_b = Y1[i % 2]
            j = cnt["yb"]
            cnt["yb"] += 1
            y_t, y_b = yb[j % 3]
            if act_recip:
                kw = dict(bias=sink_ap) if sink_ap is not None else {}
                self.act(r_t[o_sl, :], s_t[s_sl, :], AF.Ln, reads=[s_b, self.cb], writes=[r_b], **kw)
                self.act(r_t[o_sl, :], r_t[o_sl, :], AF.Exp, reads=[r_b], writes=[r_b], scale=-1.0)
            else:
                self.recip(r_t[o_sl, :], s_t[s_sl, :], reads=[s_b], writes=[r_b])
            self.tt("dve", y1_t[o_sl, :], o_t[o_sl, :], r_t[o_sl, :], ALU.mult, reads=[o_b, r_b], writes=[y1_b])
            self.tt("pool", y_t[o_sl, :], y1_t[o_sl, :], g_t[o_sl, c * 512:(c + 1) * 512], ALU.mult,
                    reads=[y1_b, g_b], writes=[y_b])
            self.dma("sp", dst_ap, y_t[o_sl, :], f"yb{j % 3}", reads=[y_b])

        sc_mla = float(96 ** -0.5)

        def load_mla(h):
            i = h % 2
            self.dma("sp", QTb[i][0][0:32, :], self.QR[h * 32:(h + 1) * 32, :], f"QTb{i}", writes=[QTb[i][1]])
            self.dma("sp", QTb[i][0][32:96, :], self.QN[h * 64:(h + 1) * 64, :], f"QTb{i}", writes=[QTb[i][1]])
            self.dma("sp", KTb[i][0][32:96, :], self.KN[h * 64:(h + 1) * 64, :], f"KTb{i}", writes=[KTb[i][1]])
            par = h % 2
            vm = self.VM.rearrange("t p h c -> p t h c")
            for qd in range(4):
                self.dma("sp", Vb[i][0][:, qd * 8:(qd + 1) * 8, :], vm[:, qd * 8:(qd + 1) * 8, h, :], f"Vq{i}_{qd}",
                         writes=[Vq[i][qd]], reads=[Vb[i][1]])
                if qd == 0:
                    self.dma("sp", Gb[i][0][par * 64:(par + 1) * 64, :], self.G[h * 64:(h + 1) * 64, :], f"Gb{i}",
                             writes=[Gb[i][1]])

        def mla_unit(h, c, t):
            i2 = h % 2
            q_t, q_b = QTb[i2]
            k_t, k_b = KTb[i2]
            v_t, v_b = Vb[i2]
            g_t, g_b = Gb[i2]
            par = h % 2
            o_sl = slice(par * 64, (par + 1) * 64)
            s_sl = slice((1 - par) * 64, (2 - par) * 64)
            j = t - 4 * c
            lo = max(j, 0) * 128
            o_t, o_b = OT[c % 2]

            def qk(i):
                st_t, st_b = ST[i % 3]
                self.mm(st_t[:, lo:512], k_t[0:96, t * 128:(t + 1) * 128], q_t[0:96, c * 512 + lo:(c + 1) * 512], True,
                        j < 0, reads=[k_b, q_b], writes=[st_b])
                if j >= 0:
                    self.mm(st_t[:, lo:lo + 128], self.ident[:], self.tri[:], False, True, reads=[self.cb],
                            writes=[st_b])

            def ex(i):
                st_t, st_b = ST[i % 3]
                pt_t, pt_b = PT[i % 4]
                self.act(pt_t[:, lo:512], st_t[:, lo:512], AF.Exp, reads=[st_b], writes=[pt_b], scale=sc_mla)

            def pv(i):
                pt_t, pt_b = PT[i % 4]
                self.mm(o_t[:, lo:512], v_t[:, t, :], pt_t[:, lo:512], t == 0, t == 4 * c + 3,
                        reads=[Vq[i2][t // 8], pt_b], writes=[o_b])

            u = dict(qk=qk, ex=ex, pv=pv)
            if t == 4 * c + 3:
                def post():
                    epilogue(o_t, o_b, o_t, o_b, o_sl, s_sl, g_t, g_b, c,
                             self.YT[h // 2, par * 64:(par + 1) * 64, c * 512:(c + 1) * 512])
                    if c == NCH - 1 and h + 2 < 16:
                        load_mla(h + 2)
                    if c == NCH - 1 and h == 0:
                        self.load_weight(self.w_out_sb, self.w_out[l], 16, D, self.wob, "w_out")
                    if c == NCH - 1 and h == 14:
                        swa_preload(0)
                u["post"] = post
            return u

        def swa_preload(i):
            p.op("pool", lambda e, t=QTb[i][0]: e.memset(t[64:128, :], 0.0), writes=[QTb[i][1]])
            p.op("pool", lambda e, t=KTb[i][0]: e.memset(t[64:128, :], 0.0), writes=[KTb[i][1]])
            self.dma("sp", KTb[i][0][0:64, :], self.KS[i * 64:(i + 1) * 64, :], f"KTb{i}", writes=[KTb[i][1]])
            load_swa_v(0, i)
            load_swa_q(i)

        def load_swa_v(kvh, var):
            self.dma("sp", Vb[var][0][:], self.VS.rearrange("t p k c -> p t k c")[:, :, kvh * 2 + var, :], f"Vb{var}",
                     writes=[Vb[var][1]] + Vq[var])

        def load_swa_q(h):
            par = h % 2
            self.dma("sp", QTb[h % 2][0][0:64, :], self.QS[h * 64:(h + 1) * 64, :], f"QTb{h % 2}", writes=[QTb[h % 2][1]])
            self.dma("sp", Gb[h % 2][0][par * 64:(par + 1) * 64, :], self.G[1024 + h * 64:1024 + (h + 1) * 64, :],
                     f"Gb{h % 2}", writes=[Gb[h % 2][1]])

        def load_mem(hm):
            self.dma("sp", QTb[hm % 2][0][:, :], self.QM[hm, :, :], f"QTb{hm % 2}", writes=[QTb[hm % 2][1]])
            self.dma("sp", Gb[hm % 2][0][:, :], self.G[1536 + hm * 128:1536 + (hm + 1) * 128, :], f"Gb{hm % 2}",
                     writes=[Gb[hm % 2][1]])

        load_mla(0)
        load_mla(1)
        units = [mla_unit(h, c, t) for h in range(16) for c in range(NCH) for t in range(4 * c + 4)]
        run_pipeline(units)
        swa_preload(1)

        sc_swa = 0.125

        def swa_unit(h, c, half):
            kvh = h // 4
            par = h % 2
            ks_t, ks_b = KTb[kvh]
            q_t, q_b = QTb[h % 2]
            g_t, g_b = Gb[h % 2]
            v_t, v_b = Vb[par]
            o_sl = slice(par * 64, (par + 1) * 64)
            s_sl = slice((1 - par) * 64, (2 - par) * 64)
            o_t, o_b = OT[c % 2]
            slots = []
            for nn in range(2):
                n = c * 4 + half * 2 + nn
                for typ in range(2):
                    t = n - 1 + typ
                    if t >= 0:
                        slots.append((nn, n, typ, t, (nn * 2 + typ) * 128))
            lo = min(s[4] for s in slots)

            def qk(i):
                st_t, st_b = ST[i % 3]
                for (nn, n, typ, t, col) in slots:
                    self.mm(st_t[:, col:col + 128], ks_t[:, t * 128:(t + 1) * 128], q_t[:, n * 128:(n + 1) * 128], True,
                            False, reads=[ks_b, q_b], writes=[st_b])
                    self.mm(st_t[:, col:col + 128], self.ident[:],
                            self.swab[:, (h * 2 + typ) * 128:(h * 2 + typ + 1) * 128], False, True,
                            reads=[self.cb], writes=[st_b])

            def ex(i):
                st_t, st_b = ST[i % 3]
                pt_t, pt_b = PT[i % 4]
                self.act(pt_t[:, lo:512], st_t[:, lo:512], AF.Exp, reads=[st_b], writes=[pt_b], scale=sc_swa)

            def pv(i):
                pt_t, pt_b = PT[i % 4]
                first_n = {}
                for (nn, n, typ, t, col) in slots:
                    oc = (half * 2 + nn) * 128
                    self.mm(o_t[:, oc:oc + 128], v_t[:, t, :], pt_t[:, col:col + 128], n not in first_n, typ == 1,
                            reads=[v_b, pt_b], writes=[o_b])
                    first_n[n] = True

            u = dict(qk=qk, ex=ex, pv=pv)
            if half == 1:
                def post():
                    epilogue(o_t, o_b, o_t, o_b, o_sl, s_sl, g_t, g_b, c,
                             self.YT[8 + h // 2, par * 64:(par + 1) * 64, c * 512:(c + 1) * 512],
                             sink_ap=self.esink[s_sl, l, h:h + 1], act_recip=True)
                    if c == NCH - 1:
                        if h + 2 < 8:
                            load_swa_q(h + 2)
                        if h == 2:
                            load_swa_v(1, 0)
                        if h == 3:
                            load_swa_v(1, 1)
                        if h >= 6:
                            load_mem(h - 6)
                u["post"] = post
            return u

        units = [swa_unit(h, c, half) for h in range(8) for c in range(NCH) for half in range(2)]
        run_pipeline(units)

        sc_mem = float(128 ** -0.5)
        def mem_unit(hm, c, mt):
            q_t, q_b = QTb[hm % 2]
            q2 = q_t
            g_t, g_b = Gb[hm % 2]
            o_t, o_b = OT[c % 2]
            s_t, s_b = SM

            def qk(i):
                st_t, st_b = ST[i % 3]
                self.mm(st_t[:, :], self.KTm[:, hm, mt * 128:(mt + 1) * 128], q2[:, c * 512:(c + 1) * 512], True,
                        True, reads=[self.memb, q_b], writes=[st_b])

            def ex(i):
                st_t, st_b = ST[i % 3]
                pt_t, pt_b = PT[i % 4]
                self.act(pt_t[:, :], st_t[:, :], AF.Exp, reads=[st_b], writes=[pt_b], scale=sc_mem)

            def pv(i):
                pt_t, pt_b = PT[i % 4]
                self.mm(o_t[:, :], self.Vm[:, mt, hm * 128:(hm + 1) * 128], pt_t[:, :], mt == 0, mt == 1,
                        reads=[self.memb, pt_b], writes=[o_b])
                self.mm(s_t[:, :], self.ones[:], pt_t[:, :], mt == 0, mt == 1, reads=[self.cb, pt_b],
                        writes=[s_b])

            u = dict(qk=qk, ex=ex, pv=pv)
            if mt == 1:
                def post():
                    sl = slice(0, 128)
                    epilogue(o_t, o_b, s_t, s_b, sl, sl, g_t, g_b, c, self.YT[12 + hm, :, c * 512:(c + 1) * 512],
                             act_recip=True)
                    if c == NCH - 1 and hm + 2 < 4:
                        load_mem(hm + 2)
                u["post"] = post
            return u

        units = [mem_unit(hm, c, mt) for hm in range(4) for c in range(NCH) for mt in range(2)]
        run_pipeline(units)

    def phase_C(self, l, x_src, last, prefetch):
        p = self.p
        w_out, wb = self.w_out_sb, self.wob
        CW = 256
        yT = [(self.sb((128, 16, CW), BF16, "yT"), p.buf("yT")) for _ in range(2)]
        xc = [(self.sb((128, D), F32, "xc"), p.buf("xc")) for _ in range(2)]
        xn = [(self.sb((128, D), F32, "xn"), p.buf("xn")) for _ in range(2)]
        junk = self.sb((128, D), BF16, "junkc")
        jb = p.buf("junkc")
        sts = [(self.sb((128, 4), F32, "stc"), p.buf("stc")) for _ in range(4)]
        p.op("dve", lambda e: e.memset(junk[:], 0.0), writes=[jb])
        nchunk = S // CW

        def load_chunk(c):
            t, b = yT[c % 2]
            for half in range(2):
                self.dma("sp", t[:, half * 8:(half + 1) * 8, :],
                         self.YT.rearrange("k p s -> p k s")[:, half * 8:(half + 1) * 8, c * CW:(c + 1) * CW],
                         f"yT{c % 2}", writes=[b])

        def load_x(T):
            x_t, x_b = xc[T % 2]
            self.dma("sp", x_t[:], x_src[T * 128:(T + 1) * 128, :], f"xc{T % 2}", writes=[x_b])

        load_chunk(0)
        load_x(0)
        for c in range(nchunk):
            if c + 1 < nchunk:
                load_chunk(c + 1)
            y_t, y_b = yT[c % 2]
            for tt in range(CW // 128):
                T = c * (CW // 128) + tt
                x_t, x_b = xc[T % 2]
                n_t, n_b = xn[T % 2]
                if T + 1 < NT:
                    load_x(T + 1)
                for half in range(2):
                    pt, pb = self.next_ps()
                    for kc in range(16):
                        self.mm(pt[:, :], y_t[:, kc, tt * 128:(tt + 1) * 128], w_out[:, kc, half * 512:(half + 1) * 512],
                                kc == 0, kc == 15, reads=[y_b, wb], writes=[pb])
                    mark = self.tt("dve", n_t[:, half * 512:(half + 1) * 512], pt[:, :],
                                   x_t[:, half * 512:(half + 1) * 512], ALU.add, reads=[pb, x_b], writes=[n_b])
                if prefetch:
                    prefetch.pop(0)(after=(mark,))
                if not last:
                    self.dma("sp", self.XRES[T * 128:(T + 1) * 128, :], n_t[:], f"xn{T % 2}", reads=[n_b])
                else:
                    st, stb = sts[T % 4]
                    self.act(junk[:], n_t[:], AF.Square, reads=[n_b], writes=[jb, stb], accum_out=st[:, 0:1])
                    self.act(st[:, 1:2], st[:, 0:1], AF.Sqrt, reads=[stb, self.cb], writes=[stb], scale=1.0 / D,
                             bias=self.epsb[:, 0:1])
                    self.recip(st[:, 2:3], st[:, 1:2], reads=[stb], writes=[stb])
                    self.stt("dve", n_t[:], n_t[:], st[:, 2:3], self.gfin[:], ALU.mult, ALU.mult,
                             reads=[n_b, stb, self.cb], writes=[n_b])
                    self.dma("sp", self.y_out[T * 128:(T + 1) * 128, :], n_t[:], f"xn{T % 2}", reads=[n_b])
        while prefetch:
            prefetch.pop(0)()


def _consts():
    ident = np.eye(128, dtype=np.float32)
    k = np.arange(128)[:, None]
    q = np.arange(128)[None, :]
    tri = np.where(q >= k, 0.0, NEGB).astype(np.float32)
    swab = np.zeros((128, 16, 128), np.float32)
    for h in range(8):
        slope = 2.0 ** (-(h + 1))
        d_prev = q + 128 - k
        swab[:, h * 2 + 0, :] = np.where(d_prev < 128, -8.0 * slope * d_prev, NEGB)
        d_cur = q - k
        swab[:, h * 2 + 1, :] = np.where(d_cur >= 0, -8.0 * slope * d_cur, NEGB)
    inv = (10000.0 ** (-np.arange(0, 32, 2, dtype=np.float32) / np.float32(32))).astype(np.float32)
    ang = (np.arange(S, dtype=np.float32)[:, None] * inv[None, :]).astype(np.float32)
    cos = np.cos(ang).astype(np.float32).T
    sin = np.sin(ang).astype(np.float32).T
    C32 = np.concatenate([cos, cos], 0)
    S32 = np.concatenate([-sin, sin], 0)
    ropeC = np.ascontiguousarray(np.tile(C32, (4, 1)))
    ropeS = np.ascontiguousarray(np.tile(S32, (4, 1)))
    return dict(c_ident=ident, c_tri=tri, c_swab=np.ascontiguousarray(swab.reshape(128, 16 * 128)),
                c_ropeC=ropeC, c_ropeS=ropeS)


def _layout_weights(attn_norm, w_in, mla_q_norm, w_uq, mla_kv_norm, w_ukv, swa_sinks, mem_norm, w_mem_kv, w_out,
                    final_norm):
    L = w_in.shape[0]
    f = lambda a: np.ascontiguousarray(a, dtype=np.float32)
    swap_cols = np.concatenate([np.arange(656, 672), np.arange(640, 656)])
    w_in_ext = np.concatenate([w_in, w_in[:, :, swap_cols]], axis=2)
    hq = np.arange(16)[:, None] * 96
    nope = (hq + np.arange(64)[None, :]).reshape(-1)
    rope = (hq + 64 + np.arange(32)[None, :]).reshape(-1)
    sw = np.concatenate([np.arange(16, 32), np.arange(0, 16)])
    ropes = (hq + 64 + sw[None, :]).reshape(-1)
    w_uq_r = w_uq[:, :, np.concatenate([nope, rope, ropes])]
    hk = np.arange(16)[:, None] * 128
    kn = (hk + np.arange(64)[None, :]).reshape(-1)
    vv = (hk + 64 + np.arange(64)[None, :]).reshape(-1)
    w_ukv_r = w_ukv[:, :, np.concatenate([kn, vv])]
    pk = lambda g, kc: np.ascontiguousarray(g.reshape(L, kc, 128).transpose(0, 2, 1))
    return dict(
        w_in=f(w_in_ext), w_uq=f(w_uq_r), w_ukv=f(w_ukv_r), w_mem=f(w_mem_kv), w_out=f(w_out),
        g_attn=f(np.broadcast_to(attn_norm[:, None, :], (L, 128, D))), g_q=f(pk(mla_q_norm, 3)),
        g_kv=f(pk(mla_kv_norm, 2)), g_mem=f(np.broadcast_to(mem_norm[:, None, :], (L, 128, D))),
        sinks=f(np.broadcast_to(swa_sinks[:, None, :], (L, 128, 8))),
        g_final=f(np.broadcast_to(final_norm[None, :], (128, D))),
    )


_CACHE = {}


def _get_nc():
    if "nc" not in _CACHE:
        b = Builder()
        _CACHE["nc"] = b.build()
        _CACHE["stats"] = b.stats
    return _CACHE["nc"]


def kernel(x, mem, attn_norm, w_in, mla_q_norm, w_uq, mla_kv_norm, w_ukv, swa_sinks, mem_norm, w_mem_kv, w_out,
           final_norm):
    x = np.asarray(x, dtype=np.float32)
    mem = np.asarray(mem, dtype=np.float32)
    B = x.shape[0]
    shared = _layout_weights(*[np.asarray(a, dtype=np.float32) for a in
                               (attn_norm, w_in, mla_q_norm, w_uq, mla_kv_norm, w_ukv, swa_sinks, mem_norm,
                                w_mem_kv, w_out, final_norm)])
    shared.update(_consts())
    nc = _get_nc()
    in_maps = []
    for b in range(B):
        m = dict(shared)
        m["x"] = np.ascontiguousarray(x[b])
        m["mem"] = np.ascontiguousarray(mem[b])
        in_maps.append(m)
    res = run_bass_kernel_spmd(nc, in_maps, core_ids=list(range(B)))
    return np.stack([np.asarray(r["y"], dtype=np.float32) for r in res.results], axis=0)
```

```python
import numpy as np
import concourse.bass as bass
import concourse.mybir as mybir
from concourse.bass_utils import run_bass_kernel_spmd

F32 = mybir.dt.float32
BF16 = mybir.dt.bfloat16
AF = mybir.ActivationFunctionType
ALU = mybir.AluOpType

S = 4096
D = 1024
NT = S // 128
NCH = S // 512
DEPTH = 2
WIN_COLS = 4032
WSPLIT = 1696
EPS = 1e-6
NEGB = -30000.0


class Buf:
    __slots__ = ("name", "w", "rc", "rd", "excl")

    def __init__(self, name, excl=False):
        self.name = name
        self.excl = excl
        self.w = None
        self.rc = {}
        self.rd = []


class Op:
    __slots__ = ("eng", "fn", "deps", "signaled", "token", "dma_sem")

    def __init__(self, eng, fn, dma_sem=None):
        self.eng = eng
        self.fn = fn
        self.deps = []
        self.signaled = False
        self.token = None
        self.dma_sem = dma_sem


class P:
    ENGS = ("pe", "act", "dve", "pool", "sp")

    def __init__(self, nc):
        self.nc = nc
        self.e = {"pe": nc.tensor, "act": nc.scalar, "dve": nc.vector, "pool": nc.gpsimd, "sp": nc.sync}
        self.ops = []
        self.last_real = {}
        self.last_dma = {}
        self.nbuf = 0

    def buf(self, name="b", excl=False):
        self.nbuf += 1
        return Buf(f"{name}{self.nbuf}", excl)

    def op(self, eng, fn, reads=(), writes=(), dma_sem=None, extra_deps=()):
        o = Op(eng, fn, dma_sem)
        deps = {id(d): d for d in extra_deps}
        is_dma = dma_sem is not None

        def add(d, raw):
            if d is None or d is o:
                return
            d_dma = d.dma_sem is not None
            if not is_dma and not d_dma and d.eng == eng:
                if eng == "pe" or not raw:
                    return
            deps[id(d)] = d

        for b in reads:
            add(b.w, True)
            if b.excl:
                for re, r in b.rc.items():
                    if re != eng:
                        add(r, False)
        for b in writes:
            if not (is_dma and b.w is not None and b.w.dma_sem == dma_sem and b.w.eng == eng):
                add(b.w, False)
            for r in b.rc.values():
                add(r, False)
            for r in b.rd:
                add(r, False)
        o.deps = list(deps.values())
        for d in o.deps:
            d.signaled = True
        for b in reads:
            if is_dma:
                b.rd.append(o)
            else:
                b.rc[eng] = o
        for b in writes:
            b.w = o
            b.rc = {}
            b.rd = []
        self.ops.append(o)
        if is_dma:
            self.last_dma[dma_sem] = o
        else:
            self.last_real[eng] = o
        return o

    def barrier(self, skip_sems=()):
        deps = [o for o in self.last_real.values()] + [o for k, o in self.last_dma.items() if k not in skip_sems]
        for d in deps:
            d.signaled = True
        for eng in self.ENGS:
            o = Op(eng, None)
            o.deps = [d for d in deps if not (d.dma_sem is None and d.eng == eng and eng == "pe")]
            self.ops.append(o)

    def emit(self):
        nc = self.nc
        esem = {e: nc.alloc_semaphore(f"es_{e}") for e in self.ENGS}
        dsem = {}
        ecnt = {e: 0 for e in self.ENGS}
        dcnt = {}
        waited = {e: {} for e in self.ENGS}
        nwait = 0
        for o in self.ops:
            eng = self.e[o.eng]
            need = {}
            for d in o.deps:
                sem, val = d.token
                k = id(sem)
                if k not in need or need[k][1] < val:
                    need[k] = (sem, val)
            for k, (sem, val) in need.items():
                if waited[o.eng].get(k, 0) < val:
                    eng.wait_ge(sem, val)
                    waited[o.eng][k] = val
                    nwait += 1
            if o.fn is None:
                continue
            ins = o.fn(eng)
            if o.dma_sem is not None:
                if o.dma_sem not in dsem:
                    dsem[o.dma_sem] = nc.alloc_semaphore(f"ds_{o.dma_sem}")
                    dcnt[o.dma_sem] = 0
                dcnt[o.dma_sem] += 16
                ins.then_inc(dsem[o.dma_sem], 16)
                o.token = (dsem[o.dma_sem], dcnt[o.dma_sem])
            elif o.signaled:
                ecnt[o.eng] += 1
                ins.then_inc(esem[o.eng], 1)
                o.token = (esem[o.eng], ecnt[o.eng])
        return dict(n_ops=len(self.ops), n_wait=nwait, ecnt=ecnt, n_dsem=len(dsem))


class Builder:
    def __init__(self, depth=DEPTH, dbg=False, stop_after=None):
        self.depth = depth
        self.dbg = dbg
        self.stop_after = stop_after
        self.nc = bass.Bass("TRN2", target_bir_lowering=False)
        self.p = P(self.nc)
        self.uid = 0
        self.guards = []
        self.region = None

    def sb(self, shape, dtype, name="t"):
        self.uid += 1
        if self.region is None:
            g = self.nc.sbuf_tensor(f"{name}_{self.uid}", list(shape), dtype)
            t = g.__enter__()
            self.guards.append(g)
            return t
        nbytes = 2 if dtype == BF16 else 4
        for d in shape[1:]:
            nbytes *= d
        nbytes = (nbytes + 63) // 64 * 64
        off = self.region[0]
        assert off + nbytes <= self.region[1], f"arena region overflow: {name} {off}+{nbytes} > {self.region[1]}"
        self.region[0] = off + nbytes
        return self.nc.alloc_sbuf_tensor_at(f"{name}_{self.uid}", list(shape), dtype, offset=off)

    def set_region(self, lo, hi):
        self.region = [lo, hi]

    def mark(self):
        return len(self.guards)

    def release(self, mark):
        while len(self.guards) > mark:
            g = self.guards.pop()
            g.__exit__(None, None, None)

    def dram(self, name, shape, dtype, kind="Internal"):
        if self.dbg and kind == "Internal":
            kind = "ExternalOutput"
        return self.nc.dram_tensor(name, list(shape), dtype, kind=kind).ap()

    def dma(self, q, out, in_, sem, reads=(), writes=(), after=()):
        return self.p.op(q, lambda e: e.dma_start(out=out, in_=in_), reads, writes, dma_sem=sem, extra_deps=after)

    def mm(self, out, lhsT, rhs, start, stop, reads=(), writes=()):
        return self.p.op("pe", lambda e: e.matmul(out, lhsT, rhs, start=start, stop=stop), reads, writes)

    def act(self, out, in_, func, reads=(), writes=(), **kw):
        return self.p.op("act", lambda e: e.activation(out=out, in_=in_, func=func, **kw), reads, writes)

    def copy(self, eng, out, in_, reads=(), writes=()):
        if eng == "act":
            return self.act(out, in_, AF.Copy, reads, writes)
        return self.p.op(eng, lambda e: e.tensor_copy(out=out, in_=in_), reads, writes)

    def tt(self, eng, out, in0, in1, op, reads=(), writes=()):
        return self.p.op(eng, lambda e: e.tensor_tensor(out=out, in0=in0, in1=in1, op=op), reads, writes)

    def ts(self, eng, out, in0, s1, op0, reads=(), writes=()):
        return self.p.op(eng, lambda e: e.tensor_scalar(out=out, in0=in0, scalar1=s1, scalar2=None, op0=op0),
                         reads, writes)

    def recip(self, out, in_, reads=(), writes=()):
        return self.p.op("dve", lambda e: e.reciprocal(out=out, in_=in_), reads, writes)

    def build(self):
        nc, p = self.nc, self.p
        L = self.depth
        ein = lambda n, s: nc.dram_tensor(n, list(s), F32, kind="ExternalInput").ap()
        self.x_in = ein("x", (S, D))
        self.mem_in = ein("mem", (256, D))
        self.w_in = ein("w_in", (L, D, WIN_COLS))
        self.w_uq = ein("w_uq", (L, 384, 2048))
        self.w_ukv = ein("w_ukv", (L, 256, 2048))
        self.w_mem = ein("w_mem", (L, D, 1024))
        self.w_out = ein("w_out", (L, 2048, D))
        self.g_attn = ein("g_attn", (L, 128, D))
        self.g_q = ein("g_q", (L, 128, 3))
        self.g_kv = ein("g_kv", (L, 128, 2))
        self.g_mem = ein("g_mem", (L, 128, D))
        self.sinks = ein("sinks", (L, 128, 8))
        self.g_final = ein("g_final", (128, D))
        self.c_ident = ein("c_ident", (128, 128))
        self.c_tri = ein("c_tri", (128, 128))
        self.c_swab = ein("c_swab", (128, 16 * 128))
        self.c_ropeC = ein("c_ropeC", (128, S))
        self.c_ropeS = ein("c_ropeS", (128, S))
        self.y_out = nc.dram_tensor("y", [S, D], F32, kind="ExternalOutput").ap()
        self.XRES = self.dram("XRES", (S, D), F32)
        self.QN = self.dram("QN", (16 * 64, S), BF16)
        self.QR = self.dram("QR", (16 * 32, S), BF16)
        self.KN = self.dram("KN", (16 * 64, S), BF16)
        self.KPE = self.dram("KPE", (32, S), BF16)
        self.VM = self.dram("VM", (NT, 128, 16, 128), BF16)
        self.QS = self.dram("QS", (8 * 64, S), BF16)
        self.KS = self.dram("KS", (2 * 64, S), BF16)
        self.VS = self.dram("VS", (NT, 128, 4, 128), BF16)
        self.QM = self.dram("QM", (4, 128, S), BF16)
        self.G = self.dram("G", (2048, S), BF16)
        self.YT = self.dram("YT", (16, 128, S), BF16)

        self.ps = []
        for i in range(6):
            t = nc.alloc_psum_tensor(f"ps{i}", [128, 512], F32)
            self.ps.append((t, p.buf("ps", excl=True)))
        self.pst = []
        for i in range(2):
            t = nc.alloc_psum_tensor(f"pst{i}", [128, 1024], BF16)
            self.pst.append((t, p.buf("pst", excl=True)))
        self.ps_rr = 0
        self.pst_rr = 0

        self.setup_consts()
        p.barrier()
        self.KTm = self.sb((128, 4, 256), BF16, "KTm")
        self.Vm = self.sb((128, 2, 512), BF16, "Vm")
        self.memb = p.buf("memkv")
        SB_END = 16512 + 212863
        base = SB_END - nc.sbuf_bytes_remaining
        base = (base + 63) // 64 * 64
        end = SB_END // 64 * 64
        self.set_region(base, end)
        self.w_in_sb = self.sb((128, 8, WIN_COLS), BF16, "w_in")
        self.w_uq_sb = self.sb((128, 3, 2048), BF16, "w_uq")
        self.w_ukv_sb = self.sb((128, 2, 2048), BF16, "w_ukv")
        o_wmem = self.region[0]
        self.w_mem_sb = self.sb((128, 8, 1024), BF16, "w_mem")
        o_wout = self.region[0]
        self.w_out_sb = self.sb((128, 16, D), BF16, "w_out")
        o_r2 = self.region[0]
        self.wA0b = p.buf("wA0")
        self.wA1b = p.buf("wA1")
        self.wuqb = p.buf("wuq")
        self.wukvb = p.buf("wukv")
        self.wmb = p.buf("wmem")
        self.wob = p.buf("wout")
        WSEMS = ("w_mem", "wA0", "wA1", "wuq", "wukv", "w_out")

        def load_WA(l):
            jobs = []

            def add(dst, src, sem, buf):
                jobs.append(lambda after=(): self.dma("pool", dst, src, sem, writes=[buf], after=after))

            for kc in range(8):
                add(self.w_mem_sb[:, kc, :], self.w_mem[l][kc * 128:(kc + 1) * 128, :], "w_mem", self.wmb)
            for kc in range(8):
                add(self.w_in_sb[:, kc, 0:WSPLIT], self.w_in[l][kc * 128:(kc + 1) * 128, 0:WSPLIT], "wA0", self.wA0b)
            for kc in range(8):
                add(self.w_in_sb[:, kc, WSPLIT:WIN_COLS], self.w_in[l][kc * 128:(kc + 1) * 128, WSPLIT:WIN_COLS], "wA1",
                    self.wA1b)
            for kc in range(3):
                add(self.w_uq_sb[:, kc, :], self.w_uq[l][kc * 128:(kc + 1) * 128, :], "wuq", self.wuqb)
            for kc in range(2):
                add(self.w_ukv_sb[:, kc, :], self.w_ukv[l][kc * 128:(kc + 1) * 128, :], "wukv", self.wukvb)
            return jobs

        for j in load_WA(0):
            j()
        for l in range(L):
            last = (l == L - 1)
            x_src = self.x_in if l == 0 else self.XRES
            self.set_region(o_wout, end)
            self.phase_M(l)
            p.barrier(skip_sems=WSEMS)
            self.set_region(o_wmem, end)
            self.phase_A(l, x_src)
            p.barrier(skip_sems=WSEMS)
            if self.stop_after == ("A", l):
                break
            self.set_region(base, o_wout)
            self.phase_B(l)
            p.barrier(skip_sems=WSEMS)
            if self.stop_after == ("B", l):
                break
            self.set_region(o_r2, end)
            self.phase_C(l, x_src, last, [] if last else load_WA(l + 1))
            p.barrier(skip_sems=WSEMS)
        p.barrier()
        self.stats = p.emit()
        return nc

    def next_ps(self):
        r = self.ps[self.ps_rr % len(self.ps)]
        self.ps_rr += 1
        return r

    def next_pst(self):
        r = self.pst[self.pst_rr % 2]
        self.pst_rr += 1
        return r

    def setup_consts(self):
        p = self.p
        self.epsb = self.sb((128, 1), F32, "eps")
        self.ident = self.sb((128, 128), BF16, "ident")
        self.tri = self.sb((128, 128), BF16, "tri")
        self.swab = self.sb((128, 16 * 128), BF16, "swab")
        self.ones = self.sb((128, 128), BF16, "ones")
        self.cb = p.buf("consts")
        p.op("dve", lambda e: e.memset(self.epsb[:], EPS), writes=[self.cb])
        self.gfin = self.sb((128, D), F32, "gfin")
        L = self.depth
        self.gq = self.sb((128, L, 3), F32, "gq")
        self.gkv = self.sb((128, L, 2), F32, "gkv")
        self.esink = self.sb((128, L, 8), F32, "esink")
        self.vbuf = [(self.sb((128, 16, 128), BF16, "vbuf"), p.buf("vbuf")) for _ in range(4)]
        self.vsbuf = [(self.sb((128, 4, 128), BF16, "vsbuf"), p.buf("vsbuf")) for _ in range(2)]
        mtmp = self.mark()
        tmp = self.sb((128, 2048), F32, "ctmp")
        tb = p.buf("ctmp")
        self.dma("sp", tmp[:, 0:128], self.c_ident[:, :], "c0", writes=[tb])
        self.copy("dve", self.ident[:], tmp[:, 0:128], reads=[tb], writes=[self.cb])
        self.dma("sp", tmp[:, 0:128], self.c_tri[:, :], "c0", reads=[], writes=[tb])
        self.copy("dve", self.tri[:], tmp[:, 0:128], reads=[tb], writes=[self.cb])
        self.dma("sp", tmp[:, :], self.c_swab[:, :], "c0", writes=[tb])
        self.copy("dve", self.swab[:], tmp[:, :], reads=[tb], writes=[self.cb])
        p.op("dve", lambda e: e.memset(self.ones[:], 1.0), writes=[self.cb])
        self.dma("sp", self.gfin[:], self.g_final[:, :], "c1", writes=[self.cb])
        for l in range(L):
            self.dma("sp", self.gq[:, l, :], self.g_q[l], "c1", writes=[self.cb])
            self.dma("sp", self.gkv[:, l, :], self.g_kv[l], "c1", writes=[self.cb])
            self.dma("sp", self.esink[:, l, :], self.sinks[l], "c1", writes=[self.cb])
        p.barrier()
        for l in range(L):
            self.act(self.esink[:, l, :], self.esink[:, l, :], AF.Exp, reads=[self.cb], writes=[self.cb])
        for (t, b) in self.vbuf + self.vsbuf:
            p.op("pool", lambda e, t=t: e.memset(t[:], 1.0), writes=[b])
        p.barrier()
        self.release(mtmp)

    def load_weight(self, dst, src, KC, N, dst_buf, sem):
        for kc in range(KC):
            for c0 in range(0, N, 2048):
                n = min(2048, N - c0)
                self.dma("pool", dst[:, kc, c0:c0 + n], src[kc * 128:(kc + 1) * 128, c0:c0 + n], sem,
                         writes=[dst_buf])

    def stt(self, eng, out, in0, scalar, in1, op0, op1, reads=(), writes=()):
        return self.p.op(eng, lambda e: e.scalar_tensor_tensor(out=out, in0=in0, scalar=scalar, in1=in1, op0=op0,
                                                               op1=op1), reads, writes)

    def tok_prep(self, src_rows, xt, xb, sem, hq, hb, junk, jb, st, stb, gt, gb):
        if src_rows is not None:
            self.dma("sp", xt[:], src_rows, sem, writes=[xb])
        self.act(junk[:], xt[:], AF.Square, reads=[xb], writes=[jb, stb], accum_out=st[:, 0:1])
        self.act(st[:, 1:2], st[:, 0:1], AF.Sqrt, reads=[stb, self.cb], writes=[stb], scale=1.0 / D, bias=self.epsb[:, 0:1])
        self.recip(st[:, 2:3], st[:, 1:2], reads=[stb], writes=[stb])
        self.stt("dve", hq[:], xt[:], st[:, 2:3], gt[:], ALU.mult, ALU.mult, reads=[xb, stb, gb], writes=[hb])

    def transpose_to(self, h, hb, dstT, dst_buf, col0, evac_eng):
        pt, ptb = self.next_pst()
        for kc in range(8):
            self.p.op("pe", lambda e, kc=kc: e.transpose(pt[:, kc * 128:(kc + 1) * 128], h[:, kc * 128:(kc + 1) * 128],
                                                         self.ident[:]),
                      reads=[hb, self.cb], writes=[ptb])
        self.copy(evac_eng, dstT[:, :, col0:col0 + 128], pt[:].rearrange("p (k t) -> p k t", k=8), reads=[ptb],
                  writes=[dst_buf])

    def phase_M(self, l):
        p = self.p
        w_mem = self.w_mem_sb
        wmb = self.wmb
        gt = self.sb((128, D), F32, "gtm")
        gb = p.buf("gtm")
        self.dma("sp", gt[:], self.g_mem[l], "gt", writes=[gb])
        xts = [(self.sb((128, D), F32, "xt"), p.buf("xt")) for _ in range(2)]
        hs = [(self.sb((128, D), BF16, "h"), p.buf("h")) for _ in range(2)]
        junk = self.sb((128, D), BF16, "junk")
        jb = p.buf("junk")
        sts = [(self.sb((128, 4), F32, "st"), p.buf("st")) for _ in range(2)]
        memT = self.sb((128, 8, 256), BF16, "memT")
        memTb = p.buf("memT")
        for mt in range(2):
            xt, xb = xts[mt]
            hq, hb = hs[mt]
            st, stb = sts[mt]
            self.tok_prep(self.mem_in[mt * 128:(mt + 1) * 128, :], xt, xb, f"xt{mt}", hq, hb, junk, jb, st, stb,
                          gt, gb)
            self.transpose_to(hq, hb, memT, memTb, mt * 128, "dve")
        for hm in range(4):
            pt, pb = self.next_ps()
            for kc in range(8):
                self.mm(pt[:, 0:256], w_mem[:, kc, hm * 128:(hm + 1) * 128], memT[:, kc, :], kc == 0, kc == 7,
                        reads=[wmb, memTb], writes=[pb])
            self.copy("act", self.KTm[:, hm, :], pt[:, 0:256], reads=[pb], writes=[self.memb])
        for mt in range(2):
            pt, pb = self.next_ps()
            for kc in range(8):
                self.mm(pt[:, :], memT[:, kc, mt * 128:(mt + 1) * 128], w_mem[:, kc, 512:1024], kc == 0, kc == 7,
                        reads=[wmb, memTb], writes=[pb])
            self.copy("dve", self.Vm[:, mt, :], pt[:, :], reads=[pb], writes=[self.memb])

    def phase_A(self, l, x_src):
        p = self.p
        epsbuf = self.cb
        w_in, w_uq, w_ukv = self.w_in_sb, self.w_uq_sb, self.w_ukv_sb
        wbuf_of = {id(w_uq): [self.wuqb], id(w_ukv): [self.wukvb]}
        gt = self.sb((128, D), F32, "gta")
        gb = p.buf("gta")
        self.dma("sp", gt[:], self.g_attn[l], "gt", writes=[gb])
        xts = [(self.sb((128, D), F32, "xt"), p.buf("xt")) for _ in range(4)]
        hs = [(self.sb((128, D), BF16, "h"), p.buf("h")) for _ in range(4)]
        junk = self.sb((128, D), BF16, "junk")
        jb = p.buf("junk")
        sts = [(self.sb((128, 4), F32, "st"), p.buf("st")) for _ in range(4)]

        hT = [(self.sb((128, 8, 512), BF16, "hT"), p.buf("hT")) for _ in range(1)]
        cqn = self.sb((128, 3, 512), BF16, "cqn")
        cqnb = p.buf("cqn")
        ckvn = self.sb((128, 2, 512), BF16, "ckvn")
        ckvnb = p.buf("ckvn")
        sq = [(self.sb((128, 512), BF16, "sq"), p.buf("sq")) for _ in range(5)]
        csb = [(self.sb((128, 512), F32, "csb"), p.buf("csb")) for _ in range(5)]
        hT_cur = [None, None]
        rs = [(self.sb((128, 512), F32, "rs"), p.buf("rs")) for _ in range(2)]
        tabs = [(self.sb((128, 2, 512), F32, "tab"), p.buf("tab")) for _ in range(2)]
        rt = [(self.sb((128, 2, 512), F32, "rt"), p.buf("rt")) for _ in range(1)]
        stg = [(self.sb((128, 512), BF16, "stg"), p.buf("stg")) for _ in range(12)]
        self.stg_rr = 0

        def next_stg():
            i = self.stg_rr % len(stg)
            self.stg_rr += 1
            return stg[i][0], stg[i][1], f"stg{i}"

        def queue_of(sem):
            return "pool" if (sem.startswith("stg") and int(sem[3:]) % 3 == 2) else "sp"

        def prep_load(c):
            for tt in range(4):
                T = c * 4 + tt
                xt, xb = xts[tt]
                self.dma("sp", xt[:], x_src[T * 128:(T + 1) * 128, :], f"xt{tt}", writes=[xb])

        def prep(c):
            for tt in range(4):
                i = (c * 4 + tt)
                xt, xb = xts[tt]
                hq, hb = hs[i % 4]
                st, stb = sts[i % 4]
                self.tok_prep(None, xt, xb, None, hq, hb, junk, jb, st, stb, gt, gb)
            tb_t, tb_b = tabs[c % 2]
            self.dma("sp", tb_t[:, 0, :], self.c_ropeC[:, c * 512:(c + 1) * 512], f"tab{c % 2}", writes=[tb_b])
            self.dma("sp", tb_t[:, 1, :], self.c_ropeS[:, c * 512:(c + 1) * 512], f"tab{c % 2}", writes=[tb_b])

        def proj(pt, pb, w, wcols, KC, rhsT, rb, M):
            if id(w) in wbuf_of:
                wbs = wbuf_of[id(w)]
            else:
                wbs = [self.wA0b] if wcols[1] <= WSPLIT else ([self.wA1b] if wcols[0] >= WSPLIT else
                                                             [self.wA0b, self.wA1b])
            for kc in range(KC):
                self.mm(pt[0:M, :], w[:, kc, wcols[0]:wcols[1]], rhsT[:, kc, :], kc == 0, kc == KC - 1,
                        reads=wbs + [rb], writes=[pb])

        def store(src_ap, dst_ap, sbuf_, sem):
            self.dma(queue_of(sem), dst_ap, src_ap, sem, reads=[sbuf_])

        ev = [0]

        def evac_copy(pt, pb, M=128):
            s_t, s_b, sem = next_stg()
            eng = "act" if ev[0] % 2 == 0 else "dve"
            ev[0] += 1
            self.copy(eng, s_t[0:M, :], pt[0:M, :], reads=[pb], writes=[s_b])
            return s_t, s_b, sem

        def latent_start(cols, k0):
            items = []
            for j, c0 in enumerate(cols):
                pt, pb = self.next_ps()
                proj(pt, pb, w_in, (c0, c0 + 128), 8, hT_cur[0], hT_cur[1], 128)
                s_t, s_b = sq[k0 + j]
                c_t, c_b = csb[k0 + j]
                self.act(s_t[:], pt[:], AF.Square, reads=[pb], writes=[s_b])
                self.copy("dve", c_t[:], pt[:], reads=[pb], writes=[c_b])
                items.append((s_t, s_b, c_t, c_b))
            return items

        def latent_finish(items, nfeat, gain, dst, dstb, k):
            pss, pssb = self.next_ps()
            n = len(items)
            for j, (s_t, s_b, c_t, c_b) in enumerate(items):
                self.mm(pss[:, :], self.ones[:], s_t[:], j == 0, j == n - 1, reads=[s_b, self.cb], writes=[pssb])
            r_t, r_b = rs[k]
            self.act(r_t[:], pss[:], AF.Sqrt, reads=[pssb, epsbuf], writes=[r_b], scale=1.0 / nfeat,
                     bias=self.epsb[:, 0:1])
            self.recip(r_t[:], r_t[:], reads=[r_b], writes=[r_b])
            for j, (s_t, s_b, c_t, c_b) in enumerate(items):
                self.stt("dve", dst[:, j, :], c_t[:], gain[:, j:j + 1], r_t[:], ALU.mult, ALU.mult,
                         reads=[c_b, r_b, self.cb], writes=[dstb])

        def rope(ptA, pbA, ptB, pbB, M, tab, tabb, c):
            r_t, r_b = rt[0]
            self.tt("dve", r_t[0:M, 0, :], ptA[0:M, :], tab[0:M, 0, :], ALU.mult, reads=[pbA, tabb], writes=[r_b])
            self.tt("dve", r_t[0:M, 1, :], ptB[0:M, :], tab[0:M, 1, :], ALU.mult, reads=[pbB, tabb], writes=[r_b])
            s_t, s_b, sem = next_stg()
            self.tt("pool", s_t[0:M, :], r_t[0:M, 0, :], r_t[0:M, 1, :], ALU.add, reads=[r_b], writes=[s_b])
            return s_t, s_b, sem

        prep_load(0)
        prep(0)
        for c in range(NCH):
            tk = slice(c * 512, (c + 1) * 512)
            hT_t, hT_b = hT[0]
            tab_t, tab_b = tabs[c % 2]
            for tt in range(4):
                i = c * 4 + tt
                hq, hb = hs[i % 4]
                self.transpose_to(hq, hb, hT_t, hT_b, tt * 128, "dve" if tt % 2 == 0 else "act")
            if c + 1 < NCH:
                prep_load(c + 1)
            hT_cur[0], hT_cur[1] = hT_t, hT_b
            it_q = latent_start([0, 128, 256], 0)
            it_kv = latent_start([384, 512], 3)
            ptA, pbA = self.next_ps()
            proj(ptA, pbA, w_in, (640, 672), 8, hT_t, hT_b, 32)
            ptB, pbB = self.next_ps()
            proj(ptB, pbB, w_in, (4000, 4032), 8, hT_t, hT_b, 32)
            s_t, s_b, sem = rope(ptA, pbA, ptB, pbB, 32, tab_t, tab_b, c)
            store(s_t[0:32, :], self.KPE[:, tk], s_b, sem)
            zcols = [672 + m * 128 for m in range(8)] + [2464 + m * 128 for m in range(4)] + \
                    [3488 + m * 128 for m in range(4)]
            for m, c0 in enumerate(zcols):
                pt, pb = self.next_ps()
                proj(pt, pb, w_in, (c0, c0 + 128), 8, hT_t, hT_b, 128)
                s_t, s_b, sem = next_stg()
                self.act(s_t[:], pt[:], AF.Silu, reads=[pb], writes=[s_b])
                store(s_t[:], self.G[m * 128:(m + 1) * 128, tk], s_b, sem)
                if m == 1:
                    latent_finish(it_q, 384, self.gq[:, l, :], cqn, cqnb, 0)
                    latent_finish(it_kv, 256, self.gkv[:, l, :], ckvn, ckvnb, 1)
            if c + 1 < NCH:
                prep(c + 1)
            for m in range(4):
                pt, pb = self.next_ps()
                c0 = 1696 + m * 128
                proj(pt, pb, w_in, (c0, c0 + 128), 8, hT_t, hT_b, 128)
                s_t, s_b, sem = evac_copy(pt, pb)
                store(s_t[:], self.QS[m * 128:(m + 1) * 128, tk], s_b, sem)
            pt, pb = self.next_ps()
            proj(pt, pb, w_in, (2208, 2336), 8, hT_t, hT_b, 128)
            s_t, s_b, sem = evac_copy(pt, pb)
            store(s_t[:], self.KS[:, tk], s_b, sem)
            for m in range(4):
                pt, pb = self.next_ps()
                c0 = 2976 + m * 128
                proj(pt, pb, w_in, (c0, c0 + 128), 8, hT_t, hT_b, 128)
                s_t, s_b, sem = evac_copy(pt, pb)
                store(s_t[:], self.QM[m, :, tk], s_b, sem)
            for tt in range(4):
                T = c * 4 + tt
                pt, pb = self.next_ps()
                for kc in range(8):
                    self.mm(pt[:, 0:128], hT_t[:, kc, tt * 128:(tt + 1) * 128], w_in[:, kc, 2336:2464], kc == 0,
                            kc == 7, reads=[self.wA1b, hT_b], writes=[pb])
                vs_t, vs_b = self.vsbuf[T % 2]
                pv = pt[:, 0:128].rearrange("p (k d) -> p k d", k=2)
                v4 = vs_t[:].rearrange("p (k v) c -> p k v c", k=2)
                ee = "dve" if tt % 2 == 0 else "act"
                self.copy(ee, v4[:, :, 0, 0:64], pv, reads=[pb], writes=[vs_b])
                self.copy(ee, v4[:, :, 1, 64:128], pv, reads=[pb], writes=[vs_b])
                store(vs_t[:], self.VS[T], vs_b, f"vsb{T % 2}")
            for tt in range(4):
                T = c * 4 + tt
                vb_t, vb_b = self.vbuf[T % 4]
                for half in range(2):
                    pt, pb = self.next_ps()
                    for kc in range(2):
                        self.mm(pt[:, :], ckvn[:, kc, tt * 128:(tt + 1) * 128],
                                w_ukv[:, kc, 1024 + half * 512:1024 + (half + 1) * 512], kc == 0, kc == 1,
                                reads=[self.wukvb, ckvnb], writes=[pb])
                    pv = pt[:, :].rearrange("p (h two d) -> p h two d", two=2, d=64)
                    vv = vb_t[:, half * 8:(half + 1) * 8, :].rearrange("p (h two) c -> p h two c", two=2)
                    ee = "dve" if half == 0 else "act"
                    self.copy(ee, vv[:, :, 0, 0:64], pv[:, :, 0, :], reads=[pb], writes=[vb_b])
                    self.copy(ee, vv[:, :, 1, 64:128], pv[:, :, 1, :], reads=[pb], writes=[vb_b])
                store(vb_t[:], self.VM[T], vb_b, f"vb{T % 4}")
            for m in range(8):
                pt, pb = self.next_ps()
                proj(pt, pb, w_uq, (m * 128, (m + 1) * 128), 3, cqn, cqnb, 128)
                s_t, s_b, sem = evac_copy(pt, pb)
                store(s_t[:], self.QN[m * 128:(m + 1) * 128, tk], s_b, sem)
            for m in range(4):
                ptA, pbA = self.next_ps()
                proj(ptA, pbA, w_uq, (1024 + m * 128, 1024 + (m + 1) * 128), 3, cqn, cqnb, 128)
                ptB, pbB = self.next_ps()
                proj(ptB, pbB, w_uq, (1536 + m * 128, 1536 + (m + 1) * 128), 3, cqn, cqnb, 128)
                s_t, s_b, sem = rope(ptA, pbA, ptB, pbB, 128, tab_t, tab_b, c)
                store(s_t[:], self.QR[m * 128:(m + 1) * 128, tk], s_b, sem)
            for m in range(8):
                pt, pb = self.next_ps()
                proj(pt, pb, w_ukv, (m * 128, (m + 1) * 128), 2, ckvn, ckvnb, 128)
                s_t, s_b, sem = evac_copy(pt, pb)
                store(s_t[:], self.KN[m * 128:(m + 1) * 128, tk], s_b, sem)
    def phase_B(self, l):
        p = self.p
        QTb = [(self.sb((128, S), BF16, "QTb"), p.buf("QTb")) for _ in range(2)]
        KTb = [(self.sb((128, S), BF16, "KTb"), p.buf("KTb")) for _ in range(2)]
        Vb = [(self.sb((128, NT, 128), BF16, "Vb"), p.buf("Vb")) for _ in range(2)]
        Vq = [[p.buf("Vq") for _ in range(4)] for _ in range(2)]
        Gb = [(self.sb((128, S), BF16, "Gb"), p.buf("Gb")) for _ in range(2)]
        PT = [(self.sb((128, 512), BF16, "PT"), p.buf("PT")) for _ in range(6)]
        Rb = [(self.sb((128, 512), F32, "R"), p.buf("R")) for _ in range(2)]
        Y1 = [(self.sb((128, 512), F32, "Y1"), p.buf("Y1")) for _ in range(2)]
        yb = [(self.sb((128, 512), BF16, "yb"), p.buf("yb")) for _ in range(3)]
        ST = self.ps[0:3]
        OT = self.ps[3:5]
        SM = self.ps[5]
        STM = self.ps[0:4]
        OTM = self.ps[4:6]
        cnt = dict(ep=0, yb=0, u=0)
        LA = 2

        def run_pipeline(units, LA=2):
            n = len(units)
            base = cnt["u"]
            cnt["u"] += n
            for i in range(min(LA, n)):
                units[i]["qk"](base + i)
            pending = None
            for i in range(n):
                if i + LA < n:
                    units[i + LA]["qk"](base + i + LA)
                units[i]["ex"](base + i)
                if pending is not None:
                    pending()
                    pending = None
                units[i]["pv"](base + i)
                if "post" in units[i]:
                    pending = units[i]["post"]
            if pending is not None:
                pending()

        for i in range(2):
            self.dma("sp", KTb[i][0][0:32, :], self.KPE[:, :], f"KTb{i}", writes=[KTb[i][1]])

        def epilogue(o_t, o_b, s_t, s_b, o_sl, s_sl, g_t, g_b, c, dst_ap, sink_ap=None, act_recip=False):
            i = cnt["ep"]
            cnt["ep"] += 1
            r_t, r_b = Rb[i % 2]
            y1_t, y1_b = Y1[i % 2]
            j = cnt["yb"]
            cnt["yb"] += 1
            y_t, y_b = yb[j % 3]
            if act_recip:
                kw = dict(bias=sink_ap) if sink_ap is not None else {}
                self.act(r_t[o_sl, :], s_t[s_sl, :], AF.Ln, reads=[s_b, self.cb], writes=[r_b], **kw)
                self.act(r_t[o_sl, :], r_t[o_sl, :], AF.Exp, reads=[r_b], writes=[r_b], scale=-1.0)
            else:
                self.recip(r_t[o_sl, :], s_t[s_sl, :], reads=[s_b], writes=[r_b])
            self.tt("dve", y1_t[o_sl, :], o_t[o_sl, :], r_t[o_sl, :], ALU.mult, reads=[o_b, r_b], writes=[y1_b])
            self.tt("pool", y_t[o_sl, :], y1_t[o_sl, :], g_t[o_sl, c * 512:(c + 1) * 512], ALU.mult,
                    reads=[y1_b, g_b], writes=[y_b])
            self.dma("sp", dst_ap, y_t[o_sl, :], f"yb{j % 3}", reads=[y_b])

        sc_mla = float(96 ** -0.5)

        def load_mla(h):
            i = h % 2
            self.dma("sp", QTb[i][0][0:32, :], self.QR[h * 32:(h + 1) * 32, :], f"QTb{i}", writes=[QTb[i][1]])
            self.dma("sp", QTb[i][0][32:96, :], self.QN[h * 64:(h + 1) * 64, :], f"QTb{i}", writes=[QTb[i][1]])
            self.dma("sp", KTb[i][0][32:96, :], self.KN[h * 64:(h + 1) * 64, :], f"KTb{i}", writes=[KTb[i][1]])
            par = h % 2
            vm = self.VM.rearrange("t p h c -> p t h c")
            for qd in range(4):
                self.dma("sp", Vb[i][0][:, qd * 8:(qd + 1) * 8, :], vm[:, qd * 8:(qd + 1) * 8, h, :], f"Vq{i}_{qd}",
                         writes=[Vq[i][qd]], reads=[Vb[i][1]])
                if qd == 0:
                    self.dma("sp", Gb[i][0][par * 64:(par + 1) * 64, :], self.G[h * 64:(h + 1) * 64, :], f"Gb{i}",
                             writes=[Gb[i][1]])

        def mla_unit(h, c, t):
            i2 = h % 2
            q_t, q_b = QTb[i2]
            k_t, k_b = KTb[i2]
            v_t, v_b = Vb[i2]
            g_t, g_b = Gb[i2]
            par = h % 2
            o_sl = slice(par * 64, (par + 1) * 64)
            s_sl = slice((1 - par) * 64, (2 - par) * 64)
            j = t - 4 * c
            lo = max(j, 0) * 128
            o_t, o_b = OTM[c % 2]

            def qk(i):
                st_t, st_b = STM[i % 4]
                self.mm(st_t[:, lo:512], k_t[0:96, t * 128:(t + 1) * 128], q_t[0:96, c * 512 + lo:(c + 1) * 512], True,
                        j < 0, reads=[k_b, q_b], writes=[st_b])
                if j >= 0:
                    self.mm(st_t[:, lo:lo + 128], self.ident[:], self.tri[:], False, True, reads=[self.cb],
                            writes=[st_b])

            def ex(i):
                st_t, st_b = STM[i % 4]
                pt_t, pt_b = PT[i % 6]
                self.act(pt_t[:, lo:512], st_t[:, lo:512], AF.Exp, reads=[st_b], writes=[pt_b], scale=sc_mla)

            def pv(i):
                pt_t, pt_b = PT[i % 6]
                self.mm(o_t[:, lo:512], v_t[:, t, :], pt_t[:, lo:512], t == 0, t == 4 * c + 3,
                        reads=[Vq[i2][t // 8], pt_b], writes=[o_b])

            u = dict(qk=qk, ex=ex, pv=pv)
            if t == 4 * c + 3:
                def post():
                    epilogue(o_t, o_b, o_t, o_b, o_sl, s_sl, g_t, g_b, c,
                             self.YT[h // 2, par * 64:(par + 1) * 64, c * 512:(c + 1) * 512])
                    if c == NCH - 1 and h + 2 < 16:
                        load_mla(h + 2)
                    if c == NCH - 1 and h == 0:
                        self.load_weight(self.w_out_sb, self.w_out[l], 16, D, self.wob, "w_out")
                    if c == NCH - 1 and h == 14:
                        swa_preload(0)
                u["post"] = post
            return u

        def swa_preload(i):
            p.op("pool", lambda e, t=QTb[i][0]: e.memset(t[64:128, :], 0.0), writes=[QTb[i][1]])
            p.op("pool", lambda e, t=KTb[i][0]: e.memset(t[64:128, :], 0.0), writes=[KTb[i][1]])
            self.dma("sp", KTb[i][0][0:64, :], self.KS[i * 64:(i + 1) * 64, :], f"KTb{i}", writes=[KTb[i][1]])
            load_swa_v(0, i)
            load_swa_q(i)

        def load_swa_v(kvh, var):
            self.dma("sp", Vb[var][0][:], self.VS.rearrange("t p k c -> p t k c")[:, :, kvh * 2 + var, :], f"Vb{var}",
                     writes=[Vb[var][1]] + Vq[var])

        def load_swa_q(h):
            par = h % 2
            self.dma("sp", QTb[h % 2][0][0:64, :], self.QS[h * 64:(h + 1) * 64, :], f"QTb{h % 2}", writes=[QTb[h % 2][1]])
            self.dma("sp", Gb[h % 2][0][par * 64:(par + 1) * 64, :], self.G[1024 + h * 64:1024 + (h + 1) * 64, :],
                     f"Gb{h % 2}", writes=[Gb[h % 2][1]])

        def load_mem(hm):
            self.dma("sp", QTb[hm % 2][0][:, :], self.QM[hm, :, :], f"QTb{hm % 2}", writes=[QTb[hm % 2][1]])
            self.dma("sp", Gb[hm % 2][0][:, :], self.G[1536 + hm * 128:1536 + (hm + 1) * 128, :], f"Gb{hm % 2}",
                     writes=[Gb[hm % 2][1]])

        load_mla(0)
        load_mla(1)
        units = [mla_unit(h, c, t) for h in range(16) for c in range(NCH) for t in range(4 * c + 4)]
        run_pipeline(units, LA=3)
        swa_preload(1)

        sc_swa = 0.125

        def swa_unit(h, c, half):
            kvh = h // 4
            par = h % 2
            ks_t, ks_b = KTb[kvh]
            q_t, q_b = QTb[h % 2]
            g_t, g_b = Gb[h % 2]
            v_t, v_b = Vb[par]
            o_sl = slice(par * 64, (par + 1) * 64)
            s_sl = slice((1 - par) * 64, (2 - par) * 64)
            o_t, o_b = OT[c % 2]
            slots = []
            for nn in range(2):
                n = c * 4 + half * 2 + nn
                for typ in range(2):
                    t = n - 1 + typ
                    if t >= 0:
                        slots.append((nn, n, typ, t, (nn * 2 + typ) * 128))
            lo = min(s[4] for s in slots)

            def qk(i):
                st_t, st_b = ST[i % 3]
                for (nn, n, typ, t, col) in slots:
                    self.mm(st_t[:, col:col + 128], ks_t[:, t * 128:(t + 1) * 128], q_t[:, n * 128:(n + 1) * 128], True,
                            False, reads=[ks_b, q_b], writes=[st_b])
                    self.mm(st_t[:, col:col + 128], self.ident[:],
                            self.swab[:, (h * 2 + typ) * 128:(h * 2 + typ + 1) * 128], False, True,
                            reads=[self.cb], writes=[st_b])

            def ex(i):
                st_t, st_b = ST[i % 3]
                pt_t, pt_b = PT[i % 4]
                self.act(pt_t[:, lo:512], st_t[:, lo:512], AF.Exp, reads=[st_b], writes=[pt_b], scale=sc_swa)

            def pv(i):
                pt_t, pt_b = PT[i % 4]
                first_n = {}
                for (nn, n, typ, t, col) in slots:
                    oc = (half * 2 + nn) * 128
                    self.mm(o_t[:, oc:oc + 128], v_t[:, t, :], pt_t[:, col:col + 128], n not in first_n, typ == 1,
                            reads=[v_b, pt_b], writes=[o_b])
                    first_n[n] = True

            u = dict(qk=qk, ex=ex, pv=pv)
            if half == 1:
                def post():
                    epilogue(o_t, o_b, o_t, o_b, o_sl, s_sl, g_t, g_b, c,
                             self.YT[8 + h // 2, par * 64:(par + 1) * 64, c * 512:(c + 1) * 512],
                             sink_ap=self.esink[s_sl, l, h:h + 1], act_recip=True)
                    if c == NCH - 1:
                        if h + 2 < 8:
                            load_swa_q(h + 2)
                        if h == 2:
                            load_swa_v(1, 0)
                        if h == 3:
                            load_swa_v(1, 1)
                        if h >= 6:
                            load_mem(h - 6)
                u["post"] = post
            return u

        units = [swa_unit(h, c, half) for h in range(8) for c in range(NCH) for half in range(2)]
        run_pipeline(units)

        sc_mem = float(128 ** -0.5)
        def mem_unit(hm, c, mt):
            q_t, q_b = QTb[hm % 2]
            q2 = q_t
            g_t, g_b = Gb[hm % 2]
            o_t, o_b = OT[c % 2]
            s_t, s_b = SM

            def qk(i):
                st_t, st_b = ST[i % 3]
                self.mm(st_t[:, :], self.KTm[:, hm, mt * 128:(mt + 1) * 128], q2[:, c * 512:(c + 1) * 512], True,
                        True, reads=[self.memb, q_b], writes=[st_b])

            def ex(i):
                st_t, st_b = ST[i % 3]
                pt_t, pt_b = PT[i % 4]
                self.act(pt_t[:, :], st_t[:, :], AF.Exp, reads=[st_b], writes=[pt_b], scale=sc_mem)

            def pv(i):
                pt_t, pt_b = PT[i % 4]
                self.mm(o_t[:, :], self.Vm[:, mt, hm * 128:(hm + 1) * 128], pt_t[:, :], mt == 0, mt == 1,
                        reads=[self.memb, pt_b], writes=[o_b])
                self.mm(s_t[:, :], self.ones[:], pt_t[:, :], mt == 0, mt == 1, reads=[self.cb, pt_b],
                        writes=[s_b])

            u = dict(qk=qk, ex=ex, pv=pv)
            if mt == 1:
                def post():
                    sl = slice(0, 128)
                    epilogue(o_t, o_b, s_t, s_b, sl, sl, g_t, g_b, c, self.YT[12 + hm, :, c * 512:(c + 1) * 512],
                             act_recip=True)
                    if c == NCH - 1 and hm + 2 < 4:
                        load_mem(hm + 2)
                u["post"] = post
            return u

        units = [mem_unit(hm, c, mt) for hm in range(4) for c in range(NCH) for mt in range(2)]
        run_pipeline(units)

    def phase_C(self, l, x_src, last, prefetch):
        p = self.p
        w_out, wb = self.w_out_sb, self.wob
        CW = 256
        yT = [(self.sb((128, 16, CW), BF16, "yT"), p.buf("yT")) for _ in range(2)]
        xc = [(self.sb((128, D), F32, "xc"), p.buf("xc")) for _ in range(2)]
        xn = [(self.sb((128, D), F32, "xn"), p.buf("xn")) for _ in range(2)]
        junk = self.sb((128, D), BF16, "junkc")
        jb = p.buf("junkc")
        sts = [(self.sb((128, 4), F32, "stc"), p.buf("stc")) for _ in range(4)]
        p.op("dve", lambda e: e.memset(junk[:], 0.0), writes=[jb])
        nchunk = S // CW

        def load_chunk(c):
            t, b = yT[c % 2]
            for half in range(2):
                self.dma("sp", t[:, half * 8:(half + 1) * 8, :],
                         self.YT.rearrange("k p s -> p k s")[:, half * 8:(half + 1) * 8, c * CW:(c + 1) * CW],
                         f"yT{c % 2}", writes=[b])

        def load_x(T):
            x_t, x_b = xc[T % 2]
            self.dma("sp", x_t[:], x_src[T * 128:(T + 1) * 128, :], f"xc{T % 2}", writes=[x_b])

        load_chunk(0)
        load_x(0)
        for c in range(nchunk):
            if c + 1 < nchunk:
                load_chunk(c + 1)
            y_t, y_b = yT[c % 2]
            for tt in range(CW // 128):
                T = c * (CW // 128) + tt
                x_t, x_b = xc[T % 2]
                n_t, n_b = xn[T % 2]
                if T + 1 < NT:
                    load_x(T + 1)
                for half in range(2):
                    pt, pb = self.next_ps()
                    for kc in range(16):
                        self.mm(pt[:, :], y_t[:, kc, tt * 128:(tt + 1) * 128], w_out[:, kc, half * 512:(half + 1) * 512],
                                kc == 0, kc == 15, reads=[y_b, wb], writes=[pb])
                    mark = self.tt("dve", n_t[:, half * 512:(half + 1) * 512], pt[:, :],
                                   x_t[:, half * 512:(half + 1) * 512], ALU.add, reads=[pb, x_b], writes=[n_b])
                if prefetch:
                    prefetch.pop(0)(after=(mark,))
                if not last:
                    self.dma("sp", self.XRES[T * 128:(T + 1) * 128, :], n_t[:], f"xn{T % 2}", reads=[n_b])
                else:
                    st, stb = sts[T % 4]
                    self.act(junk[:], n_t[:], AF.Square, reads=[n_b], writes=[jb, stb], accum_out=st[:, 0:1])
                    self.act(st[:, 1:2], st[:, 0:1], AF.Sqrt, reads=[stb, self.cb], writes=[stb], scale=1.0 / D,
                             bias=self.epsb[:, 0:1])
                    self.recip(st[:, 2:3], st[:, 1:2], reads=[stb], writes=[stb])
                    self.stt("dve", n_t[:], n_t[:], st[:, 2:3], self.gfin[:], ALU.mult, ALU.mult,
                             reads=[n_b, stb, self.cb], writes=[n_b])
                    self.dma("sp", self.y_out[T * 128:(T + 1) * 128, :], n_t[:], f"xn{T % 2}", reads=[n_b])
        while prefetch:
            prefetch.pop(0)()


def _consts():
    ident = np.eye(128, dtype=np.float32)
    k = np.arange(128)[:, None]
    q = np.arange(128)[None, :]
    tri = np.where(q >= k, 0.0, NEGB).astype(np.float32)
    swab = np.zeros((128, 16, 128), np.float32)
    for h in range(8):
        slope = 2.0 ** (-(h + 1))
        d_prev = q + 128 - k
        swab[:, h * 2 + 0, :] = np.where(d_prev < 128, -8.0 * slope * d_prev, NEGB)
        d_cur = q - k
        swab[:, h * 2 + 1, :] = np.where(d_cur >= 0, -8.0 * slope * d_cur, NEGB)
    inv = (10000.0 ** (-np.arange(0, 32, 2, dtype=np.float32) / np.float32(32))).astype(np.float32)
    ang = (np.arange(S, dtype=np.float32)[:, None] * inv[None, :]).astype(np.float32)
    cos = np.cos(ang).astype(np.float32).T
    sin = np.sin(ang).astype(np.float32).T
    C32 = np.concatenate([cos, cos], 0)
    S32 = np.concatenate([-sin, sin], 0)
    ropeC = np.ascontiguousarray(np.tile(C32, (4, 1)))
    ropeS = np.ascontiguousarray(np.tile(S32, (4, 1)))
    return dict(c_ident=ident, c_tri=tri, c_swab=np.ascontiguousarray(swab.reshape(128, 16 * 128)),
                c_ropeC=ropeC, c_ropeS=ropeS)


def _layout_weights(attn_norm, w_in, mla_q_norm, w_uq, mla_kv_norm, w_ukv, swa_sinks, mem_norm, w_mem_kv, w_out,
                    final_norm):
    L = w_in.shape[0]
    f = lambda a: np.ascontiguousarray(a, dtype=np.float32)
    swap_cols = np.concatenate([np.arange(656, 672), np.arange(640, 656)])
    w_in_ext = np.concatenate([w_in, w_in[:, :, swap_cols]], axis=2)
    hq = np.arange(16)[:, None] * 96
    nope = (hq + np.arange(64)[None, :]).reshape(-1)
    rope = (hq + 64 + np.arange(32)[None, :]).reshape(-1)
    sw = np.concatenate([np.arange(16, 32), np.arange(0, 16)])
    ropes = (hq + 64 + sw[None, :]).reshape(-1)
    w_uq_r = w_uq[:, :, np.concatenate([nope, rope, ropes])]
    hk = np.arange(16)[:, None] * 128
    kn = (hk + np.arange(64)[None, :]).reshape(-1)
    vv = (hk + 64 + np.arange(64)[None, :]).reshape(-1)
    w_ukv_r = w_ukv[:, :, np.concatenate([kn, vv])]
    pk = lambda g, kc: np.ascontiguousarray(g.reshape(L, kc, 128).transpose(0, 2, 1))
    return dict(
        w_in=f(w_in_ext), w_uq=f(w_uq_r), w_ukv=f(w_ukv_r), w_mem=f(w_mem_kv), w_out=f(w_out),
        g_attn=f(np.broadcast_to(attn_norm[:, None, :], (L, 128, D))), g_q=f(pk(mla_q_norm, 3)),
        g_kv=f(pk(mla_kv_norm, 2)), g_mem=f(np.broadcast_to(mem_norm[:, None, :], (L, 128, D))),
        sinks=f(np.broadcast_to(swa_sinks[:, None, :], (L, 128, 8))),
        g_final=f(np.broadcast_to(final_norm[None, :], (128, D))),
    )


_CACHE = {}


def _get_nc():
    if "nc" not in _CACHE:
        b = Builder()
        _CACHE["nc"] = b.build()
        _CACHE["stats"] = b.stats
    return _CACHE["nc"]


def kernel(x, mem, attn_norm, w_in, mla_q_norm, w_uq, mla_kv_norm, w_ukv, swa_sinks, mem_norm, w_mem_kv, w_out,
           final_norm):
    x = np.asarray(x, dtype=np.float32)
    mem = np.asarray(mem, dtype=np.float32)
    B = x.shape[0]
    shared = _layout_weights(*[np.asarray(a, dtype=np.float32) for a in
                               (attn_norm, w_in, mla_q_norm, w_uq, mla_kv_norm, w_ukv, swa_sinks, mem_norm,
                                w_mem_kv, w_out, final_norm)])
    shared.update(_consts())
    nc = _get_nc()
    in_maps = []
    for b in range(B):
        m = dict(shared)
        m["x"] = np.ascontiguousarray(x[b])
        m["mem"] = np.ascontiguousarray(mem[b])
        in_maps.append(m)
    res = run_bass_kernel_spmd(nc, in_maps, core_ids=list(range(B)))
    return np.stack([np.asarray(r["y"], dtype=np.float32) for r in res.results], axis=0)
```

```python
import numpy as np
import concourse.bass as bass
import concourse.mybir as mybir
from concourse.bass_utils import run_bass_kernel_spmd

F32 = mybir.dt.float32
BF16 = mybir.dt.bfloat16
AF = mybir.ActivationFunctionType
ALU = mybir.AluOpType

S = 4096
D = 1024
NT = S // 128
NCH = S // 512
DEPTH = 2
WIN_COLS = 4032
WSPLIT = 1696
EPS = 1e-6
NEGB = -30000.0


class Buf:
    __slots__ = ("name", "w", "rc", "rd", "excl")

    def __init__(self, name, excl=False):
        self.name = name
        self.excl = excl
        self.w = None
        self.rc = {}
        self.rd = []


class Op:
    __slots__ = ("eng", "fn", "deps", "signaled", "token", "dma_sem")

    def __init__(self, eng, fn, dma_sem=None):
        self.eng = eng
        self.fn = fn
        self.deps = []
        self.signaled = False
        self.token = None
        self.dma_sem = dma_sem


class P:
    ENGS = ("pe", "act", "dve", "pool", "sp")

    def __init__(self, nc):
        self.nc = nc
        self.e = {"pe": nc.tensor, "act": nc.scalar, "dve": nc.vector, "pool": nc.gpsimd, "sp": nc.sync}
        self.ops = []
        self.last_real = {}
        self.last_dma = {}
        self.nbuf = 0

    def buf(self, name="b", excl=False):
        self.nbuf += 1
        return Buf(f"{name}{self.nbuf}", excl)

    def op(self, eng, fn, reads=(), writes=(), dma_sem=None, extra_deps=()):
        o = Op(eng, fn, dma_sem)
        deps = {id(d): d for d in extra_deps}
        is_dma = dma_sem is not None

        def add(d, raw):
            if d is None or d is o:
                return
            d_dma = d.dma_sem is not None
            if not is_dma and not d_dma and d.eng == eng:
                if eng == "pe" or not raw:
                    return
            deps[id(d)] = d

        for b in reads:
            add(b.w, True)
            if b.excl:
                for re, r in b.rc.items():
                    if re != eng:
                        add(r, False)
        for b in writes:
            if not (is_dma and b.w is not None and b.w.dma_sem == dma_sem and b.w.eng == eng):
                add(b.w, False)
            for r in b.rc.values():
                add(r, False)
            for r in b.rd:
                add(r, False)
        o.deps = list(deps.values())
        for d in o.deps:
            d.signaled = True
        for b in reads:
            if is_dma:
                b.rd.append(o)
            else:
                b.rc[eng] = o
        for b in writes:
            b.w = o
            b.rc = {}
            b.rd = []
        self.ops.append(o)
        if is_dma:
            self.last_dma[dma_sem] = o
        else:
            self.last_real[eng] = o
        return o

    def barrier(self, skip_sems=()):
        deps = [o for o in self.last_real.values()] + [o for k, o in self.last_dma.items() if k not in skip_sems]
        for d in deps:
            d.signaled = True
        for eng in self.ENGS:
            o = Op(eng, None)
            o.deps = [d for d in deps if not (d.dma_sem is None and d.eng == eng and eng == "pe")]
            self.ops.append(o)

    def emit(self):
        nc = self.nc
        esem = {e: nc.alloc_semaphore(f"es_{e}") for e in self.ENGS}
        dsem = {}
        ecnt = {e: 0 for e in self.ENGS}
        dcnt = {}
        waited = {e: {} for e in self.ENGS}
        nwait = 0
        for o in self.ops:
            eng = self.e[o.eng]
            need = {}
            for d in o.deps:
                sem, val = d.token
                k = id(sem)
                if k not in need or need[k][1] < val:
                    need[k] = (sem, val)
            for k, (sem, val) in need.items():
                if waited[o.eng].get(k, 0) < val:
                    eng.wait_ge(sem, val)
                    waited[o.eng][k] = val
                    nwait += 1
            if o.fn is None:
                continue
            ins = o.fn(eng)
            if o.dma_sem is not None:
                if o.dma_sem not in dsem:
                    dsem[o.dma_sem] = nc.alloc_semaphore(f"ds_{o.dma_sem}")
                    dcnt[o.dma_sem] = 0
                dcnt[o.dma_sem] += 16
                ins.then_inc(dsem[o.dma_sem], 16)
                o.token = (dsem[o.dma_sem], dcnt[o.dma_sem])
            elif o.signaled:
                ecnt[o.eng] += 1
                ins.then_inc(esem[o.eng], 1)
                o.token = (esem[o.eng], ecnt[o.eng])
        return dict(n_ops=len(self.ops), n_wait=nwait, ecnt=ecnt, n_dsem=len(dsem))


class Builder:
    def __init__(self, depth=DEPTH, dbg=False, stop_after=None):
        self.depth = depth
        self.dbg = dbg
        self.stop_after = stop_after
        self.nc = bass.Bass("TRN2", target_bir_lowering=False)
        self.p = P(self.nc)
        self.uid = 0
        self.guards = []
        self.region = None

    def sb(self, shape, dtype, name="t"):
        self.uid += 1
        if self.region is None:
            g = self.nc.sbuf_tensor(f"{name}_{self.uid}", list(shape), dtype)
            t = g.__enter__()
            self.guards.append(g)
            return t
        nbytes = 2 if dtype == BF16 else 4
        for d in shape[1:]:
            nbytes *= d
        nbytes = (nbytes + 63) // 64 * 64
        off = self.region[0]
        assert off + nbytes <= self.region[1], f"arena region overflow: {name} {off}+{nbytes} > {self.region[1]}"
        self.region[0] = off + nbytes
        return self.nc.alloc_sbuf_tensor_at(f"{name}_{self.uid}", list(shape), dtype, offset=off)

    def set_region(self, lo, hi):
        self.region = [lo, hi]

    def mark(self):
        return len(self.guards)

    def release(self, mark):
        while len(self.guards) > mark:
            g = self.guards.pop()
            g.__exit__(None, None, None)

    def dram(self, name, shape, dtype, kind="Internal"):
        if self.dbg and kind == "Internal":
            kind = "ExternalOutput"
        return self.nc.dram_tensor(name, list(shape), dtype, kind=kind).ap()

    def dma(self, q, out, in_, sem, reads=(), writes=(), after=()):
        return self.p.op(q, lambda e: e.dma_start(out=out, in_=in_), reads, writes, dma_sem=sem, extra_deps=after)

    def mm(self, out, lhsT, rhs, start, stop, reads=(), writes=()):
        return self.p.op("pe", lambda e: e.matmul(out, lhsT, rhs, start=start, stop=stop), reads, writes)

    def act(self, out, in_, func, reads=(), writes=(), **kw):
        return self.p.op("act", lambda e: e.activation(out=out, in_=in_, func=func, **kw), reads, writes)

    def copy(self, eng, out, in_, reads=(), writes=()):
        if eng == "act":
            return self.act(out, in_, AF.Copy, reads, writes)
        return self.p.op(eng, lambda e: e.tensor_copy(out=out, in_=in_), reads, writes)

    def tt(self, eng, out, in0, in1, op, reads=(), writes=()):
        return self.p.op(eng, lambda e: e.tensor_tensor(out=out, in0=in0, in1=in1, op=op), reads, writes)

    def ts(self, eng, out, in0, s1, op0, reads=(), writes=()):
        return self.p.op(eng, lambda e: e.tensor_scalar(out=out, in0=in0, scalar1=s1, scalar2=None, op0=op0),
                         reads, writes)

    def recip(self, out, in_, reads=(), writes=()):
        return self.p.op("dve", lambda e: e.reciprocal(out=out, in_=in_), reads, writes)

    def build(self):
        nc, p = self.nc, self.p
        L = self.depth
        ein = lambda n, s: nc.dram_tensor(n, list(s), F32, kind="ExternalInput").ap()
        self.x_in = ein("x", (S, D))
        self.mem_in = ein("mem", (256, D))
        self.w_in = ein("w_in", (L, D, WIN_COLS))
        self.w_uq = ein("w_uq", (L, 384, 2048))
        self.w_ukv = ein("w_ukv", (L, 256, 2048))
        self.w_mem = ein("w_mem", (L, D, 1024))
        self.w_out = ein("w_out", (L, 2048, D))
        self.g_attn = ein("g_attn", (L, 128, D))
        self.g_q = ein("g_q", (L, 128, 3))
        self.g_kv = ein("g_kv", (L, 128, 2))
        self.g_mem = ein("g_mem", (L, 128, D))
        self.sinks = ein("sinks", (L, 128, 8))
        self.g_final = ein("g_final", (128, D))
        self.c_ident = ein("c_ident", (128, 128))
        self.c_tri = ein("c_tri", (128, 128))
        self.c_swab = ein("c_swab", (128, 16 * 128))
        self.c_ropeC = ein("c_ropeC", (128, S))
        self.c_ropeS = ein("c_ropeS", (128, S))
        self.y_out = nc.dram_tensor("y", [S, D], F32, kind="ExternalOutput").ap()
        self.XRES = self.dram("XRES", (S, D), F32)
        self.QN = self.dram("QN", (16 * 64, S), BF16)
        self.QR = self.dram("QR", (16 * 32, S), BF16)
        self.KN = self.dram("KN", (16 * 64, S), BF16)
        self.KPE = self.dram("KPE", (32, S), BF16)
        self.VM = self.dram("VM", (NT, 128, 16, 128), BF16)
        self.QS = self.dram("QS", (8 * 64, S), BF16)
        self.KS = self.dram("KS", (2 * 64, S), BF16)
        self.VS = self.dram("VS", (NT, 128, 4, 128), BF16)
        self.QM = self.dram("QM", (4, 128, S), BF16)
        self.G = self.dram("G", (2048, S), BF16)
        self.YT = self.dram("YT", (16, 128, S), BF16)

        self.ps = []
        for i in range(6):
            t = nc.alloc_psum_tensor(f"ps{i}", [128, 512], F32)
            self.ps.append((t, p.buf("ps", excl=True)))
        self.pst = []
        for i in range(2):
            t = nc.alloc_psum_tensor(f"pst{i}", [128, 1024], BF16)
            self.pst.append((t, p.buf("pst", excl=True)))
        self.ps_rr = 0
        self.pst_rr = 0

        self.setup_consts()
        p.barrier()
        self.KTm = self.sb((128, 4, 256), BF16, "KTm")
        self.Vm = self.sb((128, 2, 512), BF16, "Vm")
        self.memb = p.buf("memkv")
        SB_END = 16512 + 212863
        base = SB_END - nc.sbuf_bytes_remaining
        base = (base + 63) // 64 * 64
        end = SB_END // 64 * 64
        self.set_region(base, end)
        self.w_in_sb = self.sb((128, 8, WIN_COLS), BF16, "w_in")
        self.w_uq_sb = self.sb((128, 3, 2048), BF16, "w_uq")
        self.w_ukv_sb = self.sb((128, 2, 2048), BF16, "w_ukv")
        o_wmem = self.region[0]
        self.w_mem_sb = self.sb((128, 8, 1024), BF16, "w_mem")
        o_wout = self.region[0]
        self.w_out_sb = self.sb((128, 16, D), BF16, "w_out")
        o_r2 = self.region[0]
        self.wA0b = p.buf("wA0")
        self.wA1b = p.buf("wA1")
        self.wuqb = p.buf("wuq")
        self.wukvb = p.buf("wukv")
        self.wmb = p.buf("wmem")
        self.wob = p.buf("wout")
        WSEMS = ("w_mem", "wA0", "wA1", "wuq", "wukv", "w_out")

        def load_WA(l):
            jobs = []

            def add(dst, src, sem, buf):
                jobs.append(lambda after=(): self.dma("pool", dst, src, sem, writes=[buf], after=after))

            for kc in range(8):
                add(self.w_mem_sb[:, kc, :], self.w_mem[l][kc * 128:(kc + 1) * 128, :], "w_mem", self.wmb)
            for kc in range(8):
                add(self.w_in_sb[:, kc, 0:WSPLIT], self.w_in[l][kc * 128:(kc + 1) * 128, 0:WSPLIT], "wA0", self.wA0b)
            for kc in range(8):
                add(self.w_in_sb[:, kc, WSPLIT:WIN_COLS], self.w_in[l][kc * 128:(kc + 1) * 128, WSPLIT:WIN_COLS], "wA1",
                    self.wA1b)
            for kc in range(3):
                add(self.w_uq_sb[:, kc, :], self.w_uq[l][kc * 128:(kc + 1) * 128, :], "wuq", self.wuqb)
            for kc in range(2):
                add(self.w_ukv_sb[:, kc, :], self.w_ukv[l][kc * 128:(kc + 1) * 128, :], "wukv", self.wukvb)
            return jobs

        for j in load_WA(0):
            j()
        for l in range(L):
            last = (l == L - 1)
            x_src = self.x_in if l == 0 else self.XRES
            self.set_region(o_wout, end)
            self.phase_M(l)
            p.barrier(skip_sems=WSEMS)
            self.set_region(o_wmem, end)
            self.phase_A(l, x_src)
            p.barrier(skip_sems=WSEMS)
            if self.stop_after == ("A", l):
                break
            self.set_region(base, o_wout)
            self.phase_B(l)
            p.barrier(skip_sems=WSEMS)
            if self.stop_after == ("B", l):
                break
            self.set_region(o_r2, end)
            self.phase_C(l, x_src, last, [] if last else load_WA(l + 1))
            p.barrier(skip_sems=WSEMS)
        p.barrier()
        self.stats = p.emit()
        return nc

    def next_ps(self):
        r = self.ps[self.ps_rr % len(self.ps)]
        self.ps_rr += 1
        return r

    def next_pst(self):
        r = self.pst[self.pst_rr % 2]
        self.pst_rr += 1
        return r

    def setup_consts(self):
        p = self.p
        self.epsb = self.sb((128, 1), F32, "eps")
        self.ident = self.sb((128, 128), BF16, "ident")
        self.tri = self.sb((128, 128), BF16, "tri")
        self.swab = self.sb((128, 16 * 128), BF16, "swab")
        self.ones = self.sb((128, 128), BF16, "ones")
        self.cb = p.buf("consts")
        p.op("dve", lambda e: e.memset(self.epsb[:], EPS), writes=[self.cb])
        self.gfin = self.sb((128, D), F32, "gfin")
        L = self.depth
        self.gq = self.sb((128, L, 3), F32, "gq")
        self.gkv = self.sb((128, L, 2), F32, "gkv")
        self.esink = self.sb((128, L, 8), F32, "esink")
        self.vbuf = [(self.sb((128, 16, 128), BF16, "vbuf"), p.buf("vbuf")) for _ in range(4)]
        self.vsbuf = [(self.sb((128, 4, 128), BF16, "vsbuf"), p.buf("vsbuf")) for _ in range(2)]
        mtmp = self.mark()
        tmp = self.sb((128, 2048), F32, "ctmp")
        tb = p.buf("ctmp")
        self.dma("sp", tmp[:, 0:128], self.c_ident[:, :], "c0", writes=[tb])
        self.copy("dve", self.ident[:], tmp[:, 0:128], reads=[tb], writes=[self.cb])
        self.dma("sp", tmp[:, 0:128], self.c_tri[:, :], "c0", reads=[], writes=[tb])
        self.copy("dve", self.tri[:], tmp[:, 0:128], reads=[tb], writes=[self.cb])
        self.dma("sp", tmp[:, :], self.c_swab[:, :], "c0", writes=[tb])
        self.copy("dve", self.swab[:], tmp[:, :], reads=[tb], writes=[self.cb])
        p.op("dve", lambda e: e.memset(self.ones[:], 1.0), writes=[self.cb])
        self.dma("sp", self.gfin[:], self.g_final[:, :], "c1", writes=[self.cb])
        for l in range(L):
            self.dma("sp", self.gq[:, l, :], self.g_q[l], "c1", writes=[self.cb])
            self.dma("sp", self.gkv[:, l, :], self.g_kv[l], "c1", writes=[self.cb])
            self.dma("sp", self.esink[:, l, :], self.sinks[l], "c1", writes=[self.cb])
        p.barrier()
        for l in range(L):
            self.act(self.esink[:, l, :], self.esink[:, l, :], AF.Exp, reads=[self.cb], writes=[self.cb])
        for (t, b) in self.vbuf + self.vsbuf:
            p.op("pool", lambda e, t=t: e.memset(t[:], 1.0), writes=[b])
        p.barrier()
        self.release(mtmp)

    def load_weight(self, dst, src, KC, N, dst_buf, sem):
        for kc in range(KC):
            for c0 in range(0, N, 2048):
                n = min(2048, N - c0)
                self.dma("pool", dst[:, kc, c0:c0 + n], src[kc * 128:(kc + 1) * 128, c0:c0 + n], sem,
                         writes=[dst_buf])

    def stt(self, eng, out, in0, scalar, in1, op0, op1, reads=(), writes=()):
        return self.p.op(eng, lambda e: e.scalar_tensor_tensor(out=out, in0=in0, scalar=scalar, in1=in1, op0=op0,
                                                               op1=op1), reads, writes)

    def tok_prep(self, src_rows, xt, xb, sem, hq, hb, junk, jb, st, stb, gt, gb):
        if src_rows is not None:
            self.dma("sp", xt[:], src_rows, sem, writes=[xb])
        self.act(junk[:], xt[:], AF.Square, reads=[xb], writes=[jb, stb], accum_out=st[:, 0:1])
        self.act(st[:, 1:2], st[:, 0:1], AF.Sqrt, reads=[stb, self.cb], writes=[stb], scale=1.0 / D, bias=self.epsb[:, 0:1])
        self.recip(st[:, 2:3], st[:, 1:2], reads=[stb], writes=[stb])
        self.stt("dve", hq[:], xt[:], st[:, 2:3], gt[:], ALU.mult, ALU.mult, reads=[xb, stb, gb], writes=[hb])

    def transpose_to(self, h, hb, dstT, dst_buf, col0, evac_eng):
        pt, ptb = self.next_pst()
        for kc in range(8):
            self.p.op("pe", lambda e, kc=kc: e.transpose(pt[:, kc * 128:(kc + 1) * 128], h[:, kc * 128:(kc + 1) * 128],
                                                         self.ident[:]),
                      reads=[hb, self.cb], writes=[ptb])
        self.copy(evac_eng, dstT[:, :, col0:col0 + 128], pt[:].rearrange("p (k t) -> p k t", k=8), reads=[ptb],
                  writes=[dst_buf])

    def phase_M(self, l):
        p = self.p
        w_mem = self.w_mem_sb
        wmb = self.wmb
        gt = self.sb((128, D), F32, "gtm")
        gb = p.buf("gtm")
        self.dma("sp", gt[:], self.g_mem[l], "gt", writes=[gb])
        xts = [(self.sb((128, D), F32, "xt"), p.buf("xt")) for _ in range(2)]
        hs = [(self.sb((128, D), BF16, "h"), p.buf("h")) for _ in range(2)]
        junk = self.sb((128, D), BF16, "junk")
        jb = p.buf("junk")
        sts = [(self.sb((128, 4), F32, "st"), p.buf("st")) for _ in range(2)]
        memT = self.sb((128, 8, 256), BF16, "memT")
        memTb = p.buf("memT")
        for mt in range(2):
            xt, xb = xts[mt]
            hq, hb = hs[mt]
            st, stb = sts[mt]
            self.tok_prep(self.mem_in[mt * 128:(mt + 1) * 128, :], xt, xb, f"xt{mt}", hq, hb, junk, jb, st, stb,
                          gt, gb)
            self.transpose_to(hq, hb, memT, memTb, mt * 128, "dve")
        for hm in range(4):
            pt, pb = self.next_ps()
            for kc in range(8):
                self.mm(pt[:, 0:256], w_mem[:, kc, hm * 128:(hm + 1) * 128], memT[:, kc, :], kc == 0, kc == 7,
                        reads=[wmb, memTb], writes=[pb])
            self.copy("act", self.KTm[:, hm, :], pt[:, 0:256], reads=[pb], writes=[self.memb])
        for mt in range(2):
            pt, pb = self.next_ps()
            for kc in range(8):
                self.mm(pt[:, :], memT[:, kc, mt * 128:(mt + 1) * 128], w_mem[:, kc, 512:1024], kc == 0, kc == 7,
                        reads=[wmb, memTb], writes=[pb])
            self.copy("dve", self.Vm[:, mt, :], pt[:, :], reads=[pb], writes=[self.memb])

    def phase_A(self, l, x_src):
        p = self.p
        epsbuf = self.cb
        w_in, w_uq, w_ukv = self.w_in_sb, self.w_uq_sb, self.w_ukv_sb
        wbuf_of = {id(w_uq): [self.wuqb], id(w_ukv): [self.wukvb]}
        gt = self.sb((128, D), F32, "gta")
        gb = p.buf("gta")
        self.dma("sp", gt[:], self.g_attn[l], "gt", writes=[gb])
        xts = [(self.sb((128, D), F32, "xt"), p.buf("xt")) for _ in range(4)]
        hs = [(self.sb((128, D), BF16, "h"), p.buf("h")) for _ in range(4)]
        junk = self.sb((128, D), BF16, "junk")
        jb = p.buf("junk")
        sts = [(self.sb((128, 4), F32, "st"), p.buf("st")) for _ in range(4)]

        hT = [(self.sb((128, 8, 512), BF16, "hT"), p.buf("hT")) for _ in range(1)]
        cqn = self.sb((128, 3, 512), BF16, "cqn")
        cqnb = p.buf("cqn")
        ckvn = self.sb((128, 2, 512), BF16, "ckvn")
        ckvnb = p.buf("ckvn")
        sq = [(self.sb((128, 512), BF16, "sq"), p.buf("sq")) for _ in range(5)]
        csb = [(self.sb((128, 512), F32, "csb"), p.buf("csb")) for _ in range(5)]
        hT_cur = [None, None]
        rs = [(self.sb((128, 512), F32, "rs"), p.buf("rs")) for _ in range(2)]
        tabs = [(self.sb((128, 2, 512), F32, "tab"), p.buf("tab")) for _ in range(2)]
        rt = [(self.sb((128, 2, 512), F32, "rt"), p.buf("rt")) for _ in range(1)]
        stg = [(self.sb((128, 512), BF16, "stg"), p.buf("stg")) for _ in range(12)]
        self.stg_rr = 0

        def next_stg():
            i = self.stg_rr % len(stg)
            self.stg_rr += 1
            return stg[i][0], stg[i][1], f"stg{i}"

        def queue_of(sem):
            return "pool" if (sem.startswith("stg") and int(sem[3:]) % 3 == 2) else "sp"

        def prep_load(c):
            for tt in range(4):
                T = c * 4 + tt
                xt, xb = xts[tt]
                self.dma("sp", xt[:], x_src[T * 128:(T + 1) * 128, :], f"xt{tt}", writes=[xb])

        def prep(c):
            for tt in range(4):
                i = (c * 4 + tt)
                xt, xb = xts[tt]
                hq, hb = hs[i % 4]
                st, stb = sts[i % 4]
                self.tok_prep(None, xt, xb, None, hq, hb, junk, jb, st, stb, gt, gb)
            tb_t, tb_b = tabs[c % 2]
            self.dma("sp", tb_t[:, 0, :], self.c_ropeC[:, c * 512:(c + 1) * 512], f"tab{c % 2}", writes=[tb_b])
            self.dma("sp", tb_t[:, 1, :], self.c_ropeS[:, c * 512:(c + 1) * 512], f"tab{c % 2}", writes=[tb_b])

        def proj(pt, pb, w, wcols, KC, rhsT, rb, M):
            if id(w) in wbuf_of:
                wbs = wbuf_of[id(w)]
            else:
                wbs = [self.wA0b] if wcols[1] <= WSPLIT else ([self.wA1b] if wcols[0] >= WSPLIT else
                                                             [self.wA0b, self.wA1b])
            for kc in range(KC):
                self.mm(pt[0:M, :], w[:, kc, wcols[0]:wcols[1]], rhsT[:, kc, :], kc == 0, kc == KC - 1,
                        reads=wbs + [rb], writes=[pb])

        def store(src_ap, dst_ap, sbuf_, sem):
            self.dma(queue_of(sem), dst_ap, src_ap, sem, reads=[sbuf_])

        ev = [0]

        def evac_copy(pt, pb, M=128):
            s_t, s_b, sem = next_stg()
            eng = "act" if ev[0] % 2 == 0 else "dve"
            ev[0] += 1
            self.copy(eng, s_t[0:M, :], pt[0:M, :], reads=[pb], writes=[s_b])
            return s_t, s_b, sem

        def latent_start(cols, k0):
            items = []
            for j, c0 in enumerate(cols):
                pt, pb = self.next_ps()
                proj(pt, pb, w_in, (c0, c0 + 128), 8, hT_cur[0], hT_cur[1], 128)
                s_t, s_b = sq[k0 + j]
                c_t, c_b = csb[k0 + j]
                self.act(s_t[:], pt[:], AF.Square, reads=[pb], writes=[s_b])
                self.copy("dve", c_t[:], pt[:], reads=[pb], writes=[c_b])
                items.append((s_t, s_b, c_t, c_b))
            return items

        def latent_finish(items, nfeat, gain, dst, dstb, k):
            pss, pssb = self.next_ps()
            n = len(items)
            for j, (s_t, s_b, c_t, c_b) in enumerate(items):
                self.mm(pss[:, :], self.ones[:], s_t[:], j == 0, j == n - 1, reads=[s_b, self.cb], writes=[pssb])
            r_t, r_b = rs[k]
            self.act(r_t[:], pss[:], AF.Sqrt, reads=[pssb, epsbuf], writes=[r_b], scale=1.0 / nfeat,
                     bias=self.epsb[:, 0:1])
            self.recip(r_t[:], r_t[:], reads=[r_b], writes=[r_b])
            for j, (s_t, s_b, c_t, c_b) in enumerate(items):
                self.stt("dve", dst[:, j, :], c_t[:], gain[:, j:j + 1], r_t[:], ALU.mult, ALU.mult,
                         reads=[c_b, r_b, self.cb], writes=[dstb])

        def rope(ptA, pbA, ptB, pbB, M, tab, tabb, c):
            r_t, r_b = rt[0]
            self.tt("dve", r_t[0:M, 0, :], ptA[0:M, :], tab[0:M, 0, :], ALU.mult, reads=[pbA, tabb], writes=[r_b])
            self.tt("dve", r_t[0:M, 1, :], ptB[0:M, :], tab[0:M, 1, :], ALU.mult, reads=[pbB, tabb], writes=[r_b])
            s_t, s_b, sem = next_stg()
            self.tt("pool", s_t[0:M, :], r_t[0:M, 0, :], r_t[0:M, 1, :], ALU.add, reads=[r_b], writes=[s_b])
            return s_t, s_b, sem

        prep_load(0)
        prep(0)
        for c in range(NCH):
            tk = slice(c * 512, (c + 1) * 512)
            hT_t, hT_b = hT[0]
            tab_t, tab_b = tabs[c % 2]
            for tt in range(4):
                i = c * 4 + tt
                hq, hb = hs[i % 4]
                self.transpose_to(hq, hb, hT_t, hT_b, tt * 128, "dve" if tt % 2 == 0 else "act")
            if c + 1 < NCH:
                prep_load(c + 1)
            hT_cur[0], hT_cur[1] = hT_t, hT_b
            it_q = latent_start([0, 128, 256], 0)
            it_kv = latent_start([384, 512], 3)
            ptA, pbA = self.next_ps()
            proj(ptA, pbA, w_in, (640, 672), 8, hT_t, hT_b, 32)
            ptB, pbB = self.next_ps()
            proj(ptB, pbB, w_in, (4000, 4032), 8, hT_t, hT_b, 32)
            s_t, s_b, sem = rope(ptA, pbA, ptB, pbB, 32, tab_t, tab_b, c)
            store(s_t[0:32, :], self.KPE[:, tk], s_b, sem)
            zcols = [672 + m * 128 for m in range(8)] + [2464 + m * 128 for m in range(4)] + \
                    [3488 + m * 128 for m in range(4)]
            for m, c0 in enumerate(zcols):
                pt, pb = self.next_ps()
                proj(pt, pb, w_in, (c0, c0 + 128), 8, hT_t, hT_b, 128)
                s_t, s_b, sem = next_stg()
                self.act(s_t[:], pt[:], AF.Silu, reads=[pb], writes=[s_b])
                store(s_t[:], self.G[m * 128:(m + 1) * 128, tk], s_b, sem)
                if m == 1:
                    latent_finish(it_q, 384, self.gq[:, l, :], cqn, cqnb, 0)
                    latent_finish(it_kv, 256, self.gkv[:, l, :], ckvn, ckvnb, 1)
            if c + 1 < NCH:
                prep(c + 1)
            for m in range(4):
                pt, pb = self.next_ps()
                c0 = 1696 + m * 128
                proj(pt, pb, w_in, (c0, c0 + 128), 8, hT_t, hT_b, 128)
                s_t, s_b, sem = evac_copy(pt, pb)
                store(s_t[:], self.QS[m * 128:(m + 1) * 128, tk], s_b, sem)
            pt, pb = self.next_ps()
            proj(pt, pb, w_in, (2208, 2336), 8, hT_t, hT_b, 128)
            s_t, s_b, sem = evac_copy(pt, pb)
            store(s_t[:], self.KS[:, tk], s_b, sem)
            for m in range(4):
                pt, pb = self.next_ps()
                c0 = 2976 + m * 128
                proj(pt, pb, w_in, (c0, c0 + 128), 8, hT_t, hT_b, 128)
                s_t, s_b, sem = evac_copy(pt, pb)
                store(s_t[:], self.QM[m, :, tk], s_b, sem)
            for tt in range(4):
                T = c * 4 + tt
                pt, pb = self.next_ps()
                for kc in range(8):
                    self.mm(pt[:, 0:128], hT_t[:, kc, tt * 128:(tt + 1) * 128], w_in[:, kc, 2336:2464], kc == 0,
                            kc == 7, reads=[self.wA1b, hT_b], writes=[pb])
                vs_t, vs_b = self.vsbuf[T % 2]
                pv = pt[:, 0:128].rearrange("p (k d) -> p k d", k=2)
                v4 = vs_t[:].rearrange("p (k v) c -> p k v c", k=2)
                ee = "dve" if tt % 2 == 0 else "act"
                self.copy(ee, v4[:, :, 0, 0:64], pv, reads=[pb], writes=[vs_b])
                self.copy(ee, v4[:, :, 1, 64:128], pv, reads=[pb], writes=[vs_b])
                store(vs_t[:], self.VS[T], vs_b, f"vsb{T % 2}")
            for tt in range(4):
                T = c * 4 + tt
                vb_t, vb_b = self.vbuf[T % 4]
                for half in range(2):
                    pt, pb = self.next_ps()
                    for kc in range(2):
                        self.mm(pt[:, :], ckvn[:, kc, tt * 128:(tt + 1) * 128],
                                w_ukv[:, kc, 1024 + half * 512:1024 + (half + 1) * 512], kc == 0, kc == 1,
                                reads=[self.wukvb, ckvnb], writes=[pb])
                    pv = pt[:, :].rearrange("p (h two d) -> p h two d", two=2, d=64)
                    vv = vb_t[:, half * 8:(half + 1) * 8, :].rearrange("p (h two) c -> p h two c", two=2)
                    ee = "dve" if half == 0 else "act"
                    self.copy(ee, vv[:, :, 0, 0:64], pv[:, :, 0, :], reads=[pb], writes=[vb_b])
                    self.copy(ee, vv[:, :, 1, 64:128], pv[:, :, 1, :], reads=[pb], writes=[vb_b])
                store(vb_t[:], self.VM[T], vb_b, f"vb{T % 4}")
            for m in range(8):
                pt, pb = self.next_ps()
                proj(pt, pb, w_uq, (m * 128, (m + 1) * 128), 3, cqn, cqnb, 128)
                s_t, s_b, sem = evac_copy(pt, pb)
                store(s_t[:], self.QN[m * 128:(m + 1) * 128, tk], s_b, sem)
            for m in range(4):
                ptA, pbA = self.next_ps()
                proj(ptA, pbA, w_uq, (1024 + m * 128, 1024 + (m + 1) * 128), 3, cqn, cqnb, 128)
                ptB, pbB = self.next_ps()
                proj(ptB, pbB, w_uq, (1536 + m * 128, 1536 + (m + 1) * 128), 3, cqn, cqnb, 128)
                s_t, s_b, sem = rope(ptA, pbA, ptB, pbB, 128, tab_t, tab_b, c)
                store(s_t[:], self.QR[m * 128:(m + 1) * 128, tk], s_b, sem)
            for m in range(8):
                pt, pb = self.next_ps()
                proj(pt, pb, w_ukv, (m * 128, (m + 1) * 128), 2, ckvn, ckvnb, 128)
                s_t, s_b, sem = evac_copy(pt, pb)
                store(s_t[:], self.KN[m * 128:(m + 1) * 128, tk], s_b, sem)
    def phase_B(self, l):
        p = self.p
        QTb = [(self.sb((128, S), BF16, "QTb"), p.buf("QTb")) for _ in range(2)]
        KTb = [(self.sb((128, S), BF16, "KTb"), p.buf("KTb")) for _ in range(2)]
        Vb = [(self.sb((128, NT, 128), BF16, "Vb"), p.buf("Vb")) for _ in range(2)]
        Vq = [[p.buf("Vq") for _ in range(4)] for _ in range(2)]
        Gb = [(self.sb((128, S), BF16, "Gb"), p.buf("Gb")) for _ in range(2)]
        PT = [(self.sb((128, 512), BF16, "PT"), p.buf("PT")) for _ in range(6)]
        Rb = [(self.sb((128, 512), F32, "R"), p.buf("R")) for _ in range(3)]
        Y1 = [(self.sb((128, 512), F32, "Y1"), p.buf("Y1")) for _ in range(3)]
        yb = [(self.sb((128, 512), BF16, "yb"), p.buf("yb")) for _ in range(3)]
        ST = self.ps[0:3]
        OT = self.ps[3:5]
        SM = self.ps[5]
        STM = self.ps[0:4]
        OTM = self.ps[4:6]
        cnt = dict(ep=0, yb=0, u=0)
        LA = 2

        def run_pipeline(units, LA=2):
            n = len(units)
            base = cnt["u"]
            cnt["u"] += n
            for i in range(min(LA, n)):
                units[i]["qk"](base + i)
            pending = None
            for i in range(n):
                if i + LA < n:
                    units[i + LA]["qk"](base + i + LA)
                units[i]["ex"](base + i)
                if pending is not None:
                    pending()
                    pending = None
                units[i]["pv"](base + i)
                if "post" in units[i]:
                    pending = units[i]["post"]
            if pending is not None:
                pending()

        for i in range(2):
            self.dma("sp", KTb[i][0][0:32, :], self.KPE[:, :], f"KTb{i}", writes=[KTb[i][1]])

        def epilogue(o_t, o_b, s_t, s_b, o_sl, s_sl, g_t, g_b, c, dst_ap, sink_ap=None, act_recip=False):
            i = cnt["ep"]
            cnt["ep"] += 1
            r_t, r_b = Rb[i % 3]
            y1_t, y1_b = Y1[i % 3]
            j = cnt["yb"]
            cnt["yb"] += 1
            y_t, y_b = yb[j % 3]
            if act_recip:
                kw = dict(bias=sink_ap) if sink_ap is not None else {}
                self.act(r_t[o_sl, :], s_t[s_sl, :], AF.Ln, reads=[s_b, self.cb], writes=[r_b], **kw)
                self.act(r_t[o_sl, :], r_t[o_sl, :], AF.Exp, reads=[r_b], writes=[r_b], scale=-1.0)
            else:
                self.tt("dve", y1_t[o_sl, :], o_t[o_sl, :], g_t[o_sl, c * 512:(c + 1) * 512], ALU.mult,
                        reads=[o_b, g_b], writes=[y1_b])
                self.copy("dve", r_t[o_sl, :], s_t[s_sl, :], reads=[s_b], writes=[r_b])
                self.recip(r_t[o_sl, :], r_t[o_sl, :], reads=[r_b], writes=[r_b])
                self.tt("pool", y_t[o_sl, :], y1_t[o_sl, :], r_t[o_sl, :], ALU.mult, reads=[y1_b, r_b], writes=[y_b])
                self.dma("sp", dst_ap, y_t[o_sl, :], f"yb{j % 3}", reads=[y_b])
                return
            self.tt("dve", y1_t[o_sl, :], o_t[o_sl, :], r_t[o_sl, :], ALU.mult, reads=[o_b, r_b], writes=[y1_b])
            self.tt("pool", y_t[o_sl, :], y1_t[o_sl, :], g_t[o_sl, c * 512:(c + 1) * 512], ALU.mult,
                    reads=[y1_b, g_b], writes=[y_b])
            self.dma("sp", dst_ap, y_t[o_sl, :], f"yb{j % 3}", reads=[y_b])

        sc_mla = float(96 ** -0.5)

        def load_mla(h):
            i = h % 2
            self.dma("sp", QTb[i][0][0:32, :], self.QR[h * 32:(h + 1) * 32, :], f"QTb{i}", writes=[QTb[i][1]])
            self.dma("sp", QTb[i][0][32:96, :], self.QN[h * 64:(h + 1) * 64, :], f"QTb{i}", writes=[QTb[i][1]])
            self.dma("sp", KTb[i][0][32:96, :], self.KN[h * 64:(h + 1) * 64, :], f"KTb{i}", writes=[KTb[i][1]])
            par = h % 2
            vm = self.VM.rearrange("t p h c -> p t h c")
            for qd in range(4):
                self.dma("sp", Vb[i][0][:, qd * 8:(qd + 1) * 8, :], vm[:, qd * 8:(qd + 1) * 8, h, :], f"Vq{i}_{qd}",
                         writes=[Vq[i][qd]], reads=[Vb[i][1]])
                if qd == 0:
                    self.dma("sp", Gb[i][0][par * 64:(par + 1) * 64, :], self.G[h * 64:(h + 1) * 64, :], f"Gb{i}",
                             writes=[Gb[i][1]])

        def mla_unit(h, c, t):
            i2 = h % 2
            q_t, q_b = QTb[i2]
            k_t, k_b = KTb[i2]
            v_t, v_b = Vb[i2]
            g_t, g_b = Gb[i2]
            par = h % 2
            o_sl = slice(par * 64, (par + 1) * 64)
            s_sl = slice((1 - par) * 64, (2 - par) * 64)
            j = t - 4 * c
            lo = max(j, 0) * 128
            o_t, o_b = OTM[c % 2]

            def qk(i):
                st_t, st_b = STM[i % 4]
                self.mm(st_t[:, lo:512], k_t[0:96, t * 128:(t + 1) * 128], q_t[0:96, c * 512 + lo:(c + 1) * 512], True,
                        j < 0, reads=[k_b, q_b], writes=[st_b])
                if j >= 0:
                    self.mm(st_t[:, lo:lo + 128], self.ident[:], self.tri[:], False, True, reads=[self.cb],
                            writes=[st_b])

            def ex(i):
                st_t, st_b = STM[i % 4]
                pt_t, pt_b = PT[i % 6]
                self.act(pt_t[:, lo:512], st_t[:, lo:512], AF.Exp, reads=[st_b], writes=[pt_b], scale=sc_mla)

            def pv(i):
                pt_t, pt_b = PT[i % 6]
                self.mm(o_t[:, lo:512], v_t[:, t, :], pt_t[:, lo:512], t == 0, t == 4 * c + 3,
                        reads=[Vq[i2][t // 8], pt_b], writes=[o_b])

            u = dict(qk=qk, ex=ex, pv=pv)
            if t == 4 * c + 3:
                def post():
                    epilogue(o_t, o_b, o_t, o_b, o_sl, s_sl, g_t, g_b, c,
                             self.YT[h // 2, par * 64:(par + 1) * 64, c * 512:(c + 1) * 512])
                    if c == NCH - 1 and h + 2 < 16:
                        load_mla(h + 2)
                    if c == NCH - 1 and h == 0:
                        self.load_weight(self.w_out_sb, self.w_out[l], 16, D, self.wob, "w_out")
                    if c == NCH - 1 and h == 14:
                        swa_preload(0)
                u["post"] = post
            return u

        def swa_preload(i):
            p.op("pool", lambda e, t=QTb[i][0]: e.memset(t[64:128, :], 0.0), writes=[QTb[i][1]])
            p.op("pool", lambda e, t=KTb[i][0]: e.memset(t[64:128, :], 0.0), writes=[KTb[i][1]])
            self.dma("sp", KTb[i][0][0:64, :], self.KS[i * 64:(i + 1) * 64, :], f"KTb{i}", writes=[KTb[i][1]])
            load_swa_v(0, i)
            load_swa_q(i)

        def load_swa_v(kvh, var):
            self.dma("sp", Vb[var][0][:], self.VS.rearrange("t p k c -> p t k c")[:, :, kvh * 2 + var, :], f"Vb{var}",
                     writes=[Vb[var][1]] + Vq[var])

        def load_swa_q(h):
            par = h % 2
            self.dma("sp", QTb[h % 2][0][0:64, :], self.QS[h * 64:(h + 1) * 64, :], f"QTb{h % 2}", writes=[QTb[h % 2][1]])
            self.dma("sp", Gb[h % 2][0][par * 64:(par + 1) * 64, :], self.G[1024 + h * 64:1024 + (h + 1) * 64, :],
                     f"Gb{h % 2}", writes=[Gb[h % 2][1]])

        def load_mem(hm):
            self.dma("sp", QTb[hm % 2][0][:, :], self.QM[hm, :, :], f"QTb{hm % 2}", writes=[QTb[hm % 2][1]])
            self.dma("sp", Gb[hm % 2][0][:, :], self.G[1536 + hm * 128:1536 + (hm + 1) * 128, :], f"Gb{hm % 2}",
                     writes=[Gb[hm % 2][1]])

        load_mla(0)
        load_mla(1)
        units = [mla_unit(h, c, t) for h in range(16) for c in range(NCH) for t in range(4 * c + 4)]
        run_pipeline(units, LA=3)
        swa_preload(1)

        sc_swa = 0.125

        def swa_unit(h, c, half):
            kvh = h // 4
            par = h % 2
            ks_t, ks_b = KTb[kvh]
            q_t, q_b = QTb[h % 2]
            g_t, g_b = Gb[h % 2]
            v_t, v_b = Vb[par]
            o_sl = slice(par * 64, (par + 1) * 64)
            s_sl = slice((1 - par) * 64, (2 - par) * 64)
            o_t, o_b = OTM[c % 2]
            slots = []
            for nn in range(2):
                n = c * 4 + half * 2 + nn
                for typ in range(2):
                    t = n - 1 + typ
                    if t >= 0:
                        slots.append((nn, n, typ, t, (nn * 2 + typ) * 128))
            lo = min(s[4] for s in slots)

            def qk(i):
                st_t, st_b = STM[i % 4]
                for (nn, n, typ, t, col) in slots:
                    self.mm(st_t[:, col:col + 128], ks_t[:, t * 128:(t + 1) * 128], q_t[:, n * 128:(n + 1) * 128], True,
                            False, reads=[ks_b, q_b], writes=[st_b])
                    self.mm(st_t[:, col:col + 128], self.ident[:],
                            self.swab[:, (h * 2 + typ) * 128:(h * 2 + typ + 1) * 128], False, True,
                            reads=[self.cb], writes=[st_b])

            def ex(i):
                st_t, st_b = STM[i % 4]
                pt_t, pt_b = PT[i % 6]
                self.act(pt_t[:, lo:512], st_t[:, lo:512], AF.Exp, reads=[st_b], writes=[pt_b], scale=sc_swa)

            def pv(i):
                pt_t, pt_b = PT[i % 6]
                first_n = {}
                for (nn, n, typ, t, col) in slots:
                    oc = (half * 2 + nn) * 128
                    self.mm(o_t[:, oc:oc + 128], v_t[:, t, :], pt_t[:, col:col + 128], n not in first_n, typ == 1,
                            reads=[v_b, pt_b], writes=[o_b])
                    first_n[n] = True

            u = dict(qk=qk, ex=ex, pv=pv)
            if half == 1:
                def post():
                    epilogue(o_t, o_b, o_t, o_b, o_sl, s_sl, g_t, g_b, c,
                             self.YT[8 + h // 2, par * 64:(par + 1) * 64, c * 512:(c + 1) * 512],
                             sink_ap=self.esink[s_sl, l, h:h + 1], act_recip=True)
                    if c == NCH - 1:
                        if h + 2 < 8:
                            load_swa_q(h + 2)
                        if h == 2:
                            load_swa_v(1, 0)
                        if h == 3:
                            load_swa_v(1, 1)
                        if h >= 6:
                            load_mem(h - 6)
                u["post"] = post
            return u

        units = [swa_unit(h, c, half) for h in range(8) for c in range(NCH) for half in range(2)]
        run_pipeline(units, LA=3)

        sc_mem = float(128 ** -0.5)
        def mem_unit(hm, c, mt):
            q_t, q_b = QTb[hm % 2]
            q2 = q_t
            g_t, g_b = Gb[hm % 2]
            o_t, o_b = OT[c % 2]
            s_t, s_b = SM

            def qk(i):
                st_t, st_b = ST[i % 3]
                self.mm(st_t[:, :], self.KTm[:, hm, mt * 128:(mt + 1) * 128], q2[:, c * 512:(c + 1) * 512], True,
                        True, reads=[self.memb, q_b], writes=[st_b])

            def ex(i):
                st_t, st_b = ST[i % 3]
                pt_t, pt_b = PT[i % 4]
                self.act(pt_t[:, :], st_t[:, :], AF.Exp, reads=[st_b], writes=[pt_b], scale=sc_mem)

            def pv(i):
                pt_t, pt_b = PT[i % 4]
                self.mm(o_t[:, :], self.Vm[:, mt, hm * 128:(hm + 1) * 128], pt_t[:, :], mt == 0, mt == 1,
                        reads=[self.memb, pt_b], writes=[o_b])
                self.mm(s_t[:, :], self.ones[:], pt_t[:, :], mt == 0, mt == 1, reads=[self.cb, pt_b],
                        writes=[s_b])

            u = dict(qk=qk, ex=ex, pv=pv)
            if mt == 1:
                def post():
                    sl = slice(0, 128)
                    epilogue(o_t, o_b, s_t, s_b, sl, sl, g_t, g_b, c, self.YT[12 + hm, :, c * 512:(c + 1) * 512],
                             act_recip=True)
                    if c == NCH - 1 and hm + 2 < 4:
                        load_mem(hm + 2)
                u["post"] = post
            return u

        units = [mem_unit(hm, c, mt) for hm in range(4) for c in range(NCH) for mt in range(2)]
        run_pipeline(units)

    def phase_C(self, l, x_src, last, prefetch):
        p = self.p
        w_out, wb = self.w_out_sb, self.wob
        CW = 256
        yT = [(self.sb((128, 16, CW), BF16, "yT"), p.buf("yT")) for _ in range(2)]
        xc = [(self.sb((128, D), F32, "xc"), p.buf("xc")) for _ in range(2)]
        xn = [(self.sb((128, D), F32, "xn"), p.buf("xn")) for _ in range(2)]
        junk = self.sb((128, D), BF16, "junkc")
        jb = p.buf("junkc")
        sts = [(self.sb((128, 4), F32, "stc"), p.buf("stc")) for _ in range(4)]
        p.op("dve", lambda e: e.memset(junk[:], 0.0), writes=[jb])
        nchunk = S // CW

        def load_chunk(c):
            t, b = yT[c % 2]
            for half in range(2):
                self.dma("sp", t[:, half * 8:(half + 1) * 8, :],
                         self.YT.rearrange("k p s -> p k s")[:, half * 8:(half + 1) * 8, c * CW:(c + 1) * CW],
                         f"yT{c % 2}", writes=[b])

        def load_x(T):
            x_t, x_b = xc[T % 2]
            self.dma("sp", x_t[:], x_src[T * 128:(T + 1) * 128, :], f"xc{T % 2}", writes=[x_b])

        load_chunk(0)
        load_x(0)
        for c in range(nchunk):
            if c + 1 < nchunk:
                load_chunk(c + 1)
            y_t, y_b = yT[c % 2]
            for tt in range(CW // 128):
                T = c * (CW // 128) + tt
                x_t, x_b = xc[T % 2]
                n_t, n_b = xn[T % 2]
                if T + 1 < NT:
                    load_x(T + 1)
                for half in range(2):
                    pt, pb = self.next_ps()
                    for kc in range(16):
                        self.mm(pt[:, :], y_t[:, kc, tt * 128:(tt + 1) * 128], w_out[:, kc, half * 512:(half + 1) * 512],
                                kc == 0, kc == 15, reads=[y_b, wb], writes=[pb])
                    mark = self.tt("dve", n_t[:, half * 512:(half + 1) * 512], pt[:, :],
                                   x_t[:, half * 512:(half + 1) * 512], ALU.add, reads=[pb, x_b], writes=[n_b])
                if prefetch:
                    prefetch.pop(0)(after=(mark,))
                if not last:
                    self.dma("sp", self.XRES[T * 128:(T + 1) * 128, :], n_t[:], f"xn{T % 2}", reads=[n_b])
                else:
                    st, stb = sts[T % 4]
                    self.act(junk[:], n_t[:], AF.Square, reads=[n_b], writes=[jb, stb], accum_out=st[:, 0:1])
                    self.act(st[:, 1:2], st[:, 0:1], AF.Sqrt, reads=[stb, self.cb], writes=[stb], scale=1.0 / D,
                             bias=self.epsb[:, 0:1])
                    self.recip(st[:, 2:3], st[:, 1:2], reads=[stb], writes=[stb])
                    self.stt("dve", n_t[:], n_t[:], st[:, 2:3], self.gfin[:], ALU.mult, ALU.mult,
                             reads=[n_b, stb, self.cb], writes=[n_b])
                    self.dma("sp", self.y_out[T * 128:(T + 1) * 128, :], n_t[:], f"xn{T % 2}", reads=[n_b])
        while prefetch:
            prefetch.pop(0)()


def _consts():
    ident = np.eye(128, dtype=np.float32)
    k = np.arange(128)[:, None]
    q = np.arange(128)[None, :]
    tri = np.where(q >= k, 0.0, NEGB).astype(np.float32)
    swab = np.zeros((128, 16, 128), np.float32)
    for h in range(8):
        slope = 2.0 ** (-(h + 1))
        d_prev = q + 128 - k
        swab[:, h * 2 + 0, :] = np.where(d_prev < 128, -8.0 * slope * d_prev, NEGB)
        d_cur = q - k
        swab[:, h * 2 + 1, :] = np.where(d_cur >= 0, -8.0 * slope * d_cur, NEGB)
    inv = (10000.0 ** (-np.arange(0, 32, 2, dtype=np.float32) / np.float32(32))).astype(np.float32)
    ang = (np.arange(S, dtype=np.float32)[:, None] * inv[None, :]).astype(np.float32)
    cos = np.cos(ang).astype(np.float32).T
    sin = np.sin(ang).astype(np.float32).T
    C32 = np.concatenate([cos, cos], 0)
    S32 = np.concatenate([-sin, sin], 0)
    ropeC = np.ascontiguousarray(np.tile(C32, (4, 1)))
    ropeS = np.ascontiguousarray(np.tile(S32, (4, 1)))
    return dict(c_ident=ident, c_tri=tri, c_swab=np.ascontiguousarray(swab.reshape(128, 16 * 128)),
                c_ropeC=ropeC, c_ropeS=ropeS)


def _layout_weights(attn_norm, w_in, mla_q_norm, w_uq, mla_kv_norm, w_ukv, swa_sinks, mem_norm, w_mem_kv, w_out,
                    final_norm):
    L = w_in.shape[0]
    f = lambda a: np.ascontiguousarray(a, dtype=np.float32)
    swap_cols = np.concatenate([np.arange(656, 672), np.arange(640, 656)])
    w_in_ext = np.concatenate([w_in, w_in[:, :, swap_cols]], axis=2)
    hq = np.arange(16)[:, None] * 96
    nope = (hq + np.arange(64)[None, :]).reshape(-1)
    rope = (hq + 64 + np.arange(32)[None, :]).reshape(-1)
    sw = np.concatenate([np.arange(16, 32), np.arange(0, 16)])
    ropes = (hq + 64 + sw[None, :]).reshape(-1)
    w_uq_r = w_uq[:, :, np.concatenate([nope, rope, ropes])]
    hk = np.arange(16)[:, None] * 128
    kn = (hk + np.arange(64)[None, :]).reshape(-1)
    vv = (hk + 64 + np.arange(64)[None, :]).reshape(-1)
    w_ukv_r = w_ukv[:, :, np.concatenate([kn, vv])]
    pk = lambda g, kc: np.ascontiguousarray(g.reshape(L, kc, 128).transpose(0, 2, 1))
    return dict(
        w_in=f(w_in_ext), w_uq=f(w_uq_r), w_ukv=f(w_ukv_r), w_mem=f(w_mem_kv), w_out=f(w_out),
        g_attn=f(np.broadcast_to(attn_norm[:, None, :], (L, 128, D))), g_q=f(pk(mla_q_norm, 3)),
        g_kv=f(pk(mla_kv_norm, 2)), g_mem=f(np.broadcast_to(mem_norm[:, None, :], (L, 128, D))),
        sinks=f(np.broadcast_to(swa_sinks[:, None, :], (L, 128, 8))),
        g_final=f(np.broadcast_to(final_norm[None, :], (128, D))),
    )


_CACHE = {}


def _get_nc():
    if "nc" not in _CACHE:
        b = Builder()
        _CACHE["nc"] = b.build()
        _CACHE["stats"] = b.stats
    return _CACHE["nc"]


def kernel(x, mem, attn_norm, w_in, mla_q_norm, w_uq, mla_kv_norm, w_ukv, swa_sinks, mem_norm, w_mem_kv, w_out,
           final_norm):
    x = np.asarray(x, dtype=np.float32)
    mem = np.asarray(mem, dtype=np.float32)
    B = x.shape[0]
    shared = _layout_weights(*[np.asarray(a, dtype=np.float32) for a in
                               (attn_norm, w_in, mla_q_norm, w_uq, mla_kv_norm, w_ukv, swa_sinks, mem_norm,
                                w_mem_kv, w_out, final_norm)])
    shared.update(_consts())
    nc = _get_nc()
    in_maps = []
    for b in range(B):
        m = dict(shared)
        m["x"] = np.ascontiguousarray(x[b])
        m["mem"] = np.ascontiguousarray(mem[b])
        in_maps.append(m)
    res = run_bass_kernel_spmd(nc, in_maps, core_ids=list(range(B)))
    return np.stack([np.asarray(r["y"], dtype=np.float32) for r in res.results], axis=0)
```

```python
import numpy as np
import concourse.bass as bass
import concourse.mybir as mybir
from concourse.bass_utils import run_bass_kernel_spmd

F32 = mybir.dt.float32
BF16 = mybir.dt.bfloat16
AF = mybir.ActivationFunctionType
ALU = mybir.AluOpType

S = 4096
D = 1024
NT = S // 128
NCH = S // 512
DEPTH = 2
WIN_COLS = 4032
WSPLIT = 1696
EPS = 1e-6
NEGB = -30000.0


class Buf:
    __slots__ = ("name", "w", "rc", "rd", "excl")

    def __init__(self, name, excl=False):
        self.name = name
        self.excl = excl
        self.w = None
        self.rc = {}
        self.rd = []


class Op:
    __slots__ = ("eng", "fn", "deps", "signaled", "token", "dma_sem")

    def __init__(self, eng, fn, dma_sem=None):
        self.eng = eng
        self.fn = fn
        self.deps = []
        self.signaled = False
        self.token = None
        self.dma_sem = dma_sem


class P:
    ENGS = ("pe", "act", "dve", "pool", "sp")

    def __init__(self, nc):
        self.nc = nc
        self.e = {"pe": nc.tensor, "act": nc.scalar, "dve": nc.vector, "pool": nc.gpsimd, "sp": nc.sync}
        self.ops = []
        self.last_real = {}
        self.last_dma = {}
        self.nbuf = 0

    def buf(self, name="b", excl=False):
        self.nbuf += 1
        return Buf(f"{name}{self.nbuf}", excl)

    def op(self, eng, fn, reads=(), writes=(), dma_sem=None, extra_deps=()):
        o = Op(eng, fn, dma_sem)
        deps = {id(d): d for d in extra_deps}
        is_dma = dma_sem is not None

        def add(d, raw):
            if d is None or d is o:
                return
            d_dma = d.dma_sem is not None
            if not is_dma and not d_dma and d.eng == eng:
                if eng == "pe" or not raw:
                    return
            deps[id(d)] = d

        for b in reads:
            add(b.w, True)
            if b.excl:
                for re, r in b.rc.items():
                    if re != eng:
                        add(r, False)
        for b in writes:
            if not (is_dma and b.w is not None and b.w.dma_sem == dma_sem and b.w.eng == eng):
                add(b.w, False)
            for r in b.rc.values():
                add(r, False)
            for r in b.rd:
                add(r, False)
        o.deps = list(deps.values())
        for d in o.deps:
            d.signaled = True
        for b in reads:
            if is_dma:
                b.rd.append(o)
            else:
                b.rc[eng] = o
        for b in writes:
            b.w = o
            b.rc = {}
            b.rd = []
        self.ops.append(o)
        if is_dma:
            self.last_dma[dma_sem] = o
        else:
            self.last_real[eng] = o
        return o

    def barrier(self, skip_sems=()):
        deps = [o for o in self.last_real.values()] + [o for k, o in self.last_dma.items() if k not in skip_sems]
        for d in deps:
            d.signaled = True
        for eng in self.ENGS:
            o = Op(eng, None)
            o.deps = [d for d in deps if not (d.dma_sem is None and d.eng == eng and eng == "pe")]
            self.ops.append(o)

    def emit(self):
        nc = self.nc
        esem = {e: nc.alloc_semaphore(f"es_{e}") for e in self.ENGS}
        dsem = {}
        ecnt = {e: 0 for e in self.ENGS}
        dcnt = {}
        waited = {e: {} for e in self.ENGS}
        nwait = 0
        for o in self.ops:
            eng = self.e[o.eng]
            need = {}
            for d in o.deps:
                sem, val = d.token
                k = id(sem)
                if k not in need or need[k][1] < val:
                    need[k] = (sem, val)
            for k, (sem, val) in need.items():
                if waited[o.eng].get(k, 0) < val:
                    eng.wait_ge(sem, val)
                    waited[o.eng][k] = val
                    nwait += 1
            if o.fn is None:
                continue
            ins = o.fn(eng)
            if o.dma_sem is not None:
                if o.dma_sem not in dsem:
                    dsem[o.dma_sem] = nc.alloc_semaphore(f"ds_{o.dma_sem}")
                    dcnt[o.dma_sem] = 0
                dcnt[o.dma_sem] += 16
                ins.then_inc(dsem[o.dma_sem], 16)
                o.token = (dsem[o.dma_sem], dcnt[o.dma_sem])
            elif o.signaled:
                ecnt[o.eng] += 1
                ins.then_inc(esem[o.eng], 1)
                o.token = (esem[o.eng], ecnt[o.eng])
        return dict(n_ops=len(self.ops), n_wait=nwait, ecnt=ecnt, n_dsem=len(dsem))


class Builder:
    def __init__(self, depth=DEPTH, dbg=False, stop_after=None):
        self.depth = depth
        self.dbg = dbg
        self.stop_after = stop_after
        self.nc = bass.Bass("TRN2", target_bir_lowering=False)
        self.p = P(self.nc)
        self.uid = 0
        self.guards = []
        self.region = None

    def sb(self, shape, dtype, name="t"):
        self.uid += 1
        if self.region is None:
            g = self.nc.sbuf_tensor(f"{name}_{self.uid}", list(shape), dtype)
            t = g.__enter__()
            self.guards.append(g)
            return t
        nbytes = 2 if dtype == BF16 else 4
        for d in shape[1:]:
            nbytes *= d
        nbytes = (nbytes + 63) // 64 * 64
        off = self.region[0]
        assert off + nbytes <= self.region[1], f"arena region overflow: {name} {off}+{nbytes} > {self.region[1]}"
        self.region[0] = off + nbytes
        return self.nc.alloc_sbuf_tensor_at(f"{name}_{self.uid}", list(shape), dtype, offset=off)

    def set_region(self, lo, hi):
        self.region = [lo, hi]

    def mark(self):
        return len(self.guards)

    def release(self, mark):
        while len(self.guards) > mark:
            g = self.guards.pop()
            g.__exit__(None, None, None)

    def dram(self, name, shape, dtype, kind="Internal"):
        if self.dbg and kind == "Internal":
            kind = "ExternalOutput"
        return self.nc.dram_tensor(name, list(shape), dtype, kind=kind).ap()

    def dma(self, q, out, in_, sem, reads=(), writes=(), after=()):
        return self.p.op(q, lambda e: e.dma_start(out=out, in_=in_), reads, writes, dma_sem=sem, extra_deps=after)

    def mm(self, out, lhsT, rhs, start, stop, reads=(), writes=()):
        return self.p.op("pe", lambda e: e.matmul(out, lhsT, rhs, start=start, stop=stop), reads, writes)

    def act(self, out, in_, func, reads=(), writes=(), **kw):
        return self.p.op("act", lambda e: e.activation(out=out, in_=in_, func=func, **kw), reads, writes)

    def copy(self, eng, out, in_, reads=(), writes=()):
        if eng == "act":
            return self.act(out, in_, AF.Copy, reads, writes)
        return self.p.op(eng, lambda e: e.tensor_copy(out=out, in_=in_), reads, writes)

    def tt(self, eng, out, in0, in1, op, reads=(), writes=()):
        return self.p.op(eng, lambda e: e.tensor_tensor(out=out, in0=in0, in1=in1, op=op), reads, writes)

    def ts(self, eng, out, in0, s1, op0, reads=(), writes=()):
        return self.p.op(eng, lambda e: e.tensor_scalar(out=out, in0=in0, scalar1=s1, scalar2=None, op0=op0),
                         reads, writes)

    def recip(self, out, in_, reads=(), writes=()):
        return self.p.op("dve", lambda e: e.reciprocal(out=out, in_=in_), reads, writes)

    def build(self):
        nc, p = self.nc, self.p
        L = self.depth
        ein = lambda n, s: nc.dram_tensor(n, list(s), F32, kind="ExternalInput").ap()
        self.x_in = ein("x", (S, D))
        self.mem_in = ein("mem", (256, D))
        self.w_in = ein("w_in", (L, D, WIN_COLS))
        self.w_uq = ein("w_uq", (L, 384, 2048))
        self.w_ukv = ein("w_ukv", (L, 256, 2048))
        self.w_mem = ein("w_mem", (L, D, 1024))
        self.w_out = ein("w_out", (L, 2048, D))
        self.g_attn = ein("g_attn", (L, 128, D))
        self.g_q = ein("g_q", (L, 128, 3))
        self.g_kv = ein("g_kv", (L, 128, 2))
        self.g_mem = ein("g_mem", (L, 128, D))
        self.sinks = ein("sinks", (L, 128, 8))
        self.g_final = ein("g_final", (128, D))
        self.c_ident = ein("c_ident", (128, 128))
        self.c_tri = ein("c_tri", (128, 128))
        self.c_swab = ein("c_swab", (128, 16 * 128))
        self.c_ropeC = ein("c_ropeC", (128, S))
        self.c_ropeS = ein("c_ropeS", (128, S))
        self.y_out = nc.dram_tensor("y", [S, D], F32, kind="ExternalOutput").ap()
        self.XRES = self.dram("XRES", (S, D), F32)
        self.QN = self.dram("QN", (16 * 64, S), BF16)
        self.QR = self.dram("QR", (16 * 32, S), BF16)
        self.KN = self.dram("KN", (16 * 64, S), BF16)
        self.KPE = self.dram("KPE", (32, S), BF16)
        self.VM = self.dram("VM", (NT, 128, 16, 128), BF16)
        self.QS = self.dram("QS", (8 * 64, S), BF16)
        self.KS = self.dram("KS", (2 * 64, S), BF16)
        self.VS = self.dram("VS", (NT, 128, 4, 128), BF16)
        self.QM = self.dram("QM", (4, 128, S), BF16)
        self.G = self.dram("G", (2048, S), BF16)
        self.YT = self.dram("YT", (16, 128, S), BF16)

        self.ps = []
        for i in range(6):
            t = nc.alloc_psum_tensor(f"ps{i}", [128, 512], F32)
            self.ps.append((t, p.buf("ps", excl=True)))
        self.pst = []
        for i in range(2):
            t = nc.alloc_psum_tensor(f"pst{i}", [128, 1024], BF16)
            self.pst.append((t, p.buf("pst", excl=True)))
        self.ps_rr = 0
        self.pst_rr = 0

        self.setup_consts()
        p.barrier()
        self.KTm = self.sb((128, 4, 256), BF16, "KTm")
        self.Vm = self.sb((128, 2, 512), BF16, "Vm")
        self.memb = p.buf("memkv")
        SB_END = 16512 + 212863
        base = SB_END - nc.sbuf_bytes_remaining
        base = (base + 63) // 64 * 64
        end = SB_END // 64 * 64
        self.set_region(base, end)
        self.w_in_sb = self.sb((128, 8, WIN_COLS), BF16, "w_in")
        self.w_uq_sb = self.sb((128, 3, 2048), BF16, "w_uq")
        self.w_ukv_sb = self.sb((128, 2, 2048), BF16, "w_ukv")
        o_wmem = self.region[0]
        self.w_mem_sb = self.sb((128, 8, 1024), BF16, "w_mem")
        o_wout = self.region[0]
        self.w_out_sb = self.sb((128, 16, D), BF16, "w_out")
        o_r2 = self.region[0]
        self.wA0b = p.buf("wA0")
        self.wA1b = p.buf("wA1")
        self.wuqb = p.buf("wuq")
        self.wukvb = p.buf("wukv")
        self.wmb = p.buf("wmem")
        self.wob = p.buf("wout")
        WSEMS = ("w_mem", "wA0", "wA1", "wuq", "wukv", "w_out")

        def load_WA(l):
            jobs = []

            def add(dst, src, sem, buf):
                jobs.append(lambda after=(): self.dma("pool", dst, src, sem, writes=[buf], after=after))

            for kc in range(8):
                add(self.w_mem_sb[:, kc, :], self.w_mem[l][kc * 128:(kc + 1) * 128, :], "w_mem", self.wmb)
            for kc in range(8):
                add(self.w_in_sb[:, kc, 0:WSPLIT], self.w_in[l][kc * 128:(kc + 1) * 128, 0:WSPLIT], "wA0", self.wA0b)
            for kc in range(8):
                add(self.w_in_sb[:, kc, WSPLIT:WIN_COLS], self.w_in[l][kc * 128:(kc + 1) * 128, WSPLIT:WIN_COLS], "wA1",
                    self.wA1b)
            for kc in range(3):
                add(self.w_uq_sb[:, kc, :], self.w_uq[l][kc * 128:(kc + 1) * 128, :], "wuq", self.wuqb)
            for kc in range(2):
                add(self.w_ukv_sb[:, kc, :], self.w_ukv[l][kc * 128:(kc + 1) * 128, :], "wukv", self.wukvb)
            return jobs

        for j in load_WA(0):
            j()
        for l in range(L):
            last = (l == L - 1)
            x_src = self.x_in if l == 0 else self.XRES
            self.set_region(o_wout, end)
            self.phase_M(l)
            p.barrier(skip_sems=WSEMS)
            self.set_region(o_wmem, end)
            self.phase_A(l, x_src)
            p.barrier(skip_sems=WSEMS)
            if self.stop_after == ("A", l):
                break
            self.set_region(base, o_wout)
            self.phase_B(l)
            p.barrier(skip_sems=WSEMS)
            if self.stop_after == ("B", l):
                break
            self.set_region(o_r2, end)
            self.phase_C(l, x_src, last, [] if last else load_WA(l + 1))
            p.barrier(skip_sems=WSEMS)
        p.barrier()
        self.stats = p.emit()
        return nc

    def next_ps(self):
        r = self.ps[self.ps_rr % len(self.ps)]
        self.ps_rr += 1
        return r

    def next_pst(self):
        r = self.pst[self.pst_rr % 2]
        self.pst_rr += 1
        return r

    def setup_consts(self):
        p = self.p
        self.epsb = self.sb((128, 1), F32, "eps")
        self.ident = self.sb((128, 128), BF16, "ident")
        self.tri = self.sb((128, 128), BF16, "tri")
        self.swab = self.sb((128, 16 * 128), BF16, "swab")
        self.ones = self.sb((128, 128), BF16, "ones")
        self.cb = p.buf("consts")
        p.op("dve", lambda e: e.memset(self.epsb[:], EPS), writes=[self.cb])
        self.gfin = self.sb((128, D), F32, "gfin")
        L = self.depth
        self.gq = self.sb((128, L, 3), F32, "gq")
        self.gkv = self.sb((128, L, 2), F32, "gkv")
        self.esink = self.sb((128, L, 8), F32, "esink")
        self.vbuf = [(self.sb((128, 16, 128), BF16, "vbuf"), p.buf("vbuf")) for _ in range(4)]
        self.vsbuf = [(self.sb((128, 4, 128), BF16, "vsbuf"), p.buf("vsbuf")) for _ in range(2)]
        mtmp = self.mark()
        tmp = self.sb((128, 2048), F32, "ctmp")
        tb = p.buf("ctmp")
        self.dma("sp", tmp[:, 0:128], self.c_ident[:, :], "c0", writes=[tb])
        self.copy("dve", self.ident[:], tmp[:, 0:128], reads=[tb], writes=[self.cb])
        self.dma("sp", tmp[:, 0:128], self.c_tri[:, :], "c0", reads=[], writes=[tb])
        self.copy("dve", self.tri[:], tmp[:, 0:128], reads=[tb], writes=[self.cb])
        self.dma("sp", tmp[:, :], self.c_swab[:, :], "c0", writes=[tb])
        self.copy("dve", self.swab[:], tmp[:, :], reads=[tb], writes=[self.cb])
        p.op("dve", lambda e: e.memset(self.ones[:], 1.0), writes=[self.cb])
        self.dma("sp", self.gfin[:], self.g_final[:, :], "c1", writes=[self.cb])
        for l in range(L):
            self.dma("sp", self.gq[:, l, :], self.g_q[l], "c1", writes=[self.cb])
            self.dma("sp", self.gkv[:, l, :], self.g_kv[l], "c1", writes=[self.cb])
            self.dma("sp", self.esink[:, l, :], self.sinks[l], "c1", writes=[self.cb])
        p.barrier()
        for l in range(L):
            self.act(self.esink[:, l, :], self.esink[:, l, :], AF.Exp, reads=[self.cb], writes=[self.cb])
        for (t, b) in self.vbuf + self.vsbuf:
            p.op("pool", lambda e, t=t: e.memset(t[:], 1.0), writes=[b])
        p.barrier()
        self.release(mtmp)

    def load_weight(self, dst, src, KC, N, dst_buf, sem):
        for kc in range(KC):
            for c0 in range(0, N, 2048):
                n = min(2048, N - c0)
                self.dma("pool", dst[:, kc, c0:c0 + n], src[kc * 128:(kc + 1) * 128, c0:c0 + n], sem,
                         writes=[dst_buf])

    def stt(self, eng, out, in0, scalar, in1, op0, op1, reads=(), writes=()):
        return self.p.op(eng, lambda e: e.scalar_tensor_tensor(out=out, in0=in0, scalar=scalar, in1=in1, op0=op0,
                                                               op1=op1), reads, writes)

    def tok_prep(self, src_rows, xt, xb, sem, hq, hb, junk, jb, st, stb, gt, gb):
        if src_rows is not None:
            self.dma("sp", xt[:], src_rows, sem, writes=[xb])
        self.act(junk[:], xt[:], AF.Square, reads=[xb], writes=[jb, stb], accum_out=st[:, 0:1])
        self.act(st[:, 1:2], st[:, 0:1], AF.Sqrt, reads=[stb, self.cb], writes=[stb], scale=1.0 / D, bias=self.epsb[:, 0:1])
        self.recip(st[:, 2:3], st[:, 1:2], reads=[stb], writes=[stb])
        self.stt("dve", hq[:], xt[:], st[:, 2:3], gt[:], ALU.mult, ALU.mult, reads=[xb, stb, gb], writes=[hb])

    def transpose_to(self, h, hb, dstT, dst_buf, col0, evac_eng):
        pt, ptb = self.next_pst()
        for kc in range(8):
            self.p.op("pe", lambda e, kc=kc: e.transpose(pt[:, kc * 128:(kc + 1) * 128], h[:, kc * 128:(kc + 1) * 128],
                                                         self.ident[:]),
                      reads=[hb, self.cb], writes=[ptb])
        self.copy(evac_eng, dstT[:, :, col0:col0 + 128], pt[:].rearrange("p (k t) -> p k t", k=8), reads=[ptb],
                  writes=[dst_buf])

    def phase_M(self, l):
        p = self.p
        w_mem = self.w_mem_sb
        wmb = self.wmb
        gt = self.sb((128, D), F32, "gtm")
        gb = p.buf("gtm")
        self.dma("sp", gt[:], self.g_mem[l], "gt", writes=[gb])
        xts = [(self.sb((128, D), F32, "xt"), p.buf("xt")) for _ in range(2)]
        hs = [(self.sb((128, D), BF16, "h"), p.buf("h")) for _ in range(2)]
        junk = self.sb((128, D), BF16, "junk")
        jb = p.buf("junk")
        sts = [(self.sb((128, 4), F32, "st"), p.buf("st")) for _ in range(2)]
        memT = self.sb((128, 8, 256), BF16, "memT")
        memTb = p.buf("memT")
        for mt in range(2):
            xt, xb = xts[mt]
            hq, hb = hs[mt]
            st, stb = sts[mt]
            self.tok_prep(self.mem_in[mt * 128:(mt + 1) * 128, :], xt, xb, f"xt{mt}", hq, hb, junk, jb, st, stb,
                          gt, gb)
            self.transpose_to(hq, hb, memT, memTb, mt * 128, "dve")
        for hm in range(4):
            pt, pb = self.next_ps()
            for kc in range(8):
                self.mm(pt[:, 0:256], w_mem[:, kc, hm * 128:(hm + 1) * 128], memT[:, kc, :], kc == 0, kc == 7,
                        reads=[wmb, memTb], writes=[pb])
            self.copy("act", self.KTm[:, hm, :], pt[:, 0:256], reads=[pb], writes=[self.memb])
        for mt in range(2):
            pt, pb = self.next_ps()
            for kc in range(8):
                self.mm(pt[:, :], memT[:, kc, mt * 128:(mt + 1) * 128], w_mem[:, kc, 512:1024], kc == 0, kc == 7,
                        reads=[wmb, memTb], writes=[pb])
            self.copy("dve", self.Vm[:, mt, :], pt[:, :], reads=[pb], writes=[self.memb])

    def phase_A(self, l, x_src):
        p = self.p
        epsbuf = self.cb
        w_in, w_uq, w_ukv = self.w_in_sb, self.w_uq_sb, self.w_ukv_sb
        wbuf_of = {id(w_uq): [self.wuqb], id(w_ukv): [self.wukvb]}
        gt = self.sb((128, D), F32, "gta")
        gb = p.buf("gta")
        self.dma("sp", gt[:], self.g_attn[l], "gt", writes=[gb])
        xts = [(self.sb((128, D), F32, "xt"), p.buf("xt")) for _ in range(4)]
        hs = [(self.sb((128, D), BF16, "h"), p.buf("h")) for _ in range(4)]
        junk = self.sb((128, D), BF16, "junk")
        jb = p.buf("junk")
        sts = [(self.sb((128, 4), F32, "st"), p.buf("st")) for _ in range(4)]

        hT = [(self.sb((128, 8, 512), BF16, "hT"), p.buf("hT")) for _ in range(1)]
        cqn = self.sb((128, 3, 512), BF16, "cqn")
        cqnb = p.buf("cqn")
        ckvn = self.sb((128, 2, 512), BF16, "ckvn")
        ckvnb = p.buf("ckvn")
        sq = [(self.sb((128, 512), BF16, "sq"), p.buf("sq")) for _ in range(5)]
        csb = [(self.sb((128, 512), F32, "csb"), p.buf("csb")) for _ in range(5)]
        hT_cur = [None, None]
        rs = [(self.sb((128, 512), F32, "rs"), p.buf("rs")) for _ in range(2)]
        tabs = [(self.sb((128, 2, 512), F32, "tab"), p.buf("tab")) for _ in range(2)]
        rt = [(self.sb((128, 2, 512), F32, "rt"), p.buf("rt")) for _ in range(1)]
        stg = [(self.sb((128, 512), BF16, "stg"), p.buf("stg")) for _ in range(12)]
        self.stg_rr = 0

        def next_stg():
            i = self.stg_rr % len(stg)
            self.stg_rr += 1
            return stg[i][0], stg[i][1], f"stg{i}"

        def queue_of(sem):
            return "pool" if (sem.startswith("stg") and int(sem[3:]) % 3 == 2) else "sp"

        def prep_load(c):
            for tt in range(4):
                T = c * 4 + tt
                xt, xb = xts[tt]
                self.dma("sp", xt[:], x_src[T * 128:(T + 1) * 128, :], f"xt{tt}", writes=[xb])

        def prep(c):
            for tt in range(4):
                i = (c * 4 + tt)
                xt, xb = xts[tt]
                hq, hb = hs[i % 4]
                st, stb = sts[i % 4]
                self.tok_prep(None, xt, xb, None, hq, hb, junk, jb, st, stb, gt, gb)
            tb_t, tb_b = tabs[c % 2]
            self.dma("sp", tb_t[:, 0, :], self.c_ropeC[:, c * 512:(c + 1) * 512], f"tab{c % 2}", writes=[tb_b])
            self.dma("sp", tb_t[:, 1, :], self.c_ropeS[:, c * 512:(c + 1) * 512], f"tab{c % 2}", writes=[tb_b])

        def proj(pt, pb, w, wcols, KC, rhsT, rb, M):
            if id(w) in wbuf_of:
                wbs = wbuf_of[id(w)]
            else:
                wbs = [self.wA0b] if wcols[1] <= WSPLIT else ([self.wA1b] if wcols[0] >= WSPLIT else
                                                             [self.wA0b, self.wA1b])
            for kc in range(KC):
                self.mm(pt[0:M, :], w[:, kc, wcols[0]:wcols[1]], rhsT[:, kc, :], kc == 0, kc == KC - 1,
                        reads=wbs + [rb], writes=[pb])

        def store(src_ap, dst_ap, sbuf_, sem):
            self.dma(queue_of(sem), dst_ap, src_ap, sem, reads=[sbuf_])

        ev = [0]

        def evac_copy(pt, pb, M=128):
            s_t, s_b, sem = next_stg()
            eng = "act" if ev[0] % 2 == 0 else "dve"
            ev[0] += 1
            self.copy(eng, s_t[0:M, :], pt[0:M, :], reads=[pb], writes=[s_b])
            return s_t, s_b, sem

        def latent_start(cols, k0):
            items = []
            for j, c0 in enumerate(cols):
                pt, pb = self.next_ps()
                proj(pt, pb, w_in, (c0, c0 + 128), 8, hT_cur[0], hT_cur[1], 128)
                s_t, s_b = sq[k0 + j]
                c_t, c_b = csb[k0 + j]
                self.act(s_t[:], pt[:], AF.Square, reads=[pb], writes=[s_b])
                self.copy("dve", c_t[:], pt[:], reads=[pb], writes=[c_b])
                items.append((s_t, s_b, c_t, c_b))
            return items

        def latent_finish(items, nfeat, gain, dst, dstb, k):
            pss, pssb = self.next_ps()
            n = len(items)
            for j, (s_t, s_b, c_t, c_b) in enumerate(items):
                self.mm(pss[:, :], self.ones[:], s_t[:], j == 0, j == n - 1, reads=[s_b, self.cb], writes=[pssb])
            r_t, r_b = rs[k]
            self.act(r_t[:], pss[:], AF.Sqrt, reads=[pssb, epsbuf], writes=[r_b], scale=1.0 / nfeat,
                     bias=self.epsb[:, 0:1])
            self.recip(r_t[:], r_t[:], reads=[r_b], writes=[r_b])
            for j, (s_t, s_b, c_t, c_b) in enumerate(items):
                self.stt("dve", dst[:, j, :], c_t[:], gain[:, j:j + 1], r_t[:], ALU.mult, ALU.mult,
                         reads=[c_b, r_b, self.cb], writes=[dstb])

        def rope(ptA, pbA, ptB, pbB, M, tab, tabb, c):
            r_t, r_b = rt[0]
            self.tt("dve", r_t[0:M, 0, :], ptA[0:M, :], tab[0:M, 0, :], ALU.mult, reads=[pbA, tabb], writes=[r_b])
            self.tt("dve", r_t[0:M, 1, :], ptB[0:M, :], tab[0:M, 1, :], ALU.mult, reads=[pbB, tabb], writes=[r_b])
            s_t, s_b, sem = next_stg()
            self.tt("pool", s_t[0:M, :], r_t[0:M, 0, :], r_t[0:M, 1, :], ALU.add, reads=[r_b], writes=[s_b])
            return s_t, s_b, sem

        prep_load(0)
        prep(0)
        for c in range(NCH):
            tk = slice(c * 512, (c + 1) * 512)
            hT_t, hT_b = hT[0]
            tab_t, tab_b = tabs[c % 2]
            for tt in range(4):
                i = c * 4 + tt
                hq, hb = hs[i % 4]
                self.transpose_to(hq, hb, hT_t, hT_b, tt * 128, "dve" if tt % 2 == 0 else "act")
            if c + 1 < NCH:
                prep_load(c + 1)
            hT_cur[0], hT_cur[1] = hT_t, hT_b
            it_q = latent_start([0, 128, 256], 0)
            it_kv = latent_start([384, 512], 3)
            ptA, pbA = self.next_ps()
            proj(ptA, pbA, w_in, (640, 672), 8, hT_t, hT_b, 32)
            ptB, pbB = self.next_ps()
            proj(ptB, pbB, w_in, (4000, 4032), 8, hT_t, hT_b, 32)
            s_t, s_b, sem = rope(ptA, pbA, ptB, pbB, 32, tab_t, tab_b, c)
            store(s_t[0:32, :], self.KPE[:, tk], s_b, sem)
            zcols = [672 + m * 128 for m in range(8)] + [2464 + m * 128 for m in range(4)] + \
                    [3488 + m * 128 for m in range(4)]
            for m, c0 in enumerate(zcols):
                pt, pb = self.next_ps()
                proj(pt, pb, w_in, (c0, c0 + 128), 8, hT_t, hT_b, 128)
                s_t, s_b, sem = next_stg()
                self.act(s_t[:], pt[:], AF.Silu, reads=[pb], writes=[s_b])
                store(s_t[:], self.G[m * 128:(m + 1) * 128, tk], s_b, sem)
                if m == 1:
                    latent_finish(it_q, 384, self.gq[:, l, :], cqn, cqnb, 0)
                    latent_finish(it_kv, 256, self.gkv[:, l, :], ckvn, ckvnb, 1)
            if c + 1 < NCH:
                prep(c + 1)
            for m in range(4):
                pt, pb = self.next_ps()
                c0 = 1696 + m * 128
                proj(pt, pb, w_in, (c0, c0 + 128), 8, hT_t, hT_b, 128)
                s_t, s_b, sem = evac_copy(pt, pb)
                store(s_t[:], self.QS[m * 128:(m + 1) * 128, tk], s_b, sem)
            pt, pb = self.next_ps()
            proj(pt, pb, w_in, (2208, 2336), 8, hT_t, hT_b, 128)
            s_t, s_b, sem = evac_copy(pt, pb)
            store(s_t[:], self.KS[:, tk], s_b, sem)
            for m in range(4):
                pt, pb = self.next_ps()
                c0 = 2976 + m * 128
                proj(pt, pb, w_in, (c0, c0 + 128), 8, hT_t, hT_b, 128)
                s_t, s_b, sem = evac_copy(pt, pb)
                store(s_t[:], self.QM[m, :, tk], s_b, sem)
            for tt in range(4):
                T = c * 4 + tt
                pt, pb = self.next_ps()
                for kc in range(8):
                    self.mm(pt[:, 0:128], hT_t[:, kc, tt * 128:(tt + 1) * 128], w_in[:, kc, 2336:2464], kc == 0,
                            kc == 7, reads=[self.wA1b, hT_b], writes=[pb])
                vs_t, vs_b = self.vsbuf[T % 2]
                pv = pt[:, 0:128].rearrange("p (k d) -> p k d", k=2)
                v4 = vs_t[:].rearrange("p (k v) c -> p k v c", k=2)
                ee = "dve" if tt % 2 == 0 else "act"
                self.copy(ee, v4[:, :, 0, 0:64], pv, reads=[pb], writes=[vs_b])
                self.copy(ee, v4[:, :, 1, 64:128], pv, reads=[pb], writes=[vs_b])
                store(vs_t[:], self.VS[T], vs_b, f"vsb{T % 2}")
            for tt in range(4):
                T = c * 4 + tt
                vb_t, vb_b = self.vbuf[T % 4]
                for half in range(2):
                    pt, pb = self.next_ps()
                    for kc in range(2):
                        self.mm(pt[:, :], ckvn[:, kc, tt * 128:(tt + 1) * 128],
                                w_ukv[:, kc, 1024 + half * 512:1024 + (half + 1) * 512], kc == 0, kc == 1,
                                reads=[self.wukvb, ckvnb], writes=[pb])
                    pv = pt[:, :].rearrange("p (h two d) -> p h two d", two=2, d=64)
                    vv = vb_t[:, half * 8:(half + 1) * 8, :].rearrange("p (h two) c -> p h two c", two=2)
                    ee = "dve" if half == 0 else "act"
                    self.copy(ee, vv[:, :, 0, 0:64], pv[:, :, 0, :], reads=[pb], writes=[vb_b])
                    self.copy(ee, vv[:, :, 1, 64:128], pv[:, :, 1, :], reads=[pb], writes=[vb_b])
                store(vb_t[:], self.VM[T], vb_b, f"vb{T % 4}")
            for m in range(8):
                pt, pb = self.next_ps()
                proj(pt, pb, w_uq, (m * 128, (m + 1) * 128), 3, cqn, cqnb, 128)
                s_t, s_b, sem = evac_copy(pt, pb)
                store(s_t[:], self.QN[m * 128:(m + 1) * 128, tk], s_b, sem)
            for m in range(4):
                ptA, pbA = self.next_ps()
                proj(ptA, pbA, w_uq, (1024 + m * 128, 1024 + (m + 1) * 128), 3, cqn, cqnb, 128)
                ptB, pbB = self.next_ps()
                proj(ptB, pbB, w_uq, (1536 + m * 128, 1536 + (m + 1) * 128), 3, cqn, cqnb, 128)
                s_t, s_b, sem = rope(ptA, pbA, ptB, pbB, 128, tab_t, tab_b, c)
                store(s_t[:], self.QR[m * 128:(m + 1) * 128, tk], s_b, sem)
            for m in range(8):
                pt, pb = self.next_ps()
                proj(pt, pb, w_ukv, (m * 128, (m + 1) * 128), 2, ckvn, ckvnb, 128)
                s_t, s_b, sem = evac_copy(pt, pb)
                store(s_t[:], self.KN[m * 128:(m + 1) * 128, tk], s_b, sem)
    def phase_B(self, l):
        p = self.p
        QTb = [(self.sb((128, S), BF16, "QTb"), p.buf("QTb")) for _ in range(2)]
        KTb = [(self.sb((128, S), BF16, "KTb"), p.buf("KTb")) for _ in range(2)]
        Vb = [(self.sb((128, NT, 128), BF16, "Vb"), p.buf("Vb")) for _ in range(2)]
        Vq = [[p.buf("Vq") for _ in range(4)] for _ in range(2)]
        Gb = [(self.sb((128, S), BF16, "Gb"), p.buf("Gb")) for _ in range(2)]
        PT = [(self.sb((128, 512), BF16, "PT"), p.buf("PT")) for _ in range(6)]
        Rb = [(self.sb((128, 512), F32, "R"), p.buf("R")) for _ in range(3)]
        Y1 = [(self.sb((128, 512), F32, "Y1"), p.buf("Y1")) for _ in range(3)]
        yb = [(self.sb((128, 512), BF16, "yb"), p.buf("yb")) for _ in range(3)]
        ST = self.ps[0:3]
        OT = self.ps[3:5]
        SM = self.ps[5]
        STM = self.ps[0:4]
        OTM = self.ps[4:6]
        cnt = dict(ep=0, yb=0, u=0)
        LA = 2

        def run_pipeline(units, LA=2):
            n = len(units)
            base = cnt["u"]
            cnt["u"] += n
            for i in range(min(LA, n)):
                units[i]["qk"](base + i)
            pending = None
            for i in range(n):
                if i + LA < n:
                    units[i + LA]["qk"](base + i + LA)
                units[i]["ex"](base + i)
                if pending is not None:
                    pending()
                    pending = None
                units[i]["pv"](base + i)
                if "post" in units[i]:
                    pending = units[i]["post"]
            if pending is not None:
                pending()

        for i in range(2):
            self.dma("sp", KTb[i][0][0:32, :], self.KPE[:, :], f"KTb{i}", writes=[KTb[i][1]])

        def epilogue(o_t, o_b, s_t, s_b, o_sl, s_sl, g_t, g_b, c, dst_ap, sink_ap=None, act_recip=False):
            i = cnt["ep"]
            cnt["ep"] += 1
            r_t, r_b = Rb[i % 3]
            y1_t, y1_b = Y1[i % 3]
            j = cnt["yb"]
            cnt["yb"] += 1
            y_t, y_b = yb[j % 3]
            if act_recip and i % 4 != 3:
                kw = dict(bias=sink_ap) if sink_ap is not None else {}
                self.act(r_t[o_sl, :], s_t[s_sl, :], AF.Ln, reads=[s_b, self.cb], writes=[r_b], **kw)
                self.act(r_t[o_sl, :], r_t[o_sl, :], AF.Exp, reads=[r_b], writes=[r_b], scale=-1.0)
            else:
                self.tt("dve", y1_t[o_sl, :], o_t[o_sl, :], g_t[o_sl, c * 512:(c + 1) * 512], ALU.mult,
                        reads=[o_b, g_b], writes=[y1_b])
                if sink_ap is not None:
                    self.ts("dve", r_t[o_sl, :], s_t[s_sl, :], sink_ap, ALU.add, reads=[s_b, self.cb], writes=[r_b])
                else:
                    self.copy("dve", r_t[o_sl, :], s_t[s_sl, :], reads=[s_b], writes=[r_b])
                self.recip(r_t[o_sl, :], r_t[o_sl, :], reads=[r_b], writes=[r_b])
                self.tt("pool", y_t[o_sl, :], y1_t[o_sl, :], r_t[o_sl, :], ALU.mult, reads=[y1_b, r_b], writes=[y_b])
                self.dma("sp", dst_ap, y_t[o_sl, :], f"yb{j % 3}", reads=[y_b])
                return
            self.tt("dve", y1_t[o_sl, :], o_t[o_sl, :], r_t[o_sl, :], ALU.mult, reads=[o_b, r_b], writes=[y1_b])
            self.tt("pool", y_t[o_sl, :], y1_t[o_sl, :], g_t[o_sl, c * 512:(c + 1) * 512], ALU.mult,
                    reads=[y1_b, g_b], writes=[y_b])
            self.dma("sp", dst_ap, y_t[o_sl, :], f"yb{j % 3}", reads=[y_b])

        sc_mla = float(96 ** -0.5)

        def load_mla(h):
            i = h % 2
            self.dma("sp", QTb[i][0][0:32, :], self.QR[h * 32:(h + 1) * 32, :], f"QTb{i}", writes=[QTb[i][1]])
            self.dma("sp", QTb[i][0][32:96, :], self.QN[h * 64:(h + 1) * 64, :], f"QTb{i}", writes=[QTb[i][1]])
            self.dma("sp", KTb[i][0][32:96, :], self.KN[h * 64:(h + 1) * 64, :], f"KTb{i}", writes=[KTb[i][1]])
            par = h % 2
            vm = self.VM.rearrange("t p h c -> p t h c")
            for qd in range(4):
                self.dma("sp", Vb[i][0][:, qd * 8:(qd + 1) * 8, :], vm[:, qd * 8:(qd + 1) * 8, h, :], f"Vq{i}_{qd}",
                         writes=[Vq[i][qd]], reads=[Vb[i][1]])
                if qd == 0:
                    self.dma("sp", Gb[i][0][par * 64:(par + 1) * 64, :], self.G[h * 64:(h + 1) * 64, :], f"Gb{i}",
                             writes=[Gb[i][1]])

        def mla_unit(h, c, t):
            i2 = h % 2
            q_t, q_b = QTb[i2]
            k_t, k_b = KTb[i2]
            v_t, v_b = Vb[i2]
            g_t, g_b = Gb[i2]
            par = h % 2
            o_sl = slice(par * 64, (par + 1) * 64)
            s_sl = slice((1 - par) * 64, (2 - par) * 64)
            j = t - 4 * c
            lo = max(j, 0) * 128
            o_t, o_b = OTM[c % 2]

            def qk(i):
                st_t, st_b = STM[i % 4]
                self.mm(st_t[:, lo:512], k_t[0:96, t * 128:(t + 1) * 128], q_t[0:96, c * 512 + lo:(c + 1) * 512], True,
                        j < 0, reads=[k_b, q_b], writes=[st_b])
                if j >= 0:
                    self.mm(st_t[:, lo:lo + 128], self.ident[:], self.tri[:], False, True, reads=[self.cb],
                            writes=[st_b])

            def ex(i):
                st_t, st_b = STM[i % 4]
                pt_t, pt_b = PT[i % 6]
                self.act(pt_t[:, lo:512], st_t[:, lo:512], AF.Exp, reads=[st_b], writes=[pt_b], scale=sc_mla)

            def pv(i):
                pt_t, pt_b = PT[i % 6]
                self.mm(o_t[:, lo:512], v_t[:, t, :], pt_t[:, lo:512], t == 0, t == 4 * c + 3,
                        reads=[Vq[i2][t // 8], pt_b], writes=[o_b])

            u = dict(qk=qk, ex=ex, pv=pv)
            if t == 4 * c + 3:
                def post():
                    epilogue(o_t, o_b, o_t, o_b, o_sl, s_sl, g_t, g_b, c,
                             self.YT[h // 2, par * 64:(par + 1) * 64, c * 512:(c + 1) * 512])
                    if c == NCH - 1 and h + 2 < 16:
                        load_mla(h + 2)
                    if c == NCH - 1 and h == 0:
                        self.load_weight(self.w_out_sb, self.w_out[l], 16, D, self.wob, "w_out")
                    if c == NCH - 1 and h == 14:
                        swa_preload(0)
                u["post"] = post
            return u

        def swa_preload(i):
            p.op("pool", lambda e, t=QTb[i][0]: e.memset(t[64:128, :], 0.0), writes=[QTb[i][1]])
            p.op("pool", lambda e, t=KTb[i][0]: e.memset(t[64:128, :], 0.0), writes=[KTb[i][1]])
            self.dma("sp", KTb[i][0][0:64, :], self.KS[i * 64:(i + 1) * 64, :], f"KTb{i}", writes=[KTb[i][1]])
            load_swa_v(0, i)
            load_swa_q(i)

        def load_swa_v(kvh, var):
            self.dma("sp", Vb[var][0][:], self.VS.rearrange("t p k c -> p t k c")[:, :, kvh * 2 + var, :], f"Vb{var}",
                     writes=[Vb[var][1]] + Vq[var])

        def load_swa_q(h):
            par = h % 2
            self.dma("sp", QTb[h % 2][0][0:64, :], self.QS[h * 64:(h + 1) * 64, :], f"QTb{h % 2}", writes=[QTb[h % 2][1]])
            self.dma("sp", Gb[h % 2][0][par * 64:(par + 1) * 64, :], self.G[1024 + h * 64:1024 + (h + 1) * 64, :],
                     f"Gb{h % 2}", writes=[Gb[h % 2][1]])

        def load_mem(hm):
            self.dma("sp", QTb[hm % 2][0][:, :], self.QM[hm, :, :], f"QTb{hm % 2}", writes=[QTb[hm % 2][1]])
            self.dma("sp", Gb[hm % 2][0][:, :], self.G[1536 + hm * 128:1536 + (hm + 1) * 128, :], f"Gb{hm % 2}",
                     writes=[Gb[hm % 2][1]])

        load_mla(0)
        load_mla(1)
        units = [mla_unit(h, c, t) for h in range(16) for c in range(NCH) for t in range(4 * c + 4)]
        run_pipeline(units, LA=3)
        swa_preload(1)

        sc_swa = 0.125

        def swa_unit(h, c, half):
            kvh = h // 4
            par = h % 2
            ks_t, ks_b = KTb[kvh]
            q_t, q_b = QTb[h % 2]
            g_t, g_b = Gb[h % 2]
            v_t, v_b = Vb[par]
            o_sl = slice(par * 64, (par + 1) * 64)
            s_sl = slice((1 - par) * 64, (2 - par) * 64)
            o_t, o_b = OTM[c % 2]
            slots = []
            for nn in range(2):
                n = c * 4 + half * 2 + nn
                for typ in range(2):
                    t = n - 1 + typ
                    if t >= 0:
                        slots.append((nn, n, typ, t, (nn * 2 + typ) * 128))
            lo = min(s[4] for s in slots)

            def qk(i):
                st_t, st_b = STM[i % 4]
                for (nn, n, typ, t, col) in slots:
                    self.mm(st_t[:, col:col + 128], ks_t[:, t * 128:(t + 1) * 128], q_t[:, n * 128:(n + 1) * 128], True,
                            False, reads=[ks_b, q_b], writes=[st_b])
                    self.mm(st_t[:, col:col + 128], self.ident[:],
                            self.swab[:, (h * 2 + typ) * 128:(h * 2 + typ + 1) * 128], False, True,
                            reads=[self.cb], writes=[st_b])

            def ex(i):
                st_t, st_b = STM[i % 4]
                pt_t, pt_b = PT[i % 6]
                self.act(pt_t[:, lo:512], st_t[:, lo:512], AF.Exp, reads=[st_b], writes=[pt_b], scale=sc_swa)

            def pv(i):
                pt_t, pt_b = PT[i % 6]
                first_n = {}
                for (nn, n, typ, t, col) in slots:
                    oc = (half * 2 + nn) * 128
                    self.mm(o_t[:, oc:oc + 128], v_t[:, t, :], pt_t[:, col:col + 128], n not in first_n, typ == 1,
                            reads=[v_b, pt_b], writes=[o_b])
                    first_n[n] = True

            u = dict(qk=qk, ex=ex, pv=pv)
            if half == 1:
                def post():
                    epilogue(o_t, o_b, o_t, o_b, o_sl, s_sl, g_t, g_b, c,
                             self.YT[8 + h // 2, par * 64:(par + 1) * 64, c * 512:(c + 1) * 512],
                             sink_ap=self.esink[s_sl, l, h:h + 1], act_recip=True)
                    if c == NCH - 1:
                        if h + 2 < 8:
                            load_swa_q(h + 2)
                        if h == 2:
                            load_swa_v(1, 0)
                        if h == 3:
                            load_swa_v(1, 1)
                        if h >= 6:
                            load_mem(h - 6)
                u["post"] = post
            return u

        units = [swa_unit(h, c, half) for h in range(8) for c in range(NCH) for half in range(2)]
        run_pipeline(units, LA=3)

        sc_mem = float(128 ** -0.5)
        def mem_unit(hm, c, mt):
            q_t, q_b = QTb[hm % 2]
            q2 = q_t
            g_t, g_b = Gb[hm % 2]
            o_t, o_b = OT[c % 2]
            s_t, s_b = SM

            def qk(i):
                st_t, st_b = ST[i % 3]
                self.mm(st_t[:, :], self.KTm[:, hm, mt * 128:(mt + 1) * 128], q2[:, c * 512:(c + 1) * 512], True,
                        True, reads=[self.memb, q_b], writes=[st_b])

            def ex(i):
                st_t, st_b = ST[i % 3]
                pt_t, pt_b = PT[i % 4]
                self.act(pt_t[:, :], st_t[:, :], AF.Exp, reads=[st_b], writes=[pt_b], scale=sc_mem)

            def pv(i):
                pt_t, pt_b = PT[i % 4]
                self.mm(o_t[:, :], self.Vm[:, mt, hm * 128:(hm + 1) * 128], pt_t[:, :], mt == 0, mt == 1,
                        reads=[self.memb, pt_b], writes=[o_b])
                self.mm(s_t[:, :], self.ones[:], pt_t[:, :], mt == 0, mt == 1, reads=[self.cb, pt_b],
                        writes=[s_b])

            u = dict(qk=qk, ex=ex, pv=pv)
            if mt == 1:
                def post():
                    sl = slice(0, 128)
                    epilogue(o_t, o_b, s_t, s_b, sl, sl, g_t, g_b, c, self.YT[12 + hm, :, c * 512:(c + 1) * 512],
                             act_recip=True)
                    if c == NCH - 1 and hm + 2 < 4:
                        load_mem(hm + 2)
                u["post"] = post
            return u

        units = [mem_unit(hm, c, mt) for hm in range(4) for c in range(NCH) for mt in range(2)]
        run_pipeline(units)

    def phase_C(self, l, x_src, last, prefetch):
        p = self.p
        w_out, wb = self.w_out_sb, self.wob
        CW = 256
        yT = [(self.sb((128, 16, CW), BF16, "yT"), p.buf("yT")) for _ in range(2)]
        xc = [(self.sb((128, D), F32, "xc"), p.buf("xc")) for _ in range(2)]
        xn = [(self.sb((128, D), F32, "xn"), p.buf("xn")) for _ in range(2)]
        junk = self.sb((128, D), BF16, "junkc")
        jb = p.buf("junkc")
        sts = [(self.sb((128, 4), F32, "stc"), p.buf("stc")) for _ in range(4)]
        p.op("dve", lambda e: e.memset(junk[:], 0.0), writes=[jb])
        nchunk = S // CW

        def load_chunk(c):
            t, b = yT[c % 2]
            for half in range(2):
                self.dma("sp", t[:, half * 8:(half + 1) * 8, :],
                         self.YT.rearrange("k p s -> p k s")[:, half * 8:(half + 1) * 8, c * CW:(c + 1) * CW],
                         f"yT{c % 2}", writes=[b])

        def load_x(T):
            x_t, x_b = xc[T % 2]
            self.dma("sp", x_t[:], x_src[T * 128:(T + 1) * 128, :], f"xc{T % 2}", writes=[x_b])

        load_chunk(0)
        load_x(0)
        for c in range(nchunk):
            if c + 1 < nchunk:
                load_chunk(c + 1)
            y_t, y_b = yT[c % 2]
            for tt in range(CW // 128):
                T = c * (CW // 128) + tt
                x_t, x_b = xc[T % 2]
                n_t, n_b = xn[T % 2]
                if T + 1 < NT:
                    load_x(T + 1)
                for half in range(2):
                    pt, pb = self.next_ps()
                    for kc in range(16):
                        self.mm(pt[:, :], y_t[:, kc, tt * 128:(tt + 1) * 128], w_out[:, kc, half * 512:(half + 1) * 512],
                                kc == 0, kc == 15, reads=[y_b, wb], writes=[pb])
                    mark = self.tt("dve", n_t[:, half * 512:(half + 1) * 512], pt[:, :],
                                   x_t[:, half * 512:(half + 1) * 512], ALU.add, reads=[pb, x_b], writes=[n_b])
                if prefetch:
                    prefetch.pop(0)(after=(mark,))
                if not last:
                    self.dma("sp", self.XRES[T * 128:(T + 1) * 128, :], n_t[:], f"xn{T % 2}", reads=[n_b])
                else:
                    st, stb = sts[T % 4]
                    self.act(junk[:], n_t[:], AF.Square, reads=[n_b], writes=[jb, stb], accum_out=st[:, 0:1])
                    self.act(st[:, 1:2], st[:, 0:1], AF.Sqrt, reads=[stb, self.cb], writes=[stb], scale=1.0 / D,
                             bias=self.epsb[:, 0:1])
                    self.recip(st[:, 2:3], st[:, 1:2], reads=[stb], writes=[stb])
                    self.stt("dve", n_t[:], n_t[:], st[:, 2:3], self.gfin[:], ALU.mult, ALU.mult,
                             reads=[n_b, stb, self.cb], writes=[n_b])
                    self.dma("sp", self.y_out[T * 128:(T + 1) * 128, :], n_t[:], f"xn{T % 2}", reads=[n_b])
        while prefetch:
            prefetch.pop(0)()


def _consts():
    ident = np.eye(128, dtype=np.float32)
    k = np.arange(128)[:, None]
    q = np.arange(128)[None, :]
    tri = np.where(q >= k, 0.0, NEGB).astype(np.float32)
    swab = np.zeros((128, 16, 128), np.float32)
    for h in range(8):
        slope = 2.0 ** (-(h + 1))
        d_prev = q + 128 - k
        swab[:, h * 2 + 0, :] = np.where(d_prev < 128, -8.0 * slope * d_prev, NEGB)
        d_cur = q - k
        swab[:, h * 2 + 1, :] = np.where(d_cur >= 0, -8.0 * slope * d_cur, NEGB)
    inv = (10000.0 ** (-np.arange(0, 32, 2, dtype=np.float32) / np.float32(32))).astype(np.float32)
    ang = (np.arange(S, dtype=np.float32)[:, None] * inv[None, :]).astype(np.float32)
    cos = np.cos(ang).astype(np.float32).T
    sin = np.sin(ang).astype(np.float32).T
    C32 = np.concatenate([cos, cos], 0)
    S32 = np.concatenate([-sin, sin], 0)
    ropeC = np.ascontiguousarray(np.tile(C32, (4, 1)))
    ropeS = np.ascontiguousarray(np.tile(S32, (4, 1)))
    return dict(c_ident=ident, c_tri=tri, c_swab=np.ascontiguousarray(swab.reshape(128, 16 * 128)),
                c_ropeC=ropeC, c_ropeS=ropeS)


def _layout_weights(attn_norm, w_in, mla_q_norm, w_uq, mla_kv_norm, w_ukv, swa_sinks, mem_norm, w_mem_kv, w_out,
                    final_norm):
    L = w_in.shape[0]
    f = lambda a: np.ascontiguousarray(a, dtype=np.float32)
    swap_cols = np.concatenate([np.arange(656, 672), np.arange(640, 656)])
    w_in_ext = np.concatenate([w_in, w_in[:, :, swap_cols]], axis=2)
    hq = np.arange(16)[:, None] * 96
    nope = (hq + np.arange(64)[None, :]).reshape(-1)
    rope = (hq + 64 + np.arange(32)[None, :]).reshape(-1)
    sw = np.concatenate([np.arange(16, 32), np.arange(0, 16)])
    ropes = (hq + 64 + sw[None, :]).reshape(-1)
    w_uq_r = w_uq[:, :, np.concatenate([nope, rope, ropes])]
    hk = np.arange(16)[:, None] * 128
    kn = (hk + np.arange(64)[None, :]).reshape(-1)
    vv = (hk + 64 + np.arange(64)[None, :]).reshape(-1)
    w_ukv_r = w_ukv[:, :, np.concatenate([kn, vv])]
    pk = lambda g, kc: np.ascontiguousarray(g.reshape(L, kc, 128).transpose(0, 2, 1))
    return dict(
        w_in=f(w_in_ext), w_uq=f(w_uq_r), w_ukv=f(w_ukv_r), w_mem=f(w_mem_kv), w_out=f(w_out),
        g_attn=f(np.broadcast_to(attn_norm[:, None, :], (L, 128, D))), g_q=f(pk(mla_q_norm, 3)),
        g_kv=f(pk(mla_kv_norm, 2)), g_mem=f(np.broadcast_to(mem_norm[:, None, :], (L, 128, D))),
        sinks=f(np.broadcast_to(swa_sinks[:, None, :], (L, 128, 8))),
        g_final=f(np.broadcast_to(final_norm[None, :], (128, D))),
    )


_CACHE = {}


def _get_nc():
    if "nc" not in _CACHE:
        b = Builder()
        _CACHE["nc"] = b.build()
        _CACHE["stats"] = b.stats
    return _CACHE["nc"]


def kernel(x, mem, attn_norm, w_in, mla_q_norm, w_uq, mla_kv_norm, w_ukv, swa_sinks, mem_norm, w_mem_kv, w_out,
           final_norm):
    x = np.asarray(x, dtype=np.float32)
    mem = np.asarray(mem, dtype=np.float32)
    B = x.shape[0]
    shared = _layout_weights(*[np.asarray(a, dtype=np.float32) for a in
                               (attn_norm, w_in, mla_q_norm, w_uq, mla_kv_norm, w_ukv, swa_sinks, mem_norm,
                                w_mem_kv, w_out, final_norm)])
    shared.update(_consts())
    nc = _get_nc()
    in_maps = []
    for b in range(B):
        m = dict(shared)
        m["x"] = np.ascontiguousarray(x[b])
        m["mem"] = np.ascontiguousarray(mem[b])
        in_maps.append(m)
    res = run_bass_kernel_spmd(nc, in_maps, core_ids=list(range(B)))
    return np.stack([np.asarray(r["y"], dtype=np.float32) for r in res.results], axis=0)
```
